# Optimizing a Trainium2 kernel written in Bass

```python
import jax, jax.numpy as jnp
from jax import lax
import numpy as np


D_MODEL = 2048
BATCH = 16
SEQ = 256
DEPTH = 2
DEC_BATCH = 8
DEC_SEQ = 1024
PAST_LEN = 512

GRID_W = 64
HGRN_HEADS = 8
HGRN_DK = 128
HGRN_DV = 128
HGRN_WIDTH = HGRN_HEADS * HGRN_DV
HGRN_CHUNK = 64
CONV_WIDTH = 512
NA_HEADS = 4
NA_HEAD_DIM = 128
NA_WIDTH = NA_HEADS * NA_HEAD_DIM
NA_KH = 8
NA_KW = 16
MIX_WIDTH = HGRN_WIDTH + CONV_WIDTH + NA_WIDTH
IN_PROJ_WIDTH = 5 * HGRN_WIDTH + 3 * CONV_WIDTH + 3 * NA_WIDTH
FFN_HIDDEN = ((8 * D_MODEL + 3 * 256 - 1) // (3 * 256)) * 256
ATTN_BLOCK = 128
EPS = 1e-6

kernel_name = 'hybrid_dit_hgrn2_shortconv_natten'


def rmsnorm(x, w):
    xf = x.astype(jnp.float32)
    y = xf * lax.rsqrt(jnp.mean(xf * xf, axis=-1, keepdims=True) + EPS)
    return (y * w.astype(jnp.float32)).astype(x.dtype)


def modulate(h, shift, scale):
    return h * (1.0 + scale) + shift


def adaln_params(cvec, w_ada_l, b_ada_l):
    mod = jax.nn.silu(cvec) @ w_ada_l + b_ada_l
    return jnp.split(mod[..., None, :], 6, axis=-1)


def split_in_proj(p):
    sizes = [HGRN_WIDTH] * 5 + [CONV_WIDTH] * 3 + [NA_WIDTH] * 3
    bounds = [int(s) for s in np.cumsum(sizes)[:-1]]
    return jnp.split(p, bounds, axis=-1)


def mixer_inputs(x, shift, scale, norm_w, w_in_l):
    h = modulate(rmsnorm(x, norm_w), shift, scale)
    return split_in_proj(h @ w_in_l)


def ffn_residual(x, shift, scale, gate, norm_w, wg, wu, wd):
    h = modulate(rmsnorm(x, norm_w), shift, scale)
    return x + gate * ((jax.nn.silu(h @ wg) * (h @ wu)) @ wd)


def hgrn_lower_bounds(lb_raw):
    p = jax.nn.softmax(lb_raw.astype(jnp.float32), axis=1)
    cp = jnp.cumsum(p, axis=1)
    return cp - cp[:, :1]


def hgrn2_gates(f_raw, lb):
    xf = f_raw.astype(jnp.float32)
    log_f = jnp.logaddexp(jnp.log(lb), jnp.log1p(-lb) + jax.nn.log_sigmoid(xf))
    k = (1.0 - lb) * jax.nn.sigmoid(-xf)
    return log_f, k


def hgrn2_chunk_scan(q, k, v, log_f, s0):
    B, N, H, K = q.shape
    nc = N // HGRN_CHUNK

    def to_chunks(t):
        return t.astype(jnp.float32).reshape(B, nc, HGRN_CHUNK, H, t.shape[-1]).transpose(1, 0, 3, 2, 4)

    causal = jnp.tril(jnp.ones((HGRN_CHUNK, HGRN_CHUNK), dtype=bool))[:, :, None]

    def step(s, inp):
        qc, kc, vc, gc = inp
        b = jnp.cumsum(gc, axis=2)
        b_last = b[:, :, -1:, :]
        o_inter = jnp.einsum('bhik,bhkv->bhiv', qc * jnp.exp(b), s)
        decay = jnp.exp(jnp.where(causal, b[:, :, :, None, :] - b[:, :, None, :, :], -jnp.inf))
        attn = jnp.einsum('bhik,bhjk,bhijk->bhij', qc, kc, decay)
        o = o_inter + jnp.einsum('bhij,bhjv->bhiv', attn, vc)
        s_new = jnp.exp(b_last[:, :, 0, :])[..., None] * s + jnp.einsum('bhjk,bhjv->bhkv', kc * jnp.exp(b_last - b), vc)
        return s_new, o

    s_fin, o = lax.scan(step, s0.astype(jnp.float32), (to_chunks(q), to_chunks(k), to_chunks(v), to_chunks(log_f)))
    o = o.transpose(1, 0, 3, 2, 4).reshape(B, N, H, v.shape[-1])
    return o, s_fin


def hgrn2_bidirectional(hq, hi, hf_f, hf_b, hg, lb_f, lb_b, gnorm_w, s_f0, s_b0):
    B, N, _ = hq.shape

    def heads(t):
        return t.reshape(B, N, HGRN_HEADS, -1)

    def flip(t):
        return jnp.flip(t, axis=1)

    q = heads(hq).astype(jnp.float32) * (HGRN_DK ** -0.5)
    v = heads(hi).astype(jnp.float32)
    lf_f, k_f = hgrn2_gates(heads(hf_f), lb_f.reshape(HGRN_HEADS, HGRN_DK))
    lf_b, k_b = hgrn2_gates(heads(hf_b), lb_b.reshape(HGRN_HEADS, HGRN_DK))
    o_f, s_f = hgrn2_chunk_scan(q, k_f, v, lf_f, s_f0)
    o_b, s_b = hgrn2_chunk_scan(flip(q), flip(k_b), flip(v), flip(lf_b), s_b0)
    o = o_f + flip(o_b)
    o = rmsnorm(o, gnorm_w) * jax.nn.silu(heads(hg).astype(jnp.float32))
    return o.reshape(B, N, HGRN_WIDTH).astype(hq.dtype), s_f, s_b


def short_conv_mixer(b_gate, c_gate, xc, conv_w_l):
    u = c_gate * xc
    up = jnp.pad(u, ((0, 0), (1, 1), (0, 0)))
    y = up[:, :-2] * conv_w_l[0] + up[:, 1:-1] * conv_w_l[1] + up[:, 2:] * conv_w_l[2]
    return b_gate * y


def context_self_attention(q, k, v):
    B, L, H, Dh = q.shape
    nb = L // ATTN_BLOCK
    scale = Dh ** -0.5
    qb = q.reshape(B, nb, ATTN_BLOCK, H, Dh).transpose(1, 0, 2, 3, 4)

    def blk(qi):
        s = jnp.einsum('bqhd,blhd->bhql', qi, k).astype(jnp.float32) * scale
        p = jax.nn.softmax(s, axis=-1).astype(v.dtype)
        return jnp.einsum('bhql,blhd->bqhd', p, v)

    o = lax.map(blk, qb)
    return o.transpose(1, 0, 2, 3, 4).reshape(B, L, H * Dh)


def latent_neighbourhood_attention(q, k, v, k_ctx, v_ctx, rpb_l):
    B, N, H, Dh = q.shape
    rows = N // GRID_W
    kh = min(NA_KH, rows)
    scale = Dh ** -0.5
    row_start = np.clip(np.arange(rows) - kh // 2, 0, rows - kh).astype(np.int32)
    col = np.arange(GRID_W)
    col_start = np.clip(col - NA_KW // 2, 0, GRID_W - NA_KW)
    col_mask = (col[None, :] >= col_start[:, None]) & (col[None, :] < col_start[:, None] + NA_KW)
    dc_idx = np.clip(col[None, :] - col[:, None] + NA_KW - 1, 0, 2 * NA_KW - 2)
    col_mask_j = jnp.asarray(col_mask)[:, None, :]
    k_grid = k.reshape(B, rows, GRID_W, H, Dh)
    v_grid = v.reshape(B, rows, GRID_W, H, Dh)
    q_rows = q.reshape(B, rows, GRID_W, H, Dh).transpose(1, 0, 2, 3, 4)
    n_loc = kh * GRID_W

    def row_fn(inp):
        q_r, rs, r = inp
        k_band = lax.dynamic_slice_in_dim(k_grid, rs, kh, axis=1)
        v_band = lax.dynamic_slice_in_dim(v_grid, rs, kh, axis=1)
        dr_idx = rs + jnp.arange(kh, dtype=jnp.int32) - r + (NA_KH - 1)
        bias = rpb_l[:, dr_idx][:, :, dc_idx].transpose(0, 2, 1, 3)
        s_loc = jnp.einsum('bqhd,bkwhd->bhqkw', q_r, k_band).astype(jnp.float32) * scale + bias.astype(jnp.float32)
        s_loc = jnp.where(col_mask_j, s_loc, -jnp.inf)
        s_ctx = jnp.einsum('bqhd,blhd->bhql', q_r, k_ctx).astype(jnp.float32) * scale
        s = jnp.concatenate([s_loc.reshape(B, H, GRID_W, n_loc), s_ctx], axis=-1)
        p = jax.nn.softmax(s, axis=-1).astype(v.dtype)
        p_loc = p[..., :n_loc].reshape(B, H, GRID_W, kh, GRID_W)
        p_ctx = p[..., n_loc:]
        return jnp.einsum('bhqkw,bkwhd->bqhd', p_loc, v_band) + jnp.einsum('bhql,blhd->bqhd', p_ctx, v_ctx)

    o = lax.map(row_fn, (q_rows, jnp.asarray(row_start), jnp.arange(rows, dtype=jnp.int32)))
    return o.transpose(1, 0, 2, 3, 4).reshape(B, N, H * Dh)


def context_layer(x, mods, norm_mix_w_l, w_in_l, lb_f, lb_b, gnorm_w_l, conv_w_l, rpb_unused_free, w_out_l,
                  norm_ffn_w_l, wg, wu, wd):
    sh1, sc1, g1, sh2, sc2, g2 = mods
    hq, hi, hff, hfb, hg, cb, cc, cx, nq, nk, nv = mixer_inputs(x, sh1, sc1, norm_mix_w_l, w_in_l)
    B, L, _ = x.shape
    s0 = jnp.zeros((B, HGRN_HEADS, HGRN_DK, HGRN_DV), jnp.float32)
    o_a, s_f, s_b = hgrn2_bidirectional(hq, hi, hff, hfb, hg, lb_f, lb_b, gnorm_w_l, s0, s0)
    o_b = short_conv_mixer(cb, cc, cx, conv_w_l)
    k_c = nk.reshape(B, L, NA_HEADS, NA_HEAD_DIM)
    v_c = nv.reshape(B, L, NA_HEADS, NA_HEAD_DIM)
    o_c = context_self_attention(nq.reshape(B, L, NA_HEADS, NA_HEAD_DIM), k_c, v_c)
    x = x + g1 * (jnp.concatenate([o_a, o_b, o_c], axis=-1) @ w_out_l)
    x = ffn_residual(x, sh2, sc2, g2, norm_ffn_w_l, wg, wu, wd)
    return x, k_c, v_c, jnp.stack([s_f, s_b], axis=1)


def latent_layer(x, mods, k_ctx, v_ctx, s_f0, s_b0, norm_mix_w_l, w_in_l, lb_f, lb_b, gnorm_w_l, conv_w_l, rpb_l,
                 w_out_l, norm_ffn_w_l, wg, wu, wd):
    sh1, sc1, g1, sh2, sc2, g2 = mods
    hq, hi, hff, hfb, hg, cb, cc, cx, nq, nk, nv = mixer_inputs(x, sh1, sc1, norm_mix_w_l, w_in_l)
    B, N, _ = x.shape
    o_a, _, _ = hgrn2_bidirectional(hq, hi, hff, hfb, hg, lb_f, lb_b, gnorm_w_l, s_f0, s_b0)
    o_b = short_conv_mixer(cb, cc, cx, conv_w_l)

    def heads(t):
        return t.reshape(B, N, NA_HEADS, NA_HEAD_DIM)

    o_c = latent_neighbourhood_attention(heads(nq), heads(nk), heads(nv), k_ctx, v_ctx, rpb_l)
    x = x + g1 * (jnp.concatenate([o_a, o_b, o_c], axis=-1) @ w_out_l)
    return ffn_residual(x, sh2, sc2, g2, norm_ffn_w_l, wg, wu, wd)


def setup_inputs(seed: int = 0) -> dict:
    key = jax.random.key(seed)
    ks = jax.random.split(key, 24)

    def nrm(k, shape, s):
        return jax.random.normal(k, shape, jnp.float32) * s

    return {
        'x_prompt': nrm(ks[0], (BATCH, SEQ, D_MODEL), 1.0),
        'x_sample': nrm(ks[1], (DEC_BATCH, DEC_SEQ, D_MODEL), 1.0),
        'cache_na_k': nrm(ks[2], (DEC_BATCH, DEPTH, PAST_LEN, NA_HEADS, NA_HEAD_DIM), 1.0),
        'cache_na_v': nrm(ks[3], (DEC_BATCH, DEPTH, PAST_LEN, NA_HEADS, NA_HEAD_DIM), 1.0),
        'state_hgrn': nrm(ks[4], (DEC_BATCH, DEPTH, 2, HGRN_HEADS, HGRN_DK, HGRN_DV), 0.5),
        'c': nrm(ks[5], (DEC_BATCH, D_MODEL), 1.0),
        'c_ctx': nrm(ks[6], (D_MODEL,), 1.0),
        'w_ada': nrm(ks[7], (DEPTH, D_MODEL, 6 * D_MODEL), 0.5 * D_MODEL ** -0.5),
        'b_ada': nrm(ks[8], (DEPTH, 6 * D_MODEL), 0.02),
        'norm_mix_w': 1.0 + nrm(ks[9], (DEPTH, D_MODEL), 0.05),
        'w_in': nrm(ks[10], (DEPTH, D_MODEL, IN_PROJ_WIDTH), D_MODEL ** -0.5),
        'hgrn_lb_raw': nrm(ks[11], (2, DEPTH, HGRN_WIDTH), 1.0),
        'hgrn_gnorm_w': 1.0 + nrm(ks[12], (DEPTH, HGRN_DV), 0.05),
        'conv_w': nrm(ks[13], (DEPTH, 3, CONV_WIDTH), 0.5),
        'na_rpb': nrm(ks[14], (DEPTH, NA_HEADS, 2 * NA_KH - 1, 2 * NA_KW - 1), 0.5),
        'w_out': nrm(ks[15], (DEPTH, MIX_WIDTH, D_MODEL), MIX_WIDTH ** -0.5),
        'norm_ffn_w': 1.0 + nrm(ks[16], (DEPTH, D_MODEL), 0.05),
        'w_ffn_gate': nrm(ks[17], (DEPTH, D_MODEL, FFN_HIDDEN), D_MODEL ** -0.5),
        'w_ffn_up': nrm(ks[18], (DEPTH, D_MODEL, FFN_HIDDEN), D_MODEL ** -0.5),
        'w_ffn_down': nrm(ks[19], (DEPTH, FFN_HIDDEN, D_MODEL), FFN_HIDDEN ** -0.5),
        'final_norm_w': 1.0 + nrm(ks[20], (D_MODEL,), 0.05),
    }


def reference(x_prompt, x_sample, cache_na_k, cache_na_v, state_hgrn, c, c_ctx, w_ada, b_ada, norm_mix_w, w_in,
              hgrn_lb_raw, hgrn_gnorm_w, conv_w, na_rpb, w_out, norm_ffn_w, w_ffn_gate, w_ffn_up, w_ffn_down,
              final_norm_w):
    lbs = hgrn_lower_bounds(hgrn_lb_raw)
    xp, xs = x_prompt, x_sample
    new_k, new_v, new_s = [], [], []
    for l in range(DEPTH):
        xp, k_c, v_c, s_c = context_layer(
            xp, adaln_params(c_ctx, w_ada[l], b_ada[l]), norm_mix_w[l], w_in[l], lbs[0, l], lbs[1, l],
            hgrn_gnorm_w[l], conv_w[l], None, w_out[l], norm_ffn_w[l], w_ffn_gate[l], w_ffn_up[l], w_ffn_down[l])
        new_k.append(k_c)
        new_v.append(v_c)
        new_s.append(s_c)
        xs = latent_layer(
            xs, adaln_params(c, w_ada[l], b_ada[l]), cache_na_k[:, l], cache_na_v[:, l], state_hgrn[:, l, 0],
            state_hgrn[:, l, 1], norm_mix_w[l], w_in[l], lbs[0, l], lbs[1, l], hgrn_gnorm_w[l], conv_w[l], na_rpb[l],
            w_out[l], norm_ffn_w[l], w_ffn_gate[l], w_ffn_up[l], w_ffn_down[l])
    y_prompt = rmsnorm(xp, final_norm_w)
    y_sample = rmsnorm(xs, final_norm_w)
    new_na_k = jnp.stack(new_k, axis=1)
    new_na_v = jnp.stack(new_v, axis=1)
    new_hgrn_state = jnp.stack(new_s, axis=1).astype(x_prompt.dtype)
    return (y_prompt, y_sample, new_na_k, new_na_v, new_hgrn_state)
```

```python
import numpy as np
from contextlib import ExitStack
import concourse.bass as bass
import concourse.mybir as mybir
from concourse.bass_utils import run_bass_kernel_spmd

F32 = mybir.dt.float32
BF16 = mybir.dt.bfloat16
ALU = mybir.AluOpType
AF = mybir.ActivationFunctionType
AX = mybir.AxisListType
ENGS = ("pe", "act", "dve", "pool", "sp")
EPS = 1e-6
NEG = -30000.0


class Buf:
    __slots__ = ("w", "rs", "excl")

    def __init__(self, excl=False):
        self.w = None
        self.rs = []
        self.excl = excl


class Chan:
    __slots__ = ("sem", "cnt")

    def __init__(self, sem):
        self.sem = sem
        self.cnt = 0


class Prog:
    def __init__(self, nc, es):
        self.nc = nc
        self.es = es
        self.ins = {e: [] for e in ENGS}
        self.esem = {e: es.enter_context(nc.semaphore("s_" + e)) for e in ENGS}
        self.nchan = 0

    def chan(self):
        s = self.es.enter_context(self.nc.semaphore("d%d" % self.nchan))
        self.nchan += 1
        return Chan(s)

    def _deps(self, eng, reads, writes, is_dma):
        deps = []
        for r in reads:
            if r.w is not None:
                deps.append((r.w, True))
            if r.excl:
                for e in r.rs:
                    deps.append((e, False))
        for w in writes:
            if w.w is not None:
                deps.append((w.w, True))
            for e in w.rs:
                deps.append((e, False))
        out = []
        for d, iswr in deps:
            if (not is_dma) and d[0] == "e" and d[1] == eng:
                if eng == "pe":
                    continue
                if not iswr:
                    continue
            out.append(d)
        return out

    def _commit(self, ev, reads, writes):
        for r in reads:
            r.rs.append(ev)
        for w in writes:
            w.w = ev
            w.rs = []

    def op(self, eng, fn, reads=(), writes=()):
        deps = self._deps(eng, reads, writes, False)
        ev = ("e", eng, len(self.ins[eng]))
        self.ins[eng].append([fn, deps, False, None])
        self._commit(ev, reads, writes)

    def dma(self, eng, fn, chan, reads=(), writes=(), inc=16):
        deps = self._deps(eng, reads, writes, True)
        chan.cnt += inc
        ev = ("d", chan, chan.cnt)
        self.ins[eng].append([fn, deps, False, chan])
        self._commit(ev, reads, writes)

    def wait_all(self, eng, bufs):
        deps = []
        for b in bufs:
            if b.w is not None:
                deps.append(b.w)
            deps.extend(b.rs)
        self.ins[eng].append([None, deps, False, None])

    def emit_all(self):
        EPOCH = 16000
        for e in ENGS:
            for it in self.ins[e]:
                for d in it[1]:
                    if d[0] == "e":
                        self.ins[d[1]][d[2]][2] = True
        cnt = {}
        esems = {}
        for e in ENGS:
            c = 0
            arr = []
            for it in self.ins[e]:
                if it[2]:
                    c += 1
                k = max(c - 1, 0)
                arr.append((k // EPOCH, k % EPOCH + 1 if c > 0 else 0))
            cnt[e] = arr
            nep = (max(c - 1, 0)) // EPOCH + 1
            esems[e] = [self.esem[e]] + [self.es.enter_context(self.nc.semaphore("s_%s_%d" % (e, i))) for i in range(1, nep)]

        def emit(eng, h):
            seen_e = {}
            seen_d = {}
            for it_i, it in enumerate(self.ins[eng]):
                need_e = {}
                need_d = {}
                for d in it[1]:
                    if d[0] == "e":
                        ep, c = cnt[d[1]][d[2]]
                        key = (d[1], ep)
                        if c > seen_e.get(key, 0) and c > need_e.get(key, 0):
                            need_e[key] = c
                    else:
                        k = id(d[1])
                        if d[2] > seen_d.get(k, 0) and d[2] > need_d.get(k, (None, 0))[1]:
                            need_d[k] = (d[1], d[2])
                for key, c in need_e.items():
                    h.wait_ge(esems[key[0]][key[1]], c)
                    seen_e[key] = c
                for k, (ch, c) in need_d.items():
                    h.wait_ge(ch.sem, c)
                    seen_d[k] = c
                if it[0] is None:
                    continue
                ins = it[0](h)
                if it[3] is not None:
                    ins.then_inc(it[3].sem, 16)
                elif it[2]:
                    ins.then_inc(esems[eng][cnt[eng][it_i][0]], 1)

        with self.nc.Block() as block:
            @block.tensor
            def _(h):
                emit("pe", h)

            @block.scalar
            def _(h):
                emit("act", h)

            @block.vector
            def _(h):
                emit("dve", h)

            @block.gpsimd
            def _(h):
                emit("pool", h)

            @block.sync
            def _(h):
                emit("sp", h)


class Cfg:
    def __init__(self, D=2048, NP=16, NS=8, L=2, PAST=512, debug=False, stop=99):
        self.stop = stop
        self.D = D
        self.KC = D // 128
        self.FH = ((8 * D + 3 * 256 - 1) // (3 * 256)) * 256
        self.FC = self.FH // 128
        self.NP = NP
        self.NS = NS
        self.L = L
        self.PAST = PAST
        self.PT = PAST // 128
        self.NV = 1 + NS
        self.debug = debug


SEQ = 256
DSEQ = 1024
T = 1024
NTT = 8
GRID_W = 64
ROWS = 16
NA_KH = 8
NA_KW = 16


def host_consts():
    ident = np.eye(128, dtype=np.float32)
    j = np.arange(128)[:, None]
    i = np.arange(128)[None, :]
    same = (j // 32) == (i // 32)
    maskf = (same & (j <= i)).astype(np.float32)
    maskb = (same & (j >= i)).astype(np.float32)
    cstart = np.ones((128, 512), np.float32)
    cstart[:, ::32] = 0.0
    col = np.arange(GRID_W)
    cs = np.clip(col - NA_KW // 2, 0, GRID_W - NA_KW)
    valid = (col[None, :] >= cs[:, None]) & (col[None, :] < cs[:, None] + NA_KW)
    cm = np.where(valid, 0.0, NEG).astype(np.float32)
    colmask = np.concatenate([cm, cm], axis=0)
    rowsel = (np.arange(128)[:, None] // 32 == np.arange(4)[None, :]).astype(np.float32)
    return dict(ident=ident, maskf=maskf, maskb=maskb, cstart=cstart, colmask=colmask, rowsel=rowsel)


def build(cfg):
    D, KC, FH, FC, NP, NS, L, PAST, PT, NV = cfg.D, cfg.KC, cfg.FH, cfg.FC, cfg.NP, cfg.NS, cfg.L, cfg.PAST, cfg.PT, cfg.NV
    nc = bass.Bass("TRN2", target_bir_lowering=False)

    def din(name, shape, dt=F32):
        return nc.dram_tensor(name, list(shape), dt, kind="ExternalInput").ap()

    def dout(name, shape, dt=F32):
        return nc.dram_tensor(name, list(shape), dt, kind="ExternalOutput").ap()

    xp = din("xp", [max(NP, 1) * SEQ, D])
    xs = din("xs", [max(NS, 1) * DSEQ, D])
    ck = din("ck", [max(NS, 1) * L * PAST, 512])
    cv = din("cv", [max(NS, 1) * L * PAST, 512])
    st = din("st", [max(NS, 1) * L * 2 * 8 * 128, 128])
    cvec = din("cvec", [NV, D])
    w_ada = din("w_ada", [L * D, 6 * D])
    b_ada = din("b_ada", [L * 6 * KC, 128])
    nmw = din("nmw", [L * KC, 128])
    w_in = din("w_in", [L * D, 8192])
    lbr = din("lbr", [2 * L * 8, 128])
    gnw = din("gnw", [L, 128])
    cw = din("cw", [L * 12, 128])
    rpb = din("rpb", [L * 60, 31])
    w_out = din("w_out", [L * 2048, D])
    nfw = din("nfw", [L * KC, 128])
    wg = din("wg", [L * D, FH])
    wu = din("wu", [L * D, FH])
    wd = din("wd", [L * FH, D])
    fnw = din("fnw", [KC, 128])
    c_ident = din("ident", [128, 128])
    c_maskf = din("maskf", [128, 128])
    c_maskb = din("maskb", [128, 128])
    c_cstart = din("cstart", [128, 512])
    c_colmask = din("colmask", [128, 64])
    c_rowsel = din("rowsel", [128, 4])

    yp = dout("yp", [max(NP, 1) * SEQ, D])
    ys = dout("ys", [max(NS, 1) * DSEQ, D])
    o_nk = dout("o_nk", [max(NP, 1) * L * SEQ, 512])
    o_nv = dout("o_nv", [max(NP, 1) * L * SEQ, 512])
    o_ns = dout("o_ns", [max(NP, 1) * L * 2 * 8 * 128, 128])

    w_in_r = nc.dram_tensor("w_in_r", [L * 64 * 128, KC * 128], BF16).ap()
    w_out_r = nc.dram_tensor("w_out_r", [L * KC * 128, 16 * 128], BF16).ap()
    wg_r = nc.dram_tensor("wg_r", [L * FC * 128, KC * 128], BF16).ap()
    wu_r = nc.dram_tensor("wu_r", [L * FC * 128, KC * 128], BF16).ap()
    wd_r = nc.dram_tensor("wd_r", [L * KC * 128, FC * 128], BF16).ap()
    rpbp = nc.dram_tensor("rpbp", [L * 60, 127], F32).ap()
    ctab_d = nc.dram_tensor("ctab_d", [L * 128, 60 * 64], F32).ap()

    es = ExitStack()
    with es:
        P = Prog(nc, es)

        def sb(name, shape, dt):
            return es.enter_context(nc.sbuf_tensor("sb_" + name, list(shape), dt))

        def ps(name, shape, dt):
            return es.enter_context(nc.psum_tensor("ps_" + name, list(shape), dt))

        xT = sb("xT", [128, KC, T], F32)
        hT = sb("hT", [128, KC, T], BF16)
        mixT = sb("mixT", [128, 16, T], BF16)
        NSLOT = 4
        wring = [sb("wr%d" % i, [128, 2048], BF16) for i in range(NSLOT)]
        SCRB = 46 * 1024
        scr = sb("scr", [128, SCRB // 4], F32)
        ident_f = sb("ident_f", [128, 128], F32)
        ident_b = sb("ident_b", [128, 128], BF16)
        ones_b = sb("ones_b", [128, 128], BF16)
        maskf = sb("maskf", [128, 128], F32)
        maskb = sb("maskb", [128, 128], F32)
        cstart = sb("cstart", [128, 512], F32)
        colmask = sb("colmask", [128, 64], F32)
        rowsel = sb("rowsel", [128, 4], F32)
        mods = sb("mods", [128, L, 6, KC, NV], F32)
        A1 = sb("A1", [128, L, KC, NV], F32)
        A2 = sb("A2", [128, L, KC, NV], F32)
        badaT = sb("badaT", [128, L, 6 * KC], F32)
        nmwT = sb("nmwT", [128, L * KC], F32)
        nfwT = sb("nfwT", [128, L * KC], F32)
        fnwT = sb("fnwT", [128, KC], F32)
        gnwT = sb("gnwT", [128, L], F32)
        cwT = sb("cwT", [128, L * 12], F32)
        lbT = sb("lbT", [128, 2 * L * 8], F32)
        LB = sb("LB", [128, 2 * L * 8], F32)
        OM = sb("OM", [128, 2 * L * 8], F32)
        NOM = sb("NOM", [128, 2 * L * 8], F32)
        sT = sb("sT", [128, KC, NV], BF16)
        small = sb("small", [128, 64], F32)

        D0 = ps("D0", [128, 512], F32)
        D1 = ps("D1", [128, 512], F32)
        SC = ps("SC", [128, 3, 512], F32)
        TB = ps("TB", [128, 2, 1024], BF16)
        PV = ps("PV", [128, 512], F32)
        bD0, bD1, bPV = Buf(True), Buf(True), Buf(True)
        bSC = [Buf(True), Buf(True), Buf(True)]
        bTB = [Buf(True), Buf(True)]

        def carve(off, shape, dt):
            n = 1
            for s_ in shape:
                n *= s_
            nb = n * (4 if dt == F32 else 2)
            assert off % 4 == 0 and off + nb <= SCRB, (off, nb, SCRB)
            ap = scr[:, off // 4:(off + nb) // 4]
            if dt != F32:
                ap = ap.bitcast(dt)
            if len(shape) == 2:
                ap = ap.rearrange("p (a b) -> p a b", b=shape[1])
            elif len(shape) == 3:
                ap = ap.rearrange("p (a b c) -> p a b c", b=shape[1], c=shape[2])
            return ap, off + nb

        phase_bufs = []

        def new_phase(n):
            prev = []
            for b in phase_bufs:
                if b.w is not None:
                    prev.append(b.w)
                prev.extend(b.rs)
            del phase_bufs[:]
            out = []
            for _ in range(n):
                b = Buf()
                b.rs = list(prev)
                phase_bufs.append(b)
                out.append(b)
            return out

        ch_misc = P.chan()
        ch_stg = P.chan()
        ch_cv = P.chan()
        b_const = Buf()

        def load_const(dst, src):
            P.dma("sp", lambda h: h.dma_start(out=dst, in_=src), ch_misc, writes=[b_const])

        load_const(ident_f[:], c_ident)
        load_const(maskf[:], c_maskf)
        load_const(maskb[:], c_maskb)
        load_const(cstart[:], c_cstart)
        load_const(colmask[:], c_colmask)
        load_const(rowsel[:], c_rowsel)
        b_idb = Buf()
        P.op("dve", lambda h: h.tensor_copy(out=ident_b[:], in_=ident_f[:]), reads=[b_const], writes=[b_idb])
        P.op("pool", lambda h: h.memset(ones_b[:], 1.0), writes=[b_idb])

        (bs_stg, bs_ada0, bs_ada1, bs_cv, bs_z) = new_phase(5)
        o = 0
        stg_v, o = carve(o, [128], F32)
        cv_in, o = carve(o, [KC * 128], F32)
        cv_s = cv_in
        ztile, o = carve(o, [128], F32)
        ada_slot = []
        for i in range(2):
            a_, o = carve(o, [KC, 256], BF16)
            ada_slot.append(a_)
        ctab_s, o = carve(o, [60, 64], F32)

        b_vecs = Buf()

        def load_T(dst, src_rows, n):
            P.dma("sp", lambda h: h.dma_start(out=stg_v[0:n, :], in_=src_rows), ch_stg, writes=[bs_stg])
            P.op("pe", lambda h: h.transpose(out=SC[:, 0, 0:n], in_=stg_v[0:n, :], identity=ident_f[0:n, 0:n]),
                 reads=[bs_stg, b_const], writes=[bSC[0]])
            P.op("dve", lambda h: h.tensor_copy(out=dst, in_=SC[:, 0, 0:n]), reads=[bSC[0]], writes=[b_vecs])

        for l in range(L):
            load_T(badaT[:, l, :], b_ada[l * 6 * KC:(l + 1) * 6 * KC, :], 6 * KC)
        load_T(nmwT[:], nmw, L * KC)
        load_T(nfwT[:], nfw, L * KC)
        load_T(fnwT[:], fnw, KC)
        load_T(gnwT[:], gnw, L)
        load_T(cwT[:], cw, L * 12)
        load_T(lbT[:], lbr, 2 * L * 8)

        b_lb = Buf()
        P.op("act", lambda h: h.activation(out=lbT[:], in_=lbT[:], func=AF.Exp), reads=[b_vecs], writes=[b_vecs])
        for d_ in range(2):
            base = d_ * L * 8
            tot = small[:, 0:8]
            P.op("dve", lambda h, base=base: h.tensor_copy(out=small[:, 0:8], in_=lbT[:, base:base + 8]), reads=[b_vecs], writes=[b_lb])
            for l in range(1, L):
                P.op("dve", lambda h, base=base, l=l: h.tensor_tensor(out=small[:, 0:8], in0=small[:, 0:8], in1=lbT[:, base + l * 8:base + l * 8 + 8], op=ALU.add),
                     reads=[b_vecs, b_lb], writes=[b_lb])
            P.op("dve", lambda h: h.reciprocal(out=small[:, 8:16], in_=small[:, 0:8]), reads=[b_lb], writes=[b_lb])
            P.op("pool", lambda h, base=base: h.memset(LB[:, base:base + 8], 0.0), writes=[b_lb])
            for l in range(1, L):
                P.op("dve", lambda h, base=base, l=l: h.tensor_tensor(out=small[:, 16:24], in0=lbT[:, base + l * 8:base + l * 8 + 8], in1=small[:, 8:16], op=ALU.mult),
                     reads=[b_vecs, b_lb], writes=[b_lb])
                P.op("dve", lambda h, base=base, l=l: h.tensor_tensor(out=LB[:, base + l * 8:base + l * 8 + 8], in0=LB[:, base + (l - 1) * 8:base + (l - 1) * 8 + 8], in1=small[:, 16:24], op=ALU.add),
                     reads=[b_lb], writes=[b_lb])
        P.op("dve", lambda h: h.tensor_scalar(out=OM[:], in0=LB[:], scalar1=-1.0, scalar2=1.0, op0=ALU.mult, op1=ALU.add), reads=[b_lb], writes=[b_lb])
        P.op("dve", lambda h: h.tensor_scalar(out=NOM[:], in0=OM[:], scalar1=-1.0, scalar2=None, op0=ALU.mult), reads=[b_lb], writes=[b_lb])

        P.dma("sp", lambda h: h.dma_start(out=cv_in[0:NV, :], in_=cvec), ch_cv, writes=[bs_cv])
        P.op("act", lambda h: h.activation(out=cv_s[0:NV, :], in_=cv_in[0:NV, :], func=AF.Silu), reads=[bs_cv], writes=[bs_cv])
        b_sT = Buf()
        for k in range(KC):
            P.op("pe", lambda h, k=k: h.transpose(out=SC[:, 0, 0:NV], in_=cv_s[0:NV, k * 128:(k + 1) * 128], identity=ident_f[0:NV, 0:NV]),
                 reads=[bs_cv, b_const], writes=[bSC[0]])
            P.op("dve", lambda h, k=k: h.tensor_copy(out=sT[:, k, :], in_=SC[:, 0, 0:NV]), reads=[bSC[0]], writes=[b_sT])

        ch_ada = [P.chan(), P.chan()]
        b_mods = Buf()
        nblk = (6 * D) // 256
        bi = 0
        for l in range(L):
            wv = w_ada[l * D:(l + 1) * D, :].rearrange("(k p) n -> p k n", p=128)
            for blk in range(nblk):
                s_ = bi % 2
                bi += 1
                bsl = bs_ada0 if s_ == 0 else bs_ada1
                P.dma("pool", lambda h, s_=s_, blk=blk, wv=wv: h.dma_start(out=ada_slot[s_], in_=wv[:, :, blk * 256:(blk + 1) * 256]), ch_ada[s_], writes=[bsl])
                for mm in range(2):
                    mg = blk * 2 + mm
                    for k in range(KC):
                        P.op("pe", lambda h, s_=s_, mm=mm, k=k: h.matmul(PV[:, mm * NV:(mm + 1) * NV], lhsT=ada_slot[s_][:, k, mm * 128:(mm + 1) * 128], rhs=sT[:, k, :], start=(k == 0), stop=(k == KC - 1)),
                             reads=[bsl, b_sT], writes=[bPV])
                    j6, kk_ = mg // KC, mg % KC
                    P.op("dve", lambda h, l=l, mm=mm, mg=mg, j6=j6, kk_=kk_: h.tensor_scalar(out=mods[:, l, j6, kk_, :], in0=PV[:, mm * NV:(mm + 1) * NV], scalar1=badaT[:, l, mg:mg + 1], scalar2=None, op0=ALU.add),
                         reads=[bPV, b_vecs], writes=[b_mods])
        for l in range(L):
            for k in range(KC):
                P.op("dve", lambda h, l=l, k=k: h.tensor_scalar(out=A1[:, l, k, :], in0=mods[:, l, 1, k, :], scalar1=1.0, scalar2=nmwT[:, l * KC + k:l * KC + k + 1], op0=ALU.add, op1=ALU.mult),
                     reads=[b_mods, b_vecs], writes=[b_mods])
                P.op("dve", lambda h, l=l, k=k: h.tensor_scalar(out=A2[:, l, k, :], in0=mods[:, l, 4, k, :], scalar1=1.0, scalar2=nfwT[:, l * KC + k:l * KC + k + 1], op0=ALU.add, op1=ALU.mult),
                     reads=[b_mods, b_vecs], writes=[b_mods])

        ch_tab = P.chan()
        b_rpbp, b_ctabd = Buf(), Buf()
        if NS > 0:
            P.op("pool", lambda h: h.memset(ztile, 0.0), writes=[bs_z])
            P.dma("sp", lambda h: h.dma_start(out=rpbp, in_=ztile[0:L * 60, 0:127]), ch_tab, reads=[bs_z], writes=[b_rpbp])
            P.dma("sp", lambda h: h.dma_start(out=rpbp[:, 48:79], in_=rpb), ch_tab, writes=[b_rpbp])
            ch_tab2 = P.chan()
            b_ct = bs_ada0
            b_cts = Buf()
            phase_bufs.append(b_cts)
            for l in range(L):
                src = rpbp[l * 60:(l + 1) * 60, :]
                first = True
                for w_ in range(64):
                    for ro in range(2):
                        pp_ = ro * 64 + w_
                        if first:
                            P.dma("sp", lambda h, pp_=pp_, w_=w_, src=src: h.dma_start(out=ctab_s[pp_:pp_ + 1, :, :], in_=src[:, 63 - w_:127 - w_].unsqueeze(0)),
                                  ch_tab2, reads=[b_rpbp], writes=[b_cts])
                            first = False
                        else:
                            ch_tab2.cnt += 16
                            ev = ("d", ch_tab2, ch_tab2.cnt)
                            P.ins["sp"].append([lambda h, pp_=pp_, w_=w_, src=src: h.dma_start(out=ctab_s[pp_:pp_ + 1, :, :], in_=src[:, 63 - w_:127 - w_].unsqueeze(0)), [], False, ch_tab2])
                            b_cts.w = ev
                P.op("dve", lambda h: h.tensor_tensor(out=ctab_s, in0=ctab_s, in1=colmask[:].unsqueeze(1).to_broadcast([128, 60, 64]), op=ALU.add),
                     reads=[b_cts, b_const], writes=[b_cts])
                P.dma("sp", lambda h, l=l: h.dma_start(out=ctab_d[l * 128:(l + 1) * 128, :], in_=ctab_s.rearrange("p a b -> p (a b)")), ch_tab, reads=[b_cts], writes=[b_ctabd])

        def precast(dst, src, rows_in, cols_out, l, nchunk_k):
            ch = P.chan()
            b = Buf()
            sv = src[l * rows_in:(l + 1) * rows_in, :].rearrange("(k p) (m c) -> m p k c", p=128, c=128)
            nm = cols_out // 128
            ev = None
            for m in range(nm):
                r0 = (l * nm + m) * 128
                for k0 in range(0, nchunk_k, 16):
                    k1 = min(nchunk_k, k0 + 16)
                    ch.cnt += 16
                    ev = ("d", ch, ch.cnt)
                    P.ins["pool"].append([lambda h, m=m, r0=r0, sv=sv, k0=k0, k1=k1: h.dma_start(out=dst[r0:r0 + 128, :].rearrange("p (k c) -> p k c", c=128)[:, k0:k1, :], in_=sv[m][:, k0:k1, :]), [], False, ch])
            b.w = ev
            return b

        bw_in, bw_out, bw_g, bw_u, bw_d = [], [], [], [], []
        for l in range(L if cfg.stop >= 2 else 0):
            bw_in.append(precast(w_in_r, w_in, D, 8192, l, KC))
            bw_out.append(precast(w_out_r, w_out, 2048, D, l, 16))
            bw_g.append(precast(wg_r, wg, D, FH, l, KC))
            bw_u.append(precast(wu_r, wu, D, FH, l, KC))
            bw_d.append(precast(wd_r, wd, FH, D, l, FC))

        ring_ch = [P.chan() for _ in range(NSLOT)]
        ring_b = [Buf() for _ in range(NSLOT)]
        ring_i = [0]

        def wload(src_tile, width, bsrc):
            s_ = ring_i[0] % NSLOT
            ring_i[0] += 1
            P.dma("sp", lambda h, s_=s_: h.dma_start(out=wring[s_][:, 0:width], in_=src_tile), ring_ch[s_], reads=[bsrc], writes=[ring_b[s_]])
            return wring[s_], ring_b[s_]

        bx = [[Buf() for _ in range(2)] for _ in range(KC)]
        bh = [[Buf() for _ in range(2)] for _ in range(KC)]
        bm = [[Buf() for _ in range(2)] for _ in range(16)]

        dense_banks = [(D0, bD0), (D1, bD1)]
        dense_banks4 = dense_banks + [(SC[:, 0, :], bSC[0]), (SC[:, 1, :], bSC[1])]
        dbi = [0]

        def next_bank(banks):
            b = banks[dbi[0] % len(banks)]
            dbi[0] += 1
            return b

        def hs(n):
            return slice(n * 512, (n + 1) * 512)

        ch_xin = [P.chan(), P.chan()]
        ch_out = [P.chan(), P.chan()]

        def load_x(src_rows):
            (b_s0, b_s1) = new_phase(2)
            o = 0
            stg = []
            for i in range(2):
                a_, o = carve(o, [D], F32)
                stg.append(a_)
            bst = [b_s0, b_s1]
            for tt in range(NTT):
                s_ = tt % 2
                P.dma("sp", lambda h, s_=s_, tt=tt: h.dma_start(out=stg[s_], in_=src_rows[tt * 128:(tt + 1) * 128, :]), ch_xin[s_], writes=[bst[s_]])
                for k0 in range(0, KC, 4):
                    nk_ = min(4, KC - k0)
                    for kk_ in range(nk_):
                        P.op("pe", lambda h, s_=s_, k0=k0, kk_=kk_: h.transpose(out=SC[:, 0, kk_ * 128:(kk_ + 1) * 128], in_=stg[s_][:, (k0 + kk_) * 128:(k0 + kk_ + 1) * 128], identity=ident_f[:]),
                             reads=[bst[s_], b_const], writes=[bSC[0]])
                    P.op("act", lambda h, k0=k0, nk_=nk_, tt=tt: h.activation(out=xT[:, k0:k0 + nk_, tt * 128:(tt + 1) * 128], in_=SC[:, 0, 0:nk_ * 128].rearrange("p (a b) -> p a b", b=128), func=AF.Copy),
                         reads=[bSC[0]], writes=[bx[k][tt // 4] for k in range(k0, k0 + nk_)])

        def norm_stats(n, rbuf_ap, b_r, sq, b_sq):
            for k in range(KC):
                s_ = k % 2
                P.op("act", lambda h, k=k, s_=s_: h.activation(out=sq[s_], in_=xT[:, k, hs(n)], func=AF.Square), reads=[bx[k][n]], writes=[b_sq[s_]])
                P.op("pe", lambda h, k=k, s_=s_: h.matmul(PV[:], lhsT=ones_b[:], rhs=sq[s_], start=(k == 0), stop=(k == KC - 1)), reads=[b_sq[s_], b_idb], writes=[bPV])
            P.op("act", lambda h: h.activation(out=rbuf_ap, in_=PV[:], func=AF.Sqrt, scale=1.0 / D, bias=eps_ap), reads=[bPV, b_eps], writes=[b_r])
            P.op("dve", lambda h: h.reciprocal(out=rbuf_ap, in_=rbuf_ap), reads=[b_r], writes=[b_r])

        eps_ap = small[:, 32:33]
        b_eps = Buf()
        P.op("pool", lambda h: h.memset(small[:, 32:33], EPS), writes=[b_eps])

        def norm_mod(l, v, Aap, shj):
            (b_q0, b_q1, b_r, b_t0, b_t1) = new_phase(5)
            o = 0
            sq = []
            for i in range(2):
                a_, o = carve(o, [512], BF16)
                sq.append(a_)
            rbuf, o = carve(o, [512], F32)
            tt_ = []
            for i in range(2):
                a_, o = carve(o, [512], F32)
                tt_.append(a_)
            b_t = [b_t0, b_t1]
            for n in range(2):
                norm_stats(n, rbuf, b_r, sq, [b_q0, b_q1])
                for k in range(KC):
                    s_ = k % 2
                    P.op("dve", lambda h, k=k, s_=s_, n=n: h.scalar_tensor_tensor(out=tt_[s_], in0=xT[:, k, hs(n)], scalar=Aap[:, l, k, v:v + 1], in1=rbuf, op0=ALU.mult, op1=ALU.mult),
                         reads=[bx[k][n], b_r, b_mods], writes=[b_t[s_]])
                    P.op("act", lambda h, k=k, s_=s_, n=n: h.activation(out=hT[:, k, hs(n)], in_=tt_[s_], func=AF.Identity, bias=mods[:, l, shj, k, v:v + 1], scale=1.0),
                         reads=[b_t[s_], b_mods], writes=[bh[k][n]])

        def inproj_fm(l, m, evac, halves=(0, 1)):
            slot, bsl = wload(w_in_r[(l * 64 + m) * 128:(l * 64 + m + 1) * 128, :], KC * 128, bw_in[l])
            for n in halves:
                bank, bb = next_bank(dense_banks)
                for k in range(KC):
                    P.op("pe", lambda h, k=k, n=n, bank=bank, slot=slot: h.matmul(bank[:] if bank is D0 or bank is D1 else bank, lhsT=slot[:, k * 128:(k + 1) * 128], rhs=hT[:, k, hs(n)], start=(k == 0), stop=(k == KC - 1)),
                         reads=[bsl, bh[k][n]], writes=[bb])
                evac(n, bank[:] if bank is D0 or bank is D1 else bank, bb)

        def inproj_tm(l, m, evac):
            slot, bsl = wload(w_in_r[(l * 64 + m) * 128:(l * 64 + m + 1) * 128, :], KC * 128, bw_in[l])
            for n in range(2):
                bank, bb = next_bank(dense_banks)
                bap = bank[:] if (bank is D0 or bank is D1) else bank
                for t4 in range(4):
                    tt = n * 4 + t4
                    for k in range(KC):
                        P.op("pe", lambda h, k=k, tt=tt, t4=t4, bap=bap, slot=slot: h.matmul(bap[:, t4 * 128:(t4 + 1) * 128], lhsT=hT[:, k, tt * 128:(tt + 1) * 128], rhs=slot[:, k * 128:(k + 1) * 128], start=(k == 0), stop=(k == KC - 1)),
                             reads=[bsl, bh[k][n]], writes=[bb])
                evac(n, bap.rearrange("p (a b) -> p a b", b=128), bb)

        ch_ctx = P.chan()
        ch_ctx2 = P.chan()
        ch_tabl = P.chan()

        def attention(l, kind, u):
            nb_ = 14
            bb_ = new_phase(nb_)
            (b_q, b_k, b_v, b_ckb, b_ckT, b_cvb, b_sl, b_pf, b_pc, b_pt, b_st, b_og0, b_og1, b_tab) = bb_
            o = 0
            qTh, o = carve(o, [T], BF16)
            kTh, o = carve(o, [T], BF16)
            vh, o = carve(o, [NTT, 128], BF16)
            ckb, o = carve(o, [PT, 128], BF16)
            ckT, o = carve(o, [PAST], BF16)
            cvb, o = carve(o, [PT, 128], BF16)
            sloc, o = carve(o, [512], F32)
            pfull, o = carve(o, [640], BF16)
            pctx, o = carve(o, [512], BF16)
            ptr, o = carve(o, [9, 128], BF16)
            stat, o = carve(o, [16], F32)
            ostg = []
            for i in range(2):
                a_, o = carve(o, [4, 128], F32)
                ostg.append(a_)
            b_og = [b_og0, b_og1]
            if kind == "s":
                tab, o = carve(o, [60, 64], F32)
                P.dma("sp", lambda h: h.dma_start(out=tab, in_=ctab_d[l * 128:(l + 1) * 128, :].rearrange("p (a b) -> p a b", b=64)), ch_tabl, reads=[b_ctabd], writes=[b_tab])
            ogi = [0]
            for hd in range(4):
                def ev_q(n, pa, pb):
                    P.op("act", lambda h, n=n, pa=pa: h.activation(out=qTh[:, hs(n)], in_=pa, func=AF.Copy, scale=128.0 ** -0.5), reads=[pb], writes=[b_q])
                inproj_fm(l, 52 + hd, ev_q)

                def ev_k(n, pa, pb):
                    P.op("dve", lambda h, n=n, pa=pa: h.tensor_copy(out=kTh[:, hs(n)], in_=pa), reads=[pb], writes=[b_k])
                if getattr(cfg, "att_sub", 9) >= -1:
                    inproj_fm(l, 56 + hd, ev_k)

                def ev_v(n, pa, pb, hd=hd):
                    P.op("act", lambda h, n=n, pa=pa: h.activation(out=vh[:, n * 4:(n + 1) * 4, :], in_=pa, func=AF.Copy), reads=[pb], writes=[b_v])
                    if kind == "p" and getattr(cfg, "att_sub", 9) >= 1:
                        s_ = ogi[0] % 2
                        ogi[0] += 1
                        P.op("dve", lambda h, pa=pa, s_=s_: h.tensor_copy(out=ostg[s_], in_=pa), reads=[pb], writes=[b_og[s_]])
                        for t4 in range(4):
                            sq_ = u * 4 + n * 2 + t4 // 2
                            r0 = (sq_ * L + l) * SEQ + (t4 % 2) * 128
                            P.dma("sp", lambda h, s_=s_, t4=t4, r0=r0, hd=hd: h.dma_start(out=o_nv[r0:r0 + 128, hd * 128:(hd + 1) * 128], in_=ostg[s_][:, t4, :]),
                                  ch_out[s_], reads=[b_og[s_]])
                if getattr(cfg, "att_sub", 9) >= 0:
                    inproj_tm(l, 60 + hd, ev_v)
                if getattr(cfg, "att_sub", 9) < 1:
                    continue
                if kind == "p":
                    def ev_ko(n, pa, pb, hd=hd):
                        s_ = ogi[0] % 2
                        ogi[0] += 1
                        P.op("dve", lambda h, pa=pa, s_=s_: h.tensor_copy(out=ostg[s_], in_=pa), reads=[pb], writes=[b_og[s_]])
                        for t4 in range(4):
                            sq_ = u * 4 + n * 2 + t4 // 2
                            r0 = (sq_ * L + l) * SEQ + (t4 % 2) * 128
                            P.dma("sp", lambda h, s_=s_, t4=t4, r0=r0, hd=hd: h.dma_start(out=o_nk[r0:r0 + 128, hd * 128:(hd + 1) * 128], in_=ostg[s_][:, t4, :]),
                                  ch_out[s_], reads=[b_og[s_]])
                    inproj_tm(l, 56 + hd, ev_ko)
                else:
                    r0 = (u * L + l) * PAST
                    P.dma("pool", lambda h, r0=r0, hd=hd: h.dma_start(out=ckb, in_=ck[r0:r0 + PAST, hd * 128:(hd + 1) * 128].rearrange("(a p) c -> p a c", p=128)), ch_ctx, writes=[b_ckb])
                    P.dma("pool", lambda h, r0=r0, hd=hd: h.dma_start(out=cvb, in_=cv[r0:r0 + PAST, hd * 128:(hd + 1) * 128].rearrange("(a p) c -> p a c", p=128)), ch_ctx2, writes=[b_cvb])
                    for lt in range(PT):
                        P.op("pe", lambda h, lt=lt: h.transpose(out=TB[:, 1, lt * 128:(lt + 1) * 128], in_=ckb[:, lt, :], identity=ident_b[:]), reads=[b_ckb, b_idb], writes=[bTB[1]])
                    P.op("act", lambda h: h.activation(out=ckT, in_=TB[:, 1, 0:PAST], func=AF.Copy), reads=[bTB[1]], writes=[b_ckT])
                asub = getattr(cfg, "att_sub", 9)
                for R in range(NTT if asub >= 2 else 0):
                    if kind == "p":
                        sq_ = R // 2
                        ktiles = [2 * sq_, 2 * sq_ + 1]
                        nloc = 2
                        P.op("pe", lambda h, R=R, sq_=sq_: h.matmul(SC[:, 0, 0:256], lhsT=qTh[:, R * 128:(R + 1) * 128], rhs=kTh[:, sq_ * 256:(sq_ + 1) * 256], start=True, stop=True),
                             reads=[b_q, b_k], writes=[bSC[0]])
                        P.op("dve", lambda h: h.tensor_reduce(out=stat[:, 0:1], in_=SC[:, 0, 0:256], axis=AX.X, op=ALU.max), reads=[bSC[0]], writes=[b_st])
                        P.op("dve", lambda h: h.tensor_scalar(out=stat[:, 1:2], in0=stat[:, 0:1], scalar1=-1.0, scalar2=None, op0=ALU.mult), reads=[b_st], writes=[b_st])
                        P.op("act", lambda h: h.activation(out=pfull[:, 0:256], in_=SC[:, 0, 0:256], func=AF.Exp, bias=stat[:, 1:2], scale=1.0, accum_out=stat[:, 2:3]),
                             reads=[bSC[0], b_st], writes=[b_pf, b_st])
                        P.op("dve", lambda h: h.reciprocal(out=stat[:, 3:4], in_=stat[:, 2:3]), reads=[b_st], writes=[b_st])
                        P.op("dve", lambda h: h.tensor_scalar(out=pfull[:, 0:256], in0=pfull[:, 0:256], scalar1=stat[:, 3:4], scalar2=None, op0=ALU.mult), reads=[b_st, b_pf], writes=[b_pf])
                        nctx = 0
                    else:
                        r_a, r_b = 2 * R, 2 * R + 1
                        rs_a = min(max(r_a - NA_KH // 2, 0), ROWS - NA_KH)
                        rs_b = min(max(r_b - NA_KH // 2, 0), ROWS - NA_KH)
                        kt0 = rs_a // 2
                        kt1 = (rs_b + NA_KH - 1) // 2
                        nloc = kt1 - kt0 + 1
                        ktiles = list(range(kt0, kt0 + nloc))
                        nctx = PT
                        w1 = min(nloc, 4) * 128
                        P.op("pe", lambda h, R=R, kt0=kt0, w1=w1: h.matmul(SC[:, 0, 0:w1], lhsT=qTh[:, R * 128:(R + 1) * 128], rhs=kTh[:, kt0 * 128:kt0 * 128 + w1], start=True, stop=True),
                             reads=[b_q, b_k], writes=[bSC[0]])
                        if nloc > 4:
                            P.op("pe", lambda h, R=R, kt0=kt0: h.matmul(SC[:, 1, 0:128], lhsT=qTh[:, R * 128:(R + 1) * 128], rhs=kTh[:, (kt0 + 4) * 128:(kt0 + 5) * 128], start=True, stop=True),
                                 reads=[b_q, b_k], writes=[bSC[1]])
                        P.op("pe", lambda h, R=R: h.matmul(SC[:, 2, 0:PAST], lhsT=qTh[:, R * 128:(R + 1) * 128], rhs=ckT, start=True, stop=True),
                             reads=[b_q, b_ckT], writes=[bSC[2]])
                        scflat = SC[:].rearrange("p a b -> p (a b)")
                        c0s = []
                        for ro, (r_, rs_) in enumerate(((r_a, rs_a), (r_b, rs_b))):
                            c0 = (rs_ - 2 * kt0) * 64
                            c0s.append(c0)
                            dr0 = rs_ - r_ + 7
                            psl = slice(ro * 64, ro * 64 + 64)
                            P.op("dve", lambda h, psl=psl, c0=c0, dr0=dr0, hd=hd: h.tensor_tensor(out=sloc[psl, :], in0=scflat[psl, c0:c0 + 512], in1=tab[psl, hd * 15 + dr0:hd * 15 + dr0 + 8, :].rearrange("p a b -> p (a b)"), op=ALU.add),
                                 reads=[bSC[0], bSC[1], b_tab], writes=[b_sl])
                        P.op("dve", lambda h: h.tensor_reduce(out=stat[:, 0:1], in_=sloc, axis=AX.X, op=ALU.max), reads=[b_sl], writes=[b_st])
                        P.op("dve", lambda h: h.tensor_reduce(out=stat[:, 4:5], in_=SC[:, 2, 0:PAST], axis=AX.X, op=ALU.max), reads=[bSC[2]], writes=[b_st])
                        P.op("dve", lambda h: h.tensor_tensor(out=stat[:, 0:1], in0=stat[:, 0:1], in1=stat[:, 4:5], op=ALU.max), reads=[b_st], writes=[b_st])
                        P.op("dve", lambda h: h.tensor_scalar(out=stat[:, 1:2], in0=stat[:, 0:1], scalar1=-1.0, scalar2=None, op0=ALU.mult), reads=[b_st], writes=[b_st])
                        P.op("pool", lambda h: h.memset(pfull, 0.0), writes=[b_pf])
                        for ro in range(2):
                            psl = slice(ro * 64, ro * 64 + 64)
                            c0 = c0s[ro]
                            P.op("act", lambda h, psl=psl, c0=c0: h.activation(out=pfull[psl, c0:c0 + 512], in_=sloc[psl, :], func=AF.Exp, bias=stat[psl, 1:2], scale=1.0, accum_out=stat[psl, 2:3]),
                                 reads=[b_sl, b_st], writes=[b_pf, b_st])
                        P.op("act", lambda h: h.activation(out=pctx, in_=SC[:, 2, 0:PAST], func=AF.Exp, bias=stat[:, 1:2], scale=1.0, accum_out=stat[:, 5:6]),
                             reads=[bSC[2], b_st], writes=[b_pc, b_st])
                        P.op("dve", lambda h: h.tensor_tensor(out=stat[:, 2:3], in0=stat[:, 2:3], in1=stat[:, 5:6], op=ALU.add), reads=[b_st], writes=[b_st])
                        P.op("dve", lambda h: h.reciprocal(out=stat[:, 3:4], in_=stat[:, 2:3]), reads=[b_st], writes=[b_st])
                        P.op("dve", lambda h, nloc=nloc: h.tensor_scalar(out=pfull[:, 0:nloc * 128], in0=pfull[:, 0:nloc * 128], scalar1=stat[:, 3:4], scalar2=None, op0=ALU.mult), reads=[b_st, b_pf], writes=[b_pf])
                        P.op("dve", lambda h: h.tensor_scalar(out=pctx, in0=pctx, scalar1=stat[:, 3:4], scalar2=None, op0=ALU.mult), reads=[b_st, b_pc], writes=[b_pc])
                    if asub < 3:
                        continue
                    ntot = nloc + nctx
                    for i in range(ntot):
                        src = pfull[:, i * 128:(i + 1) * 128] if i < nloc else pctx[:, (i - nloc) * 128:(i - nloc + 1) * 128]
                        bk = 0 if i < 8 else 1
                        ii = i if i < 8 else i - 8
                        P.op("pe", lambda h, src=src, bk=bk, ii=ii: h.transpose(out=TB[:, bk, ii * 128:(ii + 1) * 128], in_=src, identity=ident_b[:]),
                             reads=[b_pf, b_pc, b_idb], writes=[bTB[bk]])
                    n0 = min(ntot, 8)
                    P.op("act", lambda h, n0=n0: h.activation(out=ptr[:, 0:n0, :], in_=TB[:, 0, 0:n0 * 128].rearrange("p (a b) -> p a b", b=128), func=AF.Copy), reads=[bTB[0]], writes=[b_pt])
                    if ntot > 8:
                        P.op("act", lambda h: h.activation(out=ptr[:, 8, :], in_=TB[:, 1, 0:128], func=AF.Copy), reads=[bTB[1]], writes=[b_pt])
                    if asub < 4:
                        continue
                    for i in range(ntot):
                        if i < nloc:
                            lh = vh[:, ktiles[i], :]
                            rd = [b_v, b_pt]
                        else:
                            lh = cvb[:, i - nloc, :]
                            rd = [b_cvb, b_pt]
                        P.op("pe", lambda h, lh=lh, i=i, ntot=ntot: h.matmul(PV[:, 0:128], lhsT=lh, rhs=ptr[:, i, :], start=(i == 0), stop=(i == ntot - 1)), reads=rd, writes=[bPV])
                    P.op("act", lambda h, hd=hd, R=R: h.activation(out=mixT[:, 12 + hd, R * 128:(R + 1) * 128], in_=PV[:, 0:128], func=AF.Copy), reads=[bPV], writes=[bm[12 + hd][R // 4]])

        def conv_mixer(l, nseq, slen):
            (b_cc, b_u, b_y, b_cb) = new_phase(4)
            o = 0
            cc_sb, o = carve(o, [T], F32)
            u_sb, o = carve(o, [T], F32)
            y_sb, o = carve(o, [T], F32)
            cb_sb, o = carve(o, [T], F32)

            def v3(ap):
                return ap.rearrange("p (s t) -> p s t", t=slen)
            for j in range(4):
                def ev_cc(n, pa, pb):
                    P.op("act", lambda h, n=n, pa=pa: h.activation(out=cc_sb[:, hs(n)], in_=pa, func=AF.Copy), reads=[pb], writes=[b_cc])
                inproj_fm(l, 44 + j, ev_cc)

                def ev_cx(n, pa, pb):
                    P.op("dve", lambda h, n=n, pa=pa: h.tensor_tensor(out=u_sb[:, hs(n)], in0=pa, in1=cc_sb[:, hs(n)], op=ALU.mult), reads=[pb, b_cc], writes=[b_u])
                inproj_fm(l, 48 + j, ev_cx)

                def ev_cb(n, pa, pb):
                    P.op("act", lambda h, n=n, pa=pa: h.activation(out=cb_sb[:, hs(n)], in_=pa, func=AF.Copy), reads=[pb], writes=[b_cb])
                inproj_fm(l, 40 + j, ev_cb)
                c0 = (l * 3 + 0) * 4 + j
                c1 = (l * 3 + 1) * 4 + j
                c2 = (l * 3 + 2) * 4 + j
                P.op("dve", lambda h, c1=c1: h.tensor_scalar(out=y_sb, in0=u_sb, scalar1=cwT[:, c1:c1 + 1], scalar2=None, op0=ALU.mult), reads=[b_u, b_vecs], writes=[b_y])
                P.op("dve", lambda h, c0=c0: h.scalar_tensor_tensor(out=v3(y_sb)[:, :, 1:slen], in0=v3(u_sb)[:, :, 0:slen - 1], scalar=cwT[:, c0:c0 + 1], in1=v3(y_sb)[:, :, 1:slen], op0=ALU.mult, op1=ALU.add),
                     reads=[b_u, b_y, b_vecs], writes=[b_y])
                P.op("dve", lambda h, c2=c2: h.scalar_tensor_tensor(out=v3(y_sb)[:, :, 0:slen - 1], in0=v3(u_sb)[:, :, 1:slen], scalar=cwT[:, c2:c2 + 1], in1=v3(y_sb)[:, :, 0:slen - 1], op0=ALU.mult, op1=ALU.add),
                     reads=[b_u, b_y, b_vecs], writes=[b_y])
                for n in range(2):
                    P.op("dve", lambda h, n=n, j=j: h.tensor_tensor(out=mixT[:, 8 + j, hs(n)], in0=y_sb[:, hs(n)], in1=cb_sb[:, hs(n)], op=ALU.mult), reads=[b_y, b_cb], writes=[bm[8 + j][n]])

        ch_st = [P.chan(), P.chan()]
        ch_so = [P.chan() for _ in range(16)]

        def hgrn(l, kind, u):
            names = ["q", "xg", "v", "v4", "oacc", "s", "F", "kk", "Bc", "B", "tE", "qt", "kt", "kh", "khT", "U", "Sp", "at", "c0", "c1", "dec"]
            bl = new_phase(len(names) + 32)
            B_ = dict(zip(names, bl))
            bU = bl[len(names):len(names) + 16]
            bSp = bl[len(names) + 16:len(names) + 32]
            o = 0
            q_bf, o = carve(o, [T], BF16)
            xg, o = carve(o, [T], BF16)
            v_tok, o = carve(o, [NTT, 128], BF16)
            V4, o = carve(o, [4, 4 * 128], BF16)
            o_acc, o = carve(o, [T], F32)
            s_sb, o = carve(o, [T], F32)
            Fb, o = carve(o, [512], F32)
            kkb, o = carve(o, [512], BF16)
            Bc, o = carve(o, [512], F32)
            Bb, o = carve(o, [512], F32)
            tE, o = carve(o, [512], F32)
            qt, o = carve(o, [512], BF16)
            kt, o = carve(o, [512], BF16)
            kh, o = carve(o, [512], BF16)
            khT, o = carve(o, [4, 128], BF16)
            U, o = carve(o, [16, 128], F32)
            Sp, o = carve(o, [16, 128], BF16)
            at_sb, o = carve(o, [4, 128], BF16)
            carry = []
            for i in range(2):
                a_, o = carve(o, [128], F32)
                carry.append(a_)
            dec, o = carve(o, [16], F32)
            bcar = [B_["c0"], B_["c1"]]
            chunks_per_seq = (SEQ if kind == "p" else DSEQ) // 32

            for hd in range(8):
                def ev_q(n, pa, pb):
                    P.op("act", lambda h, n=n, pa=pa: h.activation(out=q_bf[:, hs(n)], in_=pa, func=AF.Copy, scale=128.0 ** -0.5), reads=[pb], writes=[B_["q"]])
                inproj_fm(l, hd, ev_q)

                def ev_g(n, pa, pb):
                    P.op("dve", lambda h, n=n, pa=pa: h.tensor_copy(out=xg[:, hs(n)], in_=pa), reads=[pb], writes=[B_["xg"]])
                inproj_fm(l, 32 + hd, ev_g)

                def ev_v(n, pa, pb):
                    P.op("act", lambda h, n=n, pa=pa: h.activation(out=v_tok[:, n * 4:(n + 1) * 4, :], in_=pa, func=AF.Copy), reads=[pb], writes=[B_["v"]])
                inproj_tm(l, 8 + hd, ev_v)

                for dr in range(2):
                    col = (dr * L + l) * 8 + hd
                    lb_ap, om_ap, nom_ap = LB[:, col:col + 1], OM[:, col:col + 1], NOM[:, col:col + 1]

                    def ev_f(n, pa, pb):
                        P.op("act", lambda h, n=n, pa=pa: h.activation(out=s_sb[:, hs(n)], in_=pa, func=AF.Sigmoid), reads=[pb], writes=[B_["s"]])
                    inproj_fm(l, (16 if dr == 0 else 24) + hd, ev_f)
                    mask = maskf if dr == 0 else maskb
                    ci = 0
                    have_state = False
                    for sg in ((0, 1) if dr == 0 else (1, 0)):
                        cs_ = hs(sg)
                        P.op("dve", lambda h, cs_=cs_, om_ap=om_ap, lb_ap=lb_ap: h.tensor_scalar(out=Fb, in0=s_sb[:, cs_], scalar1=om_ap, scalar2=lb_ap, op0=ALU.mult, op1=ALU.add), reads=[B_["s"], b_lb], writes=[B_["F"]])
                        P.op("dve", lambda h, cs_=cs_, om_ap=om_ap, nom_ap=nom_ap: h.tensor_scalar(out=kkb, in0=s_sb[:, cs_], scalar1=nom_ap, scalar2=om_ap, op0=ALU.mult, op1=ALU.add), reads=[B_["s"], b_lb], writes=[B_["kk"]])
                        P.op("act", lambda h: h.activation(out=Fb, in_=Fb, func=AF.Ln), reads=[B_["F"]], writes=[B_["F"]])
                        P.op("dve", lambda h: h.tensor_tensor_scan(out=Bc, data0=cstart[:], data1=Fb, initial=0.0, op0=ALU.mult, op1=ALU.add), reads=[B_["F"], b_const], writes=[B_["Bc"]])
                        Bc3 = Bc.rearrange("p (c j) -> p c j", j=32)
                        tot = Bc3[:, :, 31]
                        totb = tot.unsqueeze(2).to_broadcast([128, 16, 32])
                        if dr == 0:
                            Bsrc, bB = Bc, B_["Bc"]
                        else:
                            P.op("dve", lambda h: h.tensor_tensor(out=Bb.rearrange("p (c j) -> p c j", j=32), in0=totb, in1=Bc3, op=ALU.subtract), reads=[B_["Bc"]], writes=[B_["B"]])
                            P.op("dve", lambda h: h.tensor_tensor(out=Bb, in0=Bb, in1=Fb, op=ALU.add), reads=[B_["B"], B_["F"]], writes=[B_["B"]])
                            Bsrc, bB = Bb, B_["B"]
                        P.op("act", lambda h, Bsrc=Bsrc: h.activation(out=tE, in_=Bsrc, func=AF.Exp), reads=[bB], writes=[B_["tE"]])
                        P.op("dve", lambda h, cs_=cs_: h.tensor_tensor(out=qt, in0=q_bf[:, cs_], in1=tE, op=ALU.mult), reads=[B_["q"], B_["tE"]], writes=[B_["qt"]])
                        P.op("act", lambda h, Bsrc=Bsrc: h.activation(out=tE, in_=Bsrc, func=AF.Exp, scale=-1.0), reads=[bB, B_["qt"]], writes=[B_["tE"]])
                        P.op("dve", lambda h: h.tensor_tensor(out=kt, in0=kkb, in1=tE, op=ALU.mult), reads=[B_["kk"], B_["tE"]], writes=[B_["kt"]])
                        P.op("dve", lambda h, Bsrc=Bsrc: h.tensor_tensor(out=tE.rearrange("p (c j) -> p c j", j=32), in0=totb, in1=Bsrc.rearrange("p (c j) -> p c j", j=32), op=ALU.subtract),
                             reads=[bB, B_["Bc"], B_["kt"]], writes=[B_["tE"]])
                        P.op("act", lambda h: h.activation(out=tE, in_=tE, func=AF.Exp), reads=[B_["tE"]], writes=[B_["tE"]])
                        P.op("dve", lambda h: h.tensor_tensor(out=kh, in0=kkb, in1=tE, op=ALU.mult), reads=[B_["kk"], B_["tE"]], writes=[B_["kh"]])
                        P.op("act", lambda h: h.activation(out=dec, in_=tot, func=AF.Exp), reads=[B_["Bc"]], writes=[B_["dec"]])
                        for t4 in range(4):
                            P.op("pe", lambda h, t4=t4: h.transpose(out=TB[:, 0, t4 * 128:(t4 + 1) * 128], in_=kh[:, t4 * 128:(t4 + 1) * 128], identity=ident_b[:]), reads=[B_["kh"], b_idb], writes=[bTB[0]])
                        P.op("act", lambda h: h.activation(out=khT, in_=TB[:, 0, 0:512].rearrange("p (a b) -> p a b", b=128), func=AF.Copy), reads=[bTB[0]], writes=[B_["khT"]])
                        V43 = V4.rearrange("p t (c v) -> p t c v", v=128)
                        for c in range(4):
                            P.op("pool", lambda h, c=c, sg=sg: h.tensor_scalar(out=V43[:, :, c, :], in0=v_tok[:, sg * 4:(sg + 1) * 4, :], scalar1=rowsel[:, c:c + 1], scalar2=None, op0=ALU.mult),
                                 reads=[B_["v"], b_const], writes=[B_["v4"]])
                        for t4 in range(4):
                            P.op("pe", lambda h, t4=t4: h.matmul(SC[:, 0, :], lhsT=khT[:, t4, :], rhs=V4[:, t4, :], start=True, stop=True), reads=[B_["khT"], B_["v4"]], writes=[bSC[0]])
                            P.op("act", lambda h, t4=t4: h.activation(out=U[:, t4 * 4:(t4 + 1) * 4, :], in_=SC[:, 0, :].rearrange("p (a b) -> p a b", b=128), func=AF.Copy), reads=[bSC[0]], writes=bU[t4 * 4:(t4 + 1) * 4])
                        order = list(range(16)) if dr == 0 else list(range(15, -1, -1))
                        prev = None
                        prev_b = None
                        for cg in order:
                            gch = sg * 16 + cg
                            sidx = gch // chunks_per_seq
                            pos = gch % chunks_per_seq
                            seq_start = (pos == 0) if dr == 0 else (pos == chunks_per_seq - 1)
                            seq_end = (pos == chunks_per_seq - 1) if dr == 0 else (pos == 0)
                            if seq_start:
                                if kind == "p":
                                    prev, prev_b = None, None
                                else:
                                    r0 = (((u * L + l) * 2 + dr) * 8 + hd) * 128
                                    P.dma("sp", lambda h, r0=r0, ci=ci: h.dma_start(out=carry[ci], in_=st[r0:r0 + 128, :]), ch_st[ci], writes=[bcar[ci]])
                                    prev, prev_b = carry[ci], bcar[ci]
                            elif cg == order[0]:
                                prev, prev_b = carry[ci], bcar[ci]
                            if prev is None:
                                P.op("pool", lambda h, cg=cg: h.memset(Sp[:, cg, :], 0.0), writes=[bSp[cg]])
                            else:
                                P.op("pool", lambda h, cg=cg, prev=prev: h.tensor_copy(out=Sp[:, cg, :], in_=prev), reads=[prev_b], writes=[bSp[cg]])
                                P.op("dve", lambda h, cg=cg, prev=prev: h.scalar_tensor_tensor(out=U[:, cg, :], in0=prev, scalar=dec[:, cg:cg + 1], in1=U[:, cg, :], op0=ALU.mult, op1=ALU.add),
                                     reads=[prev_b, B_["dec"], bU[cg]], writes=[bU[cg]])
                            prev, prev_b = U[:, cg, :], bU[cg]
                            if seq_end and kind == "p":
                                sq_ = u * 4 + sidx
                                r0 = (((sq_ * L + l) * 2 + dr) * 8 + hd) * 128
                                P.dma("sp", lambda h, r0=r0, cg=cg: h.dma_start(out=o_ns[r0:r0 + 128, :], in_=U[:, cg, :]), ch_so[cg], reads=[bU[cg]])
                        nci = 1 - ci
                        P.op("pool", lambda h, nci=nci, cg=order[-1]: h.tensor_copy(out=carry[nci], in_=U[:, cg, :]), reads=[bU[cg]], writes=[bcar[nci]])
                        ci = nci
                        for t4 in range(4):
                            P.op("pe", lambda h, t4=t4: h.matmul(SC[:, 1, t4 * 128:(t4 + 1) * 128], lhsT=kt[:, t4 * 128:(t4 + 1) * 128], rhs=qt[:, t4 * 128:(t4 + 1) * 128], start=True, stop=True),
                                 reads=[B_["kt"], B_["qt"]], writes=[bSC[1]])
                        P.op("dve", lambda h, mask=mask: h.tensor_tensor(out=at_sb, in0=SC[:, 1, :].rearrange("p (a b) -> p a b", b=128), in1=mask[:].unsqueeze(1).to_broadcast([128, 4, 128]), op=ALU.mult),
                             reads=[bSC[1], b_const], writes=[B_["at"]])
                        for t4 in range(4):
                            P.op("pe", lambda h, t4=t4, sg=sg: h.matmul(SC[:, 2, t4 * 128:(t4 + 1) * 128], lhsT=v_tok[:, sg * 4 + t4, :], rhs=at_sb[:, t4, :], start=True, stop=False),
                                 reads=[B_["v"], B_["at"]], writes=[bSC[2]])
                            for c in range(4):
                                cg = t4 * 4 + c
                                P.op("pe", lambda h, t4=t4, c=c, cg=cg: h.matmul(SC[:, 2, t4 * 128 + c * 32:t4 * 128 + (c + 1) * 32], lhsT=Sp[:, cg, :], rhs=qt[:, t4 * 128 + c * 32:t4 * 128 + (c + 1) * 32], start=False, stop=(c == 3)),
                                     reads=[bSp[cg], B_["qt"]], writes=[bSC[2]])
                        if dr == 0:
                            P.op("act", lambda h, cs_=cs_: h.activation(out=o_acc[:, cs_], in_=SC[:, 2, :], func=AF.Copy), reads=[bSC[2]], writes=[B_["oacc"]])
                        else:
                            P.op("dve", lambda h, cs_=cs_: h.tensor_tensor(out=o_acc[:, cs_], in0=SC[:, 2, :], in1=o_acc[:, cs_], op=ALU.add), reads=[bSC[2], B_["oacc"]], writes=[B_["oacc"]])
                for n in range(2):
                    P.op("act", lambda h, n=n: h.activation(out=qt, in_=o_acc[:, hs(n)], func=AF.Square), reads=[B_["oacc"]], writes=[B_["qt"]])
                    P.op("pe", lambda h: h.matmul(PV[:], lhsT=ones_b[:], rhs=qt, start=True, stop=True), reads=[B_["qt"], b_idb], writes=[bPV])
                    P.op("act", lambda h: h.activation(out=tE, in_=PV[:], func=AF.Sqrt, scale=1.0 / 128.0, bias=eps_ap), reads=[bPV, b_eps], writes=[B_["tE"]])
                    P.op("dve", lambda h: h.reciprocal(out=tE, in_=tE), reads=[B_["tE"]], writes=[B_["tE"]])
                    P.op("dve", lambda h, n=n: h.tensor_tensor(out=Bc, in0=o_acc[:, hs(n)], in1=tE, op=ALU.mult), reads=[B_["oacc"], B_["tE"]], writes=[B_["Bc"]])
                    P.op("act", lambda h, n=n: h.activation(out=Fb, in_=xg[:, hs(n)], func=AF.Sigmoid), reads=[B_["xg"]], writes=[B_["F"]])
                    P.op("dve", lambda h, n=n: h.tensor_tensor(out=Fb, in0=Fb, in1=xg[:, hs(n)], op=ALU.mult), reads=[B_["xg"], B_["F"]], writes=[B_["F"]])
                    P.op("dve", lambda h, n=n, hd=hd: h.scalar_tensor_tensor(out=mixT[:, hd, hs(n)], in0=Bc, scalar=gnwT[:, l:l + 1], in1=Fb, op0=ALU.mult, op1=ALU.mult),
                         reads=[B_["Bc"], B_["F"], b_vecs], writes=[bm[hd][n]])

        def out_proj(l, v):
            for m in range(KC):
                slot, bsl = wload(w_out_r[(l * KC + m) * 128:(l * KC + m + 1) * 128, :], 2048, bw_out[l])
                for n in range(2):
                    bank, bb = next_bank(dense_banks4)
                    bap = bank[:] if (bank is D0 or bank is D1) else bank
                    for k in range(16):
                        P.op("pe", lambda h, k=k, n=n, bap=bap, slot=slot: h.matmul(bap, lhsT=slot[:, k * 128:(k + 1) * 128], rhs=mixT[:, k, hs(n)], start=(k == 0), stop=(k == 15)),
                             reads=[bsl, bm[k][n]], writes=[bb])
                    P.op("dve", lambda h, m=m, n=n, bap=bap: h.scalar_tensor_tensor(out=xT[:, m, hs(n)], in0=bap, scalar=mods[:, l, 2, m, v:v + 1], in1=xT[:, m, hs(n)], op0=ALU.mult, op1=ALU.add),
                         reads=[bb, bx[m][n], b_mods], writes=[bx[m][n]])

        ch_wd = [P.chan(), P.chan()]

        def ffn(l, v):
            nb_ = 6
            (b_a, b_sg0, b_sg1, b_wd0, b_wd1, b_dummy) = new_phase(nb_)
            for k in range(16):
                for n in range(2):
                    if bm[k][n].w is not None:
                        b_a.rs.append(bm[k][n].w)
                    b_a.rs.extend(bm[k][n].rs)
            o = 0
            a_ext = None
            if FC > 32:
                a_ext, o = carve(o, [FC - 32, 512], BF16)
            sg_t = []
            for i in range(2):
                a_, o = carve(o, [512], F32)
                sg_t.append(a_)
            wds = []
            for i in range(2):
                a_, o = carve(o, [FC * 128], BF16)
                wds.append(a_)
            b_sg = [b_sg0, b_sg1]
            b_wd = [b_wd0, b_wd1]
            mflat = mixT[:].rearrange("p a b -> p (a b)")

            def a_ap(j):
                if j < 32:
                    return mflat[:, j * 512:(j + 1) * 512]
                return a_ext[:, j - 32, :]
            wdi = 0
            for n in range(2):
                for j in range(FC):
                    sl_g, bg_ = wload(wg_r[(l * FC + j) * 128:(l * FC + j + 1) * 128, :], KC * 128, bw_g[l])
                    sl_u, bu_ = wload(wu_r[(l * FC + j) * 128:(l * FC + j + 1) * 128, :], KC * 128, bw_u[l])
                    bank_g, bbg = next_bank(dense_banks4)
                    bank_u, bbu = next_bank(dense_banks4)
                    bg_ap = bank_g[:] if (bank_g is D0 or bank_g is D1) else bank_g
                    bu_ap = bank_u[:] if (bank_u is D0 or bank_u is D1) else bank_u
                    for k in range(KC):
                        P.op("pe", lambda h, k=k, n=n, bg_ap=bg_ap, sl_g=sl_g: h.matmul(bg_ap, lhsT=sl_g[:, k * 128:(k + 1) * 128], rhs=hT[:, k, hs(n)], start=(k == 0), stop=(k == KC - 1)),
                             reads=[bg_, bh[k][n]], writes=[bbg])
                    for k in range(KC):
                        P.op("pe", lambda h, k=k, n=n, bu_ap=bu_ap, sl_u=sl_u: h.matmul(bu_ap, lhsT=sl_u[:, k * 128:(k + 1) * 128], rhs=hT[:, k, hs(n)], start=(k == 0), stop=(k == KC - 1)),
                             reads=[bu_, bh[k][n]], writes=[bbu])
                    s_ = j % 2
                    P.op("act", lambda h, s_=s_, bg_ap=bg_ap: h.activation(out=sg_t[s_], in_=bg_ap, func=AF.Silu), reads=[bbg], writes=[b_sg[s_]])
                    P.op("dve", lambda h, s_=s_, bu_ap=bu_ap, j=j: h.tensor_tensor(out=a_ap(j), in0=bu_ap, in1=sg_t[s_], op=ALU.mult), reads=[bbu, b_sg[s_]], writes=[b_a])
                for m in range(KC):
                    s_ = wdi % 2
                    wdi += 1
                    P.dma("sp", lambda h, s_=s_, m=m: h.dma_start(out=wds[s_], in_=wd_r[(l * KC + m) * 128:(l * KC + m + 1) * 128, :]), ch_wd[s_], reads=[bw_d[l]], writes=[b_wd[s_]])
                    bank, bb = next_bank(dense_banks4)
                    bap = bank[:] if (bank is D0 or bank is D1) else bank
                    for j in range(FC):
                        P.op("pe", lambda h, j=j, s_=s_, bap=bap: h.matmul(bap, lhsT=wds[s_][:, j * 128:(j + 1) * 128], rhs=a_ap(j), start=(j == 0), stop=(j == FC - 1)),
                             reads=[b_wd[s_], b_a], writes=[bb])
                    P.op("dve", lambda h, m=m, n=n, bap=bap: h.scalar_tensor_tensor(out=xT[:, m, hs(n)], in0=bap, scalar=mods[:, l, 5, m, v:v + 1], in1=xT[:, m, hs(n)], op0=ALU.mult, op1=ALU.add),
                         reads=[bb, bx[m][n], b_mods], writes=[bx[m][n]])
            for k in range(16):
                for n in range(2):
                    bm[k][n].w = None
                    bm[k][n].rs = list(b_a.rs) + ([b_a.w] if b_a.w is not None else [])

        def final_store(dst_rows):
            (b_q0, b_q1, b_r, b_y0, b_y1, b_hy) = new_phase(6)
            for k in range(KC):
                for n in range(2):
                    if bh[k][n].w is not None:
                        b_hy.rs.append(bh[k][n].w)
                    b_hy.rs.extend(bh[k][n].rs)
            o = 0
            sq = []
            for i in range(2):
                a_, o = carve(o, [512], BF16)
                sq.append(a_)
            rbuf, o = carve(o, [512], F32)
            ystg = []
            for i in range(2):
                a_, o = carve(o, [D], F32)
                ystg.append(a_)
            b_ys = [b_y0, b_y1]
            yT = hT[:].rearrange("p a b -> p (a b)").bitcast(F32).rearrange("p (a b) -> p a b", b=512)
            si = 0
            for n in range(2):
                norm_stats(n, rbuf, b_r, sq, [b_q0, b_q1])
                for k in range(KC):
                    P.op("dve", lambda h, k=k, n=n: h.scalar_tensor_tensor(out=yT[:, k, :], in0=xT[:, k, hs(n)], scalar=fnwT[:, k:k + 1], in1=rbuf, op0=ALU.mult, op1=ALU.mult),
                         reads=[bx[k][n], b_r, b_vecs], writes=[b_hy])
                for t4 in range(4):
                    s_ = si % 2
                    si += 1
                    for k0 in range(0, KC, 4):
                        nk_ = min(4, KC - k0)
                        for kk_ in range(nk_):
                            P.op("pe", lambda h, k0=k0, kk_=kk_, t4=t4: h.transpose(out=SC[:, 0, kk_ * 128:(kk_ + 1) * 128], in_=yT[:, k0 + kk_, t4 * 128:(t4 + 1) * 128], identity=ident_f[:]),
                                 reads=[b_hy, b_const], writes=[bSC[0]])
                        P.op("act", lambda h, s_=s_, k0=k0, nk_=nk_: h.activation(out=ystg[s_][:, k0 * 128:(k0 + nk_) * 128], in_=SC[:, 0, 0:nk_ * 128], func=AF.Copy), reads=[bSC[0]], writes=[b_ys[s_]])
                    tt = n * 4 + t4
                    P.dma("sp", lambda h, s_=s_, tt=tt: h.dma_start(out=dst_rows[tt * 128:(tt + 1) * 128, :], in_=ystg[s_]), ch_out[s_], reads=[b_ys[s_]])
            for k in range(KC):
                for n in range(2):
                    bh[k][n].w = None
                    bh[k][n].rs = list(b_hy.rs) + ([b_hy.w] if b_hy.w is not None else [])

        units = [("p", u) for u in range(NP // 4)] + [("s", u) for u in range(NS)]
        for kind, u in units:
            if kind == "p":
                src = xp[u * T:(u + 1) * T, :]
                dst = yp[u * T:(u + 1) * T, :]
                v = 0
                nseq, slen = 4, SEQ
            else:
                src = xs[u * T:(u + 1) * T, :]
                dst = ys[u * T:(u + 1) * T, :]
                v = 1 + u
                nseq, slen = 1, DSEQ
            if cfg.stop < 3:
                break
            load_x(src)
            for l in range(L):
                norm_mod(l, v, A1, 0)
                if cfg.stop >= 4 and kind in getattr(cfg, "att_kinds", "ps"):
                    attention(l, kind, u)
                if cfg.stop >= 5:
                    conv_mixer(l, nseq, slen)
                if cfg.stop >= 6:
                    hgrn(l, kind, u)
                if cfg.stop >= 7:
                    out_proj(l, v)
                    norm_mod(l, v, A2, 3)
                    ffn(l, v)
            if cfg.stop >= 8:
                final_store(dst)

        fin = Buf()
        for ch in [ch_out[0], ch_out[1], ch_misc, ch_tab] + ch_so:
            if ch.cnt:
                fin.rs.append(("d", ch, ch.cnt))
        P.wait_all("sp", [fin] + phase_bufs)
        P.emit_all()
    return nc


N_ACTIVE = 1


def make_in_maps(cfg, n_act, x_prompt, x_sample, cache_na_k, cache_na_v, state_hgrn, c, c_ctx, w_ada, b_ada, norm_mix_w, w_in,
                 hgrn_lb_raw, hgrn_gnorm_w, conv_w, na_rpb, w_out, norm_ffn_w, w_ffn_gate, w_ffn_up, w_ffn_down, final_norm_w):
    f = lambda a: np.ascontiguousarray(np.asarray(a, dtype=np.float32))
    D, L, KC, NP, NS = cfg.D, cfg.L, cfg.KC, cfg.NP, cfg.NS
    consts = host_consts()
    shared = dict(
        w_ada=f(w_ada).reshape(L * D, 6 * D), b_ada=f(b_ada).reshape(L * 6 * KC, 128), nmw=f(norm_mix_w).reshape(L * KC, 128),
        w_in=f(w_in).reshape(L * D, 8192), lbr=f(hgrn_lb_raw).reshape(2 * L * 8, 128), gnw=f(hgrn_gnorm_w).reshape(L, 128),
        cw=f(conv_w).reshape(L * 12, 128), rpb=f(na_rpb).reshape(L * 60, 31), w_out=f(w_out).reshape(L * 2048, D),
        nfw=f(norm_ffn_w).reshape(L * KC, 128), wg=f(w_ffn_gate).reshape(L * D, cfg.FH), wu=f(w_ffn_up).reshape(L * D, cfg.FH),
        wd=f(w_ffn_down).reshape(L * cfg.FH, D), fnw=f(final_norm_w).reshape(KC, 128), **consts)
    maps = []
    for ci in range(n_act):
        m = dict(shared)
        m["xp"] = f(x_prompt[ci * NP:(ci + 1) * NP]).reshape(NP * SEQ, D)
        m["xs"] = f(x_sample[ci * NS:(ci + 1) * NS]).reshape(NS * DSEQ, D)
        m["ck"] = f(cache_na_k[ci * NS:(ci + 1) * NS]).reshape(NS * L * cfg.PAST, 512)
        m["cv"] = f(cache_na_v[ci * NS:(ci + 1) * NS]).reshape(NS * L * cfg.PAST, 512)
        m["st"] = f(state_hgrn[ci * NS:(ci + 1) * NS]).reshape(NS * L * 2 * 8 * 128, 128)
        m["cvec"] = np.concatenate([f(c_ctx).reshape(1, D), f(c[ci * NS:(ci + 1) * NS]).reshape(NS, D)], axis=0)
        maps.append(m)
    return maps


def gather_outputs(cfg, res):
    D, L, NP, NS = cfg.D, cfg.L, cfg.NP, cfg.NS
    yp = np.concatenate([r["yp"].reshape(NP, SEQ, D) for r in res], axis=0)
    ys = np.concatenate([r["ys"].reshape(NS, DSEQ, D) for r in res], axis=0)
    nk = np.concatenate([r["o_nk"].reshape(NP, L, SEQ, 4, 128) for r in res], axis=0)
    nv = np.concatenate([r["o_nv"].reshape(NP, L, SEQ, 4, 128) for r in res], axis=0)
    ns = np.concatenate([r["o_ns"].reshape(NP, L, 2, 8, 128, 128) for r in res], axis=0)
    return (yp.astype(np.float32), ys.astype(np.float32), nk.astype(np.float32), nv.astype(np.float32), ns.astype(np.float32))


def kernel(**inputs):
    xpr = np.asarray(inputs["x_prompt"])
    xsa = np.asarray(inputs["x_sample"])
    n_act = N_ACTIVE
    cfg = Cfg(D=xpr.shape[2], NP=xpr.shape[0] // n_act, NS=xsa.shape[0] // n_act, L=np.asarray(inputs["w_in"]).shape[0],
              PAST=np.asarray(inputs["cache_na_k"]).shape[2])
    nc = build(cfg)
    maps = make_in_maps(cfg, n_act, **inputs)
    res = run_bass_kernel_spmd(nc, maps, core_ids=list(range(n_act)))
    return gather_outputs(cfg, res.results)
```

```python
import numpy as np
from contextlib import ExitStack
import concourse.bass as bass
import concourse.mybir as mybir
from concourse.bass_utils import run_bass_kernel_spmd

F32 = mybir.dt.float32
BF16 = mybir.dt.bfloat16
ALU = mybir.AluOpType
AF = mybir.ActivationFunctionType
AX = mybir.AxisListType
ENGS = ("pe", "act", "dve", "pool", "sp")
EPS = 1e-6
NEG = -30000.0


class Buf:
    __slots__ = ("w", "rs", "excl")

    def __init__(self, excl=False):
        self.w = None
        self.rs = []
        self.excl = excl


class Chan:
    __slots__ = ("sem", "cnt")

    def __init__(self, sem):
        self.sem = sem
        self.cnt = 0


class Prog:
    def __init__(self, nc, es):
        self.nc = nc
        self.es = es
        self.ins = {e: [] for e in ENGS}
        self.esem = {e: es.enter_context(nc.semaphore("s_" + e)) for e in ENGS}
        self.nchan = 0

    def chan(self):
        s = self.es.enter_context(self.nc.semaphore("d%d" % self.nchan))
        self.nchan += 1
        return Chan(s)

    def _deps(self, eng, reads, writes, is_dma):
        deps = []
        for r in reads:
            if r.w is not None:
                deps.append((r.w, True))
            if r.excl:
                for e in r.rs:
                    deps.append((e, False))
        for w in writes:
            if w.w is not None:
                deps.append((w.w, True))
            for e in w.rs:
                deps.append((e, False))
        out = []
        for d, iswr in deps:
            if (not is_dma) and d[0] == "e" and d[1] == eng:
                if eng == "pe":
                    continue
                if not iswr:
                    continue
            out.append(d)
        return out

    def _commit(self, ev, reads, writes):
        for r in reads:
            r.rs.append(ev)
        for w in writes:
            w.w = ev
            w.rs = []

    def op(self, eng, fn, reads=(), writes=()):
        deps = self._deps(eng, reads, writes, False)
        ev = ("e", eng, len(self.ins[eng]))
        self.ins[eng].append([fn, deps, False, None])
        self._commit(ev, reads, writes)

    def dma(self, eng, fn, chan, reads=(), writes=(), inc=16):
        deps = self._deps(eng, reads, writes, True)
        chan.cnt += inc
        ev = ("d", chan, chan.cnt)
        self.ins[eng].append([fn, deps, False, chan])
        self._commit(ev, reads, writes)

    def wait_all(self, eng, bufs):
        deps = []
        for b in bufs:
            if b.w is not None:
                deps.append(b.w)
            deps.extend(b.rs)
        self.ins[eng].append([None, deps, False, None])

    def emit_all(self):
        EPOCH = 16000
        for e in ENGS:
            for it in self.ins[e]:
                for d in it[1]:
                    if d[0] == "e":
                        self.ins[d[1]][d[2]][2] = True
        cnt = {}
        esems = {}
        for e in ENGS:
            c = 0
            arr = []
            for it in self.ins[e]:
                if it[2]:
                    c += 1
                k = max(c - 1, 0)
                arr.append((k // EPOCH, k % EPOCH + 1 if c > 0 else 0))
            cnt[e] = arr
            nep = (max(c - 1, 0)) // EPOCH + 1
            esems[e] = [self.esem[e]] + [self.es.enter_context(self.nc.semaphore("s_%s_%d" % (e, i))) for i in range(1, nep)]

        def emit(eng, h):
            seen_e = {}
            seen_d = {}
            for it_i, it in enumerate(self.ins[eng]):
                need_e = {}
                need_d = {}
                for d in it[1]:
                    if d[0] == "e":
                        ep, c = cnt[d[1]][d[2]]
                        key = (d[1], ep)
                        if c > seen_e.get(key, 0) and c > need_e.get(key, 0):
                            need_e[key] = c
                    else:
                        k = id(d[1])
                        if d[2] > seen_d.get(k, 0) and d[2] > need_d.get(k, (None, 0))[1]:
                            need_d[k] = (d[1], d[2])
                for key, c in need_e.items():
                    h.wait_ge(esems[key[0]][key[1]], c)
                    seen_e[key] = c
                for k, (ch, c) in need_d.items():
                    h.wait_ge(ch.sem, c)
                    seen_d[k] = c
                if it[0] is None:
                    continue
                ins = it[0](h)
                if it[3] is not None:
                    ins.then_inc(it[3].sem, 16)
                elif it[2]:
                    ins.then_inc(esems[eng][cnt[eng][it_i][0]], 1)

        with self.nc.Block() as block:
            @block.tensor
            def _(h):
                emit("pe", h)

            @block.scalar
            def _(h):
                emit("act", h)

            @block.vector
            def _(h):
                emit("dve", h)

            @block.gpsimd
            def _(h):
                emit("pool", h)

            @block.sync
            def _(h):
                emit("sp", h)


class Cfg:
    def __init__(self, D=2048, NP=16, NS=8, L=2, PAST=512, debug=False, stop=99):
        self.stop = stop
        self.D = D
        self.KC = D // 128
        self.FH = ((8 * D + 3 * 256 - 1) // (3 * 256)) * 256
        self.FC = self.FH // 128
        self.NP = NP
        self.NS = NS
        self.L = L
        self.PAST = PAST
        self.PT = PAST // 128
        self.NV = 1 + NS
        self.debug = debug


SEQ = 256
DSEQ = 1024
T = 1024
NTT = 8
GRID_W = 64
ROWS = 16
NA_KH = 8
NA_KW = 16


def host_consts():
    ident = np.eye(128, dtype=np.float32)
    j = np.arange(128)[:, None]
    i = np.arange(128)[None, :]
    same = (j // 32) == (i // 32)
    maskf = (same & (j <= i)).astype(np.float32)
    maskb = (same & (j >= i)).astype(np.float32)
    cstart = np.ones((128, 512), np.float32)
    cstart[:, ::32] = 0.0
    col = np.arange(GRID_W)
    cs = np.clip(col - NA_KW // 2, 0, GRID_W - NA_KW)
    valid = (col[None, :] >= cs[:, None]) & (col[None, :] < cs[:, None] + NA_KW)
    cm = np.where(valid, 0.0, NEG).astype(np.float32)
    colmask = np.concatenate([cm, cm], axis=0)
    rowsel = (np.arange(128)[:, None] // 32 == np.arange(4)[None, :]).astype(np.float32)
    return dict(ident=ident, maskf=maskf, maskb=maskb, cstart=cstart, colmask=colmask, rowsel=rowsel)


def build(cfg):
    D, KC, FH, FC, NP, NS, L, PAST, PT, NV = cfg.D, cfg.KC, cfg.FH, cfg.FC, cfg.NP, cfg.NS, cfg.L, cfg.PAST, cfg.PT, cfg.NV
    nc = bass.Bass("TRN2", target_bir_lowering=False)

    def din(name, shape, dt=F32):
        return nc.dram_tensor(name, list(shape), dt, kind="ExternalInput").ap()

    def dout(name, shape, dt=F32):
        return nc.dram_tensor(name, list(shape), dt, kind="ExternalOutput").ap()

    xp = din("xp", [max(NP, 1) * SEQ, D])
    xs = din("xs", [max(NS, 1) * DSEQ, D])
    ck = din("ck", [max(NS, 1) * L * PAST, 512])
    cv = din("cv", [max(NS, 1) * L * PAST, 512])
    st = din("st", [max(NS, 1) * L * 2 * 8 * 128, 128])
    cvec = din("cvec", [NV, D])
    w_ada = din("w_ada", [L * D, 6 * D])
    b_ada = din("b_ada", [L * 6 * KC, 128])
    nmw = din("nmw", [L * KC, 128])
    w_in = din("w_in", [L * D, 8192])
    lbr = din("lbr", [2 * L * 8, 128])
    gnw = din("gnw", [L, 128])
    cw = din("cw", [L * 12, 128])
    rpb = din("rpb", [L * 60, 31])
    w_out = din("w_out", [L * 2048, D])
    nfw = din("nfw", [L * KC, 128])
    wg = din("wg", [L * D, FH])
    wu = din("wu", [L * D, FH])
    wd = din("wd", [L * FH, D])
    fnw = din("fnw", [KC, 128])
    c_ident = din("ident", [128, 128])
    c_maskf = din("maskf", [128, 128])
    c_maskb = din("maskb", [128, 128])
    c_cstart = din("cstart", [128, 512])
    c_colmask = din("colmask", [128, 64])
    c_rowsel = din("rowsel", [128, 4])

    yp = dout("yp", [max(NP, 1) * SEQ, D])
    ys = dout("ys", [max(NS, 1) * DSEQ, D])
    o_nk = dout("o_nk", [max(NP, 1) * L * SEQ, 512])
    o_nv = dout("o_nv", [max(NP, 1) * L * SEQ, 512])
    o_ns = dout("o_ns", [max(NP, 1) * L * 2 * 8 * 128, 128])

    w_in_r = nc.dram_tensor("w_in_r", [L * 64 * 128, KC * 128], BF16).ap()
    w_out_r = nc.dram_tensor("w_out_r", [L * KC * 128, 16 * 128], BF16).ap()
    wg_r = nc.dram_tensor("wg_r", [L * FC * 128, KC * 128], BF16).ap()
    wu_r = nc.dram_tensor("wu_r", [L * FC * 128, KC * 128], BF16).ap()
    wd_r = nc.dram_tensor("wd_r", [L * KC * 128, FC * 128], BF16).ap()
    rpbp = nc.dram_tensor("rpbp", [L * 60, 127], F32).ap()
    ctab_d = nc.dram_tensor("ctab_d", [L * 128, 60 * 64], F32).ap()

    es = ExitStack()
    with es:
        P = Prog(nc, es)

        def sb(name, shape, dt):
            return es.enter_context(nc.sbuf_tensor("sb_" + name, list(shape), dt))

        def ps(name, shape, dt):
            return es.enter_context(nc.psum_tensor("ps_" + name, list(shape), dt))

        xT = sb("xT", [128, KC, T], F32)
        hT = sb("hT", [128, KC, T], BF16)
        mixT = sb("mixT", [128, 16, T], BF16)
        NSLOT = 4
        wring = [sb("wr%d" % i, [128, 2048], BF16) for i in range(NSLOT)]
        SCRB = 46 * 1024
        scr = sb("scr", [128, SCRB // 4], F32)
        ident_f = sb("ident_f", [128, 128], F32)
        ident_b = sb("ident_b", [128, 128], BF16)
        ones_b = sb("ones_b", [128, 128], BF16)
        maskf = sb("maskf", [128, 128], F32)
        maskb = sb("maskb", [128, 128], F32)
        cstart = sb("cstart", [128, 512], F32)
        colmask = sb("colmask", [128, 64], F32)
        rowsel = sb("rowsel", [128, 4], F32)
        mods = sb("mods", [128, L, 6, KC, NV], F32)
        A1 = sb("A1", [128, L, KC, NV], F32)
        A2 = sb("A2", [128, L, KC, NV], F32)
        badaT = sb("badaT", [128, L, 6 * KC], F32)
        nmwT = sb("nmwT", [128, L * KC], F32)
        nfwT = sb("nfwT", [128, L * KC], F32)
        fnwT = sb("fnwT", [128, KC], F32)
        gnwT = sb("gnwT", [128, L], F32)
        cwT = sb("cwT", [128, L * 12], F32)
        lbT = sb("lbT", [128, 2 * L * 8], F32)
        LB = sb("LB", [128, 2 * L * 8], F32)
        OM = sb("OM", [128, 2 * L * 8], F32)
        NOM = sb("NOM", [128, 2 * L * 8], F32)
        sT = sb("sT", [128, KC, NV], BF16)
        small = sb("small", [128, 64], F32)

        D0 = ps("D0", [128, 512], F32)
        D1 = ps("D1", [128, 512], F32)
        SC = ps("SC", [128, 3, 512], F32)
        TB = ps("TB", [128, 2, 1024], BF16)
        PV = ps("PV", [128, 512], F32)
        bD0, bD1, bPV = Buf(True), Buf(True), Buf(True)
        bSC = [Buf(True), Buf(True), Buf(True)]
        bTB = [Buf(True), Buf(True)]

        def carve(off, shape, dt):
            n = 1
            for s_ in shape:
                n *= s_
            nb = n * (4 if dt == F32 else 2)
            assert off % 4 == 0 and off + nb <= SCRB, (off, nb, SCRB)
            ap = scr[:, off // 4:(off + nb) // 4]
            if dt != F32:
                ap = ap.bitcast(dt)
            if len(shape) == 2:
                ap = ap.rearrange("p (a b) -> p a b", b=shape[1])
            elif len(shape) == 3:
                ap = ap.rearrange("p (a b c) -> p a b c", b=shape[1], c=shape[2])
            return ap, off + nb

        phase_bufs = []

        def new_phase(n):
            prev = []
            for b in phase_bufs:
                if b.w is not None:
                    prev.append(b.w)
                prev.extend(b.rs)
            del phase_bufs[:]
            out = []
            for _ in range(n):
                b = Buf()
                b.rs = list(prev)
                phase_bufs.append(b)
                out.append(b)
            return out

        ch_misc = P.chan()
        ch_stg = P.chan()
        ch_cv = P.chan()
        b_const = Buf()

        def load_const(dst, src):
            P.dma("sp", lambda h: h.dma_start(out=dst, in_=src), ch_misc, writes=[b_const])

        load_const(ident_f[:], c_ident)
        load_const(maskf[:], c_maskf)
        load_const(maskb[:], c_maskb)
        load_const(cstart[:], c_cstart)
        load_const(colmask[:], c_colmask)
        load_const(rowsel[:], c_rowsel)
        b_idb = Buf()
        P.op("dve", lambda h: h.tensor_copy(out=ident_b[:], in_=ident_f[:]), reads=[b_const], writes=[b_idb])
        P.op("pool", lambda h: h.memset(ones_b[:], 1.0), writes=[b_idb])

        (bs_stg, bs_ada0, bs_ada1, bs_cv, bs_z) = new_phase(5)
        o = 0
        stg_v, o = carve(o, [128], F32)
        cv_in, o = carve(o, [KC * 128], F32)
        cv_s = cv_in
        ztile, o = carve(o, [128], F32)
        ada_slot = []
        for i in range(2):
            a_, o = carve(o, [KC, 256], BF16)
            ada_slot.append(a_)
        ctab_s, o = carve(o, [60, 64], F32)

        b_vecs = Buf()

        def load_T(dst, src_rows, n):
            P.dma("sp", lambda h: h.dma_start(out=stg_v[0:n, :], in_=src_rows), ch_stg, writes=[bs_stg])
            P.op("pe", lambda h: h.transpose(out=SC[:, 0, 0:n], in_=stg_v[0:n, :], identity=ident_f[0:n, 0:n]),
                 reads=[bs_stg, b_const], writes=[bSC[0]])
            P.op("dve", lambda h: h.tensor_copy(out=dst, in_=SC[:, 0, 0:n]), reads=[bSC[0]], writes=[b_vecs])

        for l in range(L):
            load_T(badaT[:, l, :], b_ada[l * 6 * KC:(l + 1) * 6 * KC, :], 6 * KC)
        load_T(nmwT[:], nmw, L * KC)
        load_T(nfwT[:], nfw, L * KC)
        load_T(fnwT[:], fnw, KC)
        load_T(gnwT[:], gnw, L)
        load_T(cwT[:], cw, L * 12)
        load_T(lbT[:], lbr, 2 * L * 8)

        b_lb = Buf()
        P.op("act", lambda h: h.activation(out=lbT[:], in_=lbT[:], func=AF.Exp), reads=[b_vecs], writes=[b_vecs])
        for d_ in range(2):
            base = d_ * L * 8
            tot = small[:, 0:8]
            P.op("dve", lambda h, base=base: h.tensor_copy(out=small[:, 0:8], in_=lbT[:, base:base + 8]), reads=[b_vecs], writes=[b_lb])
            for l in range(1, L):
                P.op("dve", lambda h, base=base, l=l: h.tensor_tensor(out=small[:, 0:8], in0=small[:, 0:8], in1=lbT[:, base + l * 8:base + l * 8 + 8], op=ALU.add),
                     reads=[b_vecs, b_lb], writes=[b_lb])
            P.op("dve", lambda h: h.reciprocal(out=small[:, 8:16], in_=small[:, 0:8]), reads=[b_lb], writes=[b_lb])
            P.op("pool", lambda h, base=base: h.memset(LB[:, base:base + 8], 0.0), writes=[b_lb])
            for l in range(1, L):
                P.op("dve", lambda h, base=base, l=l: h.tensor_tensor(out=small[:, 16:24], in0=lbT[:, base + l * 8:base + l * 8 + 8], in1=small[:, 8:16], op=ALU.mult),
                     reads=[b_vecs, b_lb], writes=[b_lb])
                P.op("dve", lambda h, base=base, l=l: h.tensor_tensor(out=LB[:, base + l * 8:base + l * 8 + 8], in0=LB[:, base + (l - 1) * 8:base + (l - 1) * 8 + 8], in1=small[:, 16:24], op=ALU.add),
                     reads=[b_lb], writes=[b_lb])
        P.op("dve", lambda h: h.tensor_scalar(out=OM[:], in0=LB[:], scalar1=-1.0, scalar2=1.0, op0=ALU.mult, op1=ALU.add), reads=[b_lb], writes=[b_lb])
        P.op("dve", lambda h: h.tensor_scalar(out=NOM[:], in0=OM[:], scalar1=-1.0, scalar2=None, op0=ALU.mult), reads=[b_lb], writes=[b_lb])

        P.dma("sp", lambda h: h.dma_start(out=cv_in[0:NV, :], in_=cvec), ch_cv, writes=[bs_cv])
        P.op("act", lambda h: h.activation(out=cv_s[0:NV, :], in_=cv_in[0:NV, :], func=AF.Silu), reads=[bs_cv], writes=[bs_cv])
        b_sT = Buf()
        for k in range(KC):
            P.op("pe", lambda h, k=k: h.transpose(out=SC[:, 0, 0:NV], in_=cv_s[0:NV, k * 128:(k + 1) * 128], identity=ident_f[0:NV, 0:NV]),
                 reads=[bs_cv, b_const], writes=[bSC[0]])
            P.op("dve", lambda h, k=k: h.tensor_copy(out=sT[:, k, :], in_=SC[:, 0, 0:NV]), reads=[bSC[0]], writes=[b_sT])

        ch_ada = [P.chan(), P.chan()]
        b_mods = Buf()
        nblk = (6 * D) // 256
        bi = 0
        for l in range(L):
            wv = w_ada[l * D:(l + 1) * D, :].rearrange("(k p) n -> p k n", p=128)
            for blk in range(nblk):
                s_ = bi % 2
                bi += 1
                bsl = bs_ada0 if s_ == 0 else bs_ada1
                P.dma("pool", lambda h, s_=s_, blk=blk, wv=wv: h.dma_start(out=ada_slot[s_], in_=wv[:, :, blk * 256:(blk + 1) * 256]), ch_ada[s_], writes=[bsl])
                for mm in range(2):
                    mg = blk * 2 + mm
                    for k in range(KC):
                        P.op("pe", lambda h, s_=s_, mm=mm, k=k: h.matmul(PV[:, mm * NV:(mm + 1) * NV], lhsT=ada_slot[s_][:, k, mm * 128:(mm + 1) * 128], rhs=sT[:, k, :], start=(k == 0), stop=(k == KC - 1)),
                             reads=[bsl, b_sT], writes=[bPV])
                    j6, kk_ = mg // KC, mg % KC
                    P.op("dve", lambda h, l=l, mm=mm, mg=mg, j6=j6, kk_=kk_: h.tensor_scalar(out=mods[:, l, j6, kk_, :], in0=PV[:, mm * NV:(mm + 1) * NV], scalar1=badaT[:, l, mg:mg + 1], scalar2=None, op0=ALU.add),
                         reads=[bPV, b_vecs], writes=[b_mods])
        for l in range(L):
            for k in range(KC):
                P.op("dve", lambda h, l=l, k=k: h.tensor_scalar(out=A1[:, l, k, :], in0=mods[:, l, 1, k, :], scalar1=1.0, scalar2=nmwT[:, l * KC + k:l * KC + k + 1], op0=ALU.add, op1=ALU.mult),
                     reads=[b_mods, b_vecs], writes=[b_mods])
                P.op("dve", lambda h, l=l, k=k: h.tensor_scalar(out=A2[:, l, k, :], in0=mods[:, l, 4, k, :], scalar1=1.0, scalar2=nfwT[:, l * KC + k:l * KC + k + 1], op0=ALU.add, op1=ALU.mult),
                     reads=[b_mods, b_vecs], writes=[b_mods])

        ch_tab = P.chan()
        b_rpbp, b_ctabd = Buf(), Buf()
        if NS > 0:
            P.op("pool", lambda h: h.memset(ztile, 0.0), writes=[bs_z])
            P.dma("sp", lambda h: h.dma_start(out=rpbp, in_=ztile[0:L * 60, 0:127]), ch_tab, reads=[bs_z], writes=[b_rpbp])
            P.dma("sp", lambda h: h.dma_start(out=rpbp[:, 48:79], in_=rpb), ch_tab, writes=[b_rpbp])
            ch_tab2 = P.chan()
            b_ct = bs_ada0
            b_cts = Buf()
            phase_bufs.append(b_cts)
            for l in range(L):
                src = rpbp[l * 60:(l + 1) * 60, :]
                first = True
                for w_ in range(64):
                    for ro in range(2):
                        pp_ = ro * 64 + w_
                        if first:
                            P.dma("sp", lambda h, pp_=pp_, w_=w_, src=src: h.dma_start(out=ctab_s[pp_:pp_ + 1, :, :], in_=src[:, 63 - w_:127 - w_].unsqueeze(0)),
                                  ch_tab2, reads=[b_rpbp], writes=[b_cts])
                            first = False
                        else:
                            ch_tab2.cnt += 16
                            ev = ("d", ch_tab2, ch_tab2.cnt)
                            P.ins["sp"].append([lambda h, pp_=pp_, w_=w_, src=src: h.dma_start(out=ctab_s[pp_:pp_ + 1, :, :], in_=src[:, 63 - w_:127 - w_].unsqueeze(0)), [], False, ch_tab2])
                            b_cts.w = ev
                P.op("dve", lambda h: h.tensor_tensor(out=ctab_s, in0=ctab_s, in1=colmask[:].unsqueeze(1).to_broadcast([128, 60, 64]), op=ALU.add),
                     reads=[b_cts, b_const], writes=[b_cts])
                P.dma("sp", lambda h, l=l: h.dma_start(out=ctab_d[l * 128:(l + 1) * 128, :], in_=ctab_s.rearrange("p a b -> p (a b)")), ch_tab, reads=[b_cts], writes=[b_ctabd])

        def precast(dst, src, rows_in, cols_out, l, nchunk_k):
            ch = P.chan()
            b = Buf()
            sv = src[l * rows_in:(l + 1) * rows_in, :].rearrange("(k p) (m c) -> m p k c", p=128, c=128)
            nm = cols_out // 128
            ev = None
            for m in range(nm):
                r0 = (l * nm + m) * 128
                for k0 in range(0, nchunk_k, 16):
                    k1 = min(nchunk_k, k0 + 16)
                    ch.cnt += 16
                    ev = ("d", ch, ch.cnt)
                    P.ins["pool"].append([lambda h, m=m, r0=r0, sv=sv, k0=k0, k1=k1: h.dma_start(out=dst[r0:r0 + 128, :].rearrange("p (k c) -> p k c", c=128)[:, k0:k1, :], in_=sv[m][:, k0:k1, :]), [], False, ch])
            b.w = ev
            return b

        bw_in, bw_out, bw_g, bw_u, bw_d = [], [], [], [], []
        for l in range(L if cfg.stop >= 2 else 0):
            bw_in.append(precast(w_in_r, w_in, D, 8192, l, KC))
            bw_out.append(precast(w_out_r, w_out, 2048, D, l, 16))
            bw_g.append(precast(wg_r, wg, D, FH, l, KC))
            bw_u.append(precast(wu_r, wu, D, FH, l, KC))
            bw_d.append(precast(wd_r, wd, FH, D, l, FC))

        ring_ch = [P.chan() for _ in range(NSLOT)]
        ring_b = [Buf() for _ in range(NSLOT)]
        ring_i = [0]

        def wload(src_tile, width, bsrc):
            s_ = ring_i[0] % NSLOT
            ring_i[0] += 1
            P.dma("sp", lambda h, s_=s_: h.dma_start(out=wring[s_][:, 0:width], in_=src_tile), ring_ch[s_], reads=[bsrc], writes=[ring_b[s_]])
            return wring[s_], ring_b[s_]

        bx = [[Buf() for _ in range(2)] for _ in range(KC)]
        bh = [[Buf() for _ in range(2)] for _ in range(KC)]
        bm = [[Buf() for _ in range(2)] for _ in range(16)]

        dense_banks = [(D0, bD0), (D1, bD1)]
        dense_banks4 = dense_banks + [(SC[:, 0, :], bSC[0]), (SC[:, 1, :], bSC[1])]
        dbi = [0]

        def next_bank(banks):
            b = banks[dbi[0] % len(banks)]
            dbi[0] += 1
            return b

        def hs(n):
            return slice(n * 512, (n + 1) * 512)

        ch_xin = [P.chan(), P.chan()]
        ch_out = [P.chan(), P.chan()]

        def load_x(src_rows):
            (b_s0, b_s1) = new_phase(2)
            o = 0
            stg = []
            for i in range(2):
                a_, o = carve(o, [D], F32)
                stg.append(a_)
            bst = [b_s0, b_s1]
            for tt in range(NTT):
                s_ = tt % 2
                P.dma("sp", lambda h, s_=s_, tt=tt: h.dma_start(out=stg[s_], in_=src_rows[tt * 128:(tt + 1) * 128, :]), ch_xin[s_], writes=[bst[s_]])
                for k0 in range(0, KC, 4):
                    nk_ = min(4, KC - k0)
                    for kk_ in range(nk_):
                        P.op("pe", lambda h, s_=s_, k0=k0, kk_=kk_: h.transpose(out=SC[:, 0, kk_ * 128:(kk_ + 1) * 128], in_=stg[s_][:, (k0 + kk_) * 128:(k0 + kk_ + 1) * 128], identity=ident_f[:]),
                             reads=[bst[s_], b_const], writes=[bSC[0]])
                    P.op("act", lambda h, k0=k0, nk_=nk_, tt=tt: h.activation(out=xT[:, k0:k0 + nk_, tt * 128:(tt + 1) * 128], in_=SC[:, 0, 0:nk_ * 128].rearrange("p (a b) -> p a b", b=128), func=AF.Copy),
                         reads=[bSC[0]], writes=[bx[k][tt // 4] for k in range(k0, k0 + nk_)])

        def norm_stats(n, rbuf_ap, b_r, sq, b_sq):
            for k in range(KC):
                s_ = k % 2
                P.op("act", lambda h, k=k, s_=s_: h.activation(out=sq[s_], in_=xT[:, k, hs(n)], func=AF.Square), reads=[bx[k][n]], writes=[b_sq[s_]])
                P.op("pe", lambda h, k=k, s_=s_: h.matmul(PV[:], lhsT=ones_b[:], rhs=sq[s_], start=(k == 0), stop=(k == KC - 1)), reads=[b_sq[s_], b_idb], writes=[bPV])
            P.op("act", lambda h: h.activation(out=rbuf_ap, in_=PV[:], func=AF.Sqrt, scale=1.0 / D, bias=eps_ap), reads=[bPV, b_eps], writes=[b_r])
            P.op("dve", lambda h: h.reciprocal(out=rbuf_ap, in_=rbuf_ap), reads=[b_r], writes=[b_r])

        eps_ap = small[:, 32:33]
        b_eps = Buf()
        P.op("pool", lambda h: h.memset(small[:, 32:33], EPS), writes=[b_eps])

        def norm_mod(l, v, Aap, shj):
            (b_q0, b_q1, b_r, b_t0, b_t1) = new_phase(5)
            o = 0
            sq = []
            for i in range(2):
                a_, o = carve(o, [512], BF16)
                sq.append(a_)
            rbuf, o = carve(o, [512], F32)
            tt_ = []
            for i in range(2):
                a_, o = carve(o, [512], F32)
                tt_.append(a_)
            b_t = [b_t0, b_t1]
            for n in range(2):
                norm_stats(n, rbuf, b_r, sq, [b_q0, b_q1])
                for k in range(KC):
                    s_ = k % 2
                    P.op("dve", lambda h, k=k, s_=s_, n=n: h.scalar_tensor_tensor(out=tt_[s_], in0=xT[:, k, hs(n)], scalar=Aap[:, l, k, v:v + 1], in1=rbuf, op0=ALU.mult, op1=ALU.mult),
                         reads=[bx[k][n], b_r, b_mods], writes=[b_t[s_]])
                    P.op("act", lambda h, k=k, s_=s_, n=n: h.activation(out=hT[:, k, hs(n)], in_=tt_[s_], func=AF.Identity, bias=mods[:, l, shj, k, v:v + 1], scale=1.0),
                         reads=[b_t[s_], b_mods], writes=[bh[k][n]])

        def inproj_fm(l, m, evac, halves=(0, 1)):
            slot, bsl = wload(w_in_r[(l * 64 + m) * 128:(l * 64 + m + 1) * 128, :], KC * 128, bw_in[l])
            for n in halves:
                bank, bb = next_bank(dense_banks)
                for k in range(KC):
                    P.op("pe", lambda h, k=k, n=n, bank=bank, slot=slot: h.matmul(bank[:] if bank is D0 or bank is D1 else bank, lhsT=slot[:, k * 128:(k + 1) * 128], rhs=hT[:, k, hs(n)], start=(k == 0), stop=(k == KC - 1)),
                         reads=[bsl, bh[k][n]], writes=[bb])
                evac(n, bank[:] if bank is D0 or bank is D1 else bank, bb)

        def inproj_tm(l, m, evac):
            slot, bsl = wload(w_in_r[(l * 64 + m) * 128:(l * 64 + m + 1) * 128, :], KC * 128, bw_in[l])
            for n in range(2):
                bank, bb = next_bank(dense_banks)
                bap = bank[:] if (bank is D0 or bank is D1) else bank
                for t4 in range(4):
                    tt = n * 4 + t4
                    for k in range(KC):
                        P.op("pe", lambda h, k=k, tt=tt, t4=t4, bap=bap, slot=slot: h.matmul(bap[:, t4 * 128:(t4 + 1) * 128], lhsT=hT[:, k, tt * 128:(tt + 1) * 128], rhs=slot[:, k * 128:(k + 1) * 128], start=(k == 0), stop=(k == KC - 1)),
                             reads=[bsl, bh[k][n]], writes=[bb])
                evac(n, bap.rearrange("p (a b) -> p a b", b=128), bb)

        ch_ctx = P.chan()
        ch_ctx2 = P.chan()
        ch_tabl = P.chan()

        def attention(l, kind, u):
            nb_ = 14
            bb_ = new_phase(nb_)
            (b_q, b_k, b_v, b_ckb, b_ckT, b_cvb, b_sl, b_pf, b_pc, b_pt, b_st, b_og0, b_og1, b_tab) = bb_
            o = 0
            qTh, o = carve(o, [T], BF16)
            kTh, o = carve(o, [T], BF16)
            vh, o = carve(o, [NTT, 128], BF16)
            ckb, o = carve(o, [PT, 128], BF16)
            ckT, o = carve(o, [PAST], BF16)
            cvb, o = carve(o, [PT, 128], BF16)
            sloc, o = carve(o, [512], F32)
            pfull, o = carve(o, [640], BF16)
            pctx, o = carve(o, [512], BF16)
            ptr, o = carve(o, [9, 128], BF16)
            stat, o = carve(o, [16], F32)
            ostg = []
            for i in range(2):
                a_, o = carve(o, [4, 128], F32)
                ostg.append(a_)
            b_og = [b_og0, b_og1]
            if kind == "s":
                tab, o = carve(o, [60, 64], F32)
                P.dma("sp", lambda h: h.dma_start(out=tab, in_=ctab_d[l * 128:(l + 1) * 128, :].rearrange("p (a b) -> p a b", b=64)), ch_tabl, reads=[b_ctabd], writes=[b_tab])
            ogi = [0]
            for hd in range(4):
                def ev_q(n, pa, pb):
                    P.op("act", lambda h, n=n, pa=pa: h.activation(out=qTh[:, hs(n)], in_=pa, func=AF.Copy, scale=128.0 ** -0.5), reads=[pb], writes=[b_q])
                inproj_fm(l, 52 + hd, ev_q)

                def ev_k(n, pa, pb):
                    P.op("dve", lambda h, n=n, pa=pa: h.tensor_copy(out=kTh[:, hs(n)], in_=pa), reads=[pb], writes=[b_k])
                if getattr(cfg, "att_sub", 9) >= -1:
                    inproj_fm(l, 56 + hd, ev_k)

                def ev_v(n, pa, pb, hd=hd):
                    P.op("act", lambda h, n=n, pa=pa: h.activation(out=vh[:, n * 4:(n + 1) * 4, :], in_=pa, func=AF.Copy), reads=[pb], writes=[b_v])
                    if kind == "p" and getattr(cfg, "att_sub", 9) >= 1:
                        s_ = ogi[0] % 2
                        ogi[0] += 1
                        P.op("dve", lambda h, pa=pa, s_=s_: h.tensor_copy(out=ostg[s_], in_=pa), reads=[pb], writes=[b_og[s_]])
                        for t4 in range(4):
                            sq_ = u * 4 + n * 2 + t4 // 2
                            r0 = (sq_ * L + l) * SEQ + (t4 % 2) * 128
                            P.dma("sp", lambda h, s_=s_, t4=t4, r0=r0, hd=hd: h.dma_start(out=o_nv[r0:r0 + 128, hd * 128:(hd + 1) * 128], in_=ostg[s_][:, t4, :]),
                                  ch_out[s_], reads=[b_og[s_]])
                if getattr(cfg, "att_sub", 9) >= 0:
                    inproj_tm(l, 60 + hd, ev_v)
                if getattr(cfg, "att_sub", 9) < 1:
                    continue
                if kind == "p":
                    def ev_ko(n, pa, pb, hd=hd):
                        s_ = ogi[0] % 2
                        ogi[0] += 1
                        P.op("dve", lambda h, pa=pa, s_=s_: h.tensor_copy(out=ostg[s_], in_=pa), reads=[pb], writes=[b_og[s_]])
                        for t4 in range(4):
                            sq_ = u * 4 + n * 2 + t4 // 2
                            r0 = (sq_ * L + l) * SEQ + (t4 % 2) * 128
                            P.dma("sp", lambda h, s_=s_, t4=t4, r0=r0, hd=hd: h.dma_start(out=o_nk[r0:r0 + 128, hd * 128:(hd + 1) * 128], in_=ostg[s_][:, t4, :]),
                                  ch_out[s_], reads=[b_og[s_]])
                    inproj_tm(l, 56 + hd, ev_ko)
                else:
                    r0 = (u * L + l) * PAST
                    P.dma("pool", lambda h, r0=r0, hd=hd: h.dma_start(out=ckb, in_=ck[r0:r0 + PAST, hd * 128:(hd + 1) * 128].rearrange("(a p) c -> p a c", p=128)), ch_ctx, writes=[b_ckb])
                    P.dma("pool", lambda h, r0=r0, hd=hd: h.dma_start(out=cvb, in_=cv[r0:r0 + PAST, hd * 128:(hd + 1) * 128].rearrange("(a p) c -> p a c", p=128)), ch_ctx2, writes=[b_cvb])
                    for lt in range(PT):
                        P.op("pe", lambda h, lt=lt: h.transpose(out=TB[:, 1, lt * 128:(lt + 1) * 128], in_=ckb[:, lt, :], identity=ident_b[:]), reads=[b_ckb, b_idb], writes=[bTB[1]])
                    P.op("act", lambda h: h.activation(out=ckT, in_=TB[:, 1, 0:PAST], func=AF.Copy), reads=[bTB[1]], writes=[b_ckT])
                asub = getattr(cfg, "att_sub", 9)
                for R in range(NTT if asub >= 2 else 0):
                    if kind == "p":
                        sq_ = R // 2
                        ktiles = [2 * sq_, 2 * sq_ + 1]
                        nloc = 2
                        P.op("pe", lambda h, R=R, sq_=sq_: h.matmul(SC[:, 0, 0:256], lhsT=qTh[:, R * 128:(R + 1) * 128], rhs=kTh[:, sq_ * 256:(sq_ + 1) * 256], start=True, stop=True),
                             reads=[b_q, b_k], writes=[bSC[0]])
                        P.op("dve", lambda h: h.tensor_reduce(out=stat[:, 0:1], in_=SC[:, 0, 0:256], axis=AX.X, op=ALU.max), reads=[bSC[0]], writes=[b_st])
                        P.op("dve", lambda h: h.tensor_scalar(out=stat[:, 1:2], in0=stat[:, 0:1], scalar1=-1.0, scalar2=None, op0=ALU.mult), reads=[b_st], writes=[b_st])
                        P.op("act", lambda h: h.activation(out=pfull[:, 0:256], in_=SC[:, 0, 0:256], func=AF.Exp, bias=stat[:, 1:2], scale=1.0, accum_out=stat[:, 2:3]),
                             reads=[bSC[0], b_st], writes=[b_pf, b_st])
                        P.op("dve", lambda h: h.reciprocal(out=stat[:, 3:4], in_=stat[:, 2:3]), reads=[b_st], writes=[b_st])
                        P.op("dve", lambda h: h.tensor_scalar(out=pfull[:, 0:256], in0=pfull[:, 0:256], scalar1=stat[:, 3:4], scalar2=None, op0=ALU.mult), reads=[b_st, b_pf], writes=[b_pf])
                        nctx = 0
                    else:
                        r_a, r_b = 2 * R, 2 * R + 1
                        rs_a = min(max(r_a - NA_KH // 2, 0), ROWS - NA_KH)
                        rs_b = min(max(r_b - NA_KH // 2, 0), ROWS - NA_KH)
                        kt0 = rs_a // 2
                        kt1 = (rs_b + NA_KH - 1) // 2
                        nloc = kt1 - kt0 + 1
                        ktiles = list(range(kt0, kt0 + nloc))
                        nctx = PT
                        w1 = min(nloc, 4) * 128
                        P.op("pe", lambda h, R=R, kt0=kt0, w1=w1: h.matmul(SC[:, 0, 0:w1], lhsT=qTh[:, R * 128:(R + 1) * 128], rhs=kTh[:, kt0 * 128:kt0 * 128 + w1], start=True, stop=True),
                             reads=[b_q, b_k], writes=[bSC[0]])
                        if nloc > 4:
                            P.op("pe", lambda h, R=R, kt0=kt0: h.matmul(SC[:, 1, 0:128], lhsT=qTh[:, R * 128:(R + 1) * 128], rhs=kTh[:, (kt0 + 4) * 128:(kt0 + 5) * 128], start=True, stop=True),
                                 reads=[b_q, b_k], writes=[bSC[1]])
                        P.op("pe", lambda h, R=R: h.matmul(SC[:, 2, 0:PAST], lhsT=qTh[:, R * 128:(R + 1) * 128], rhs=ckT, start=True, stop=True),
                             reads=[b_q, b_ckT], writes=[bSC[2]])
                        scflat = SC[:].rearrange("p a b -> p (a b)")
                        c0s = []
                        for ro, (r_, rs_) in enumerate(((r_a, rs_a), (r_b, rs_b))):
                            c0 = (rs_ - 2 * kt0) * 64
                            c0s.append(c0)
                            dr0 = rs_ - r_ + 7
                            psl = slice(ro * 64, ro * 64 + 64)
                            P.op("dve", lambda h, psl=psl, c0=c0, dr0=dr0, hd=hd: h.tensor_tensor(out=sloc[psl, :], in0=scflat[psl, c0:c0 + 512], in1=tab[psl, hd * 15 + dr0:hd * 15 + dr0 + 8, :].rearrange("p a b -> p (a b)"), op=ALU.add),
                                 reads=[bSC[0], bSC[1], b_tab], writes=[b_sl])
                        P.op("dve", lambda h: h.tensor_reduce(out=stat[:, 0:1], in_=sloc, axis=AX.X, op=ALU.max), reads=[b_sl], writes=[b_st])
                        P.op("dve", lambda h: h.tensor_reduce(out=stat[:, 4:5], in_=SC[:, 2, 0:PAST], axis=AX.X, op=ALU.max), reads=[bSC[2]], writes=[b_st])
                        P.op("dve", lambda h: h.tensor_tensor(out=stat[:, 0:1], in0=stat[:, 0:1], in1=stat[:, 4:5], op=ALU.max), reads=[b_st], writes=[b_st])
                        P.op("dve", lambda h: h.tensor_scalar(out=stat[:, 1:2], in0=stat[:, 0:1], scalar1=-1.0, scalar2=None, op0=ALU.mult), reads=[b_st], writes=[b_st])
                        P.op("pool", lambda h: h.memset(pfull, 0.0), writes=[b_pf])
                        for ro in range(2):
                            psl = slice(ro * 64, ro * 64 + 64)
                            c0 = c0s[ro]
                            P.op("act", lambda h, psl=psl, c0=c0: h.activation(out=pfull[psl, c0:c0 + 512], in_=sloc[psl, :], func=AF.Exp, bias=stat[psl, 1:2], scale=1.0, accum_out=stat[psl, 2:3]),
                                 reads=[b_sl, b_st], writes=[b_pf, b_st])
                        P.op("act", lambda h: h.activation(out=pctx, in_=SC[:, 2, 0:PAST], func=AF.Exp, bias=stat[:, 1:2], scale=1.0, accum_out=stat[:, 5:6]),
                             reads=[bSC[2], b_st], writes=[b_pc, b_st])
                        P.op("dve", lambda h: h.tensor_tensor(out=stat[:, 2:3], in0=stat[:, 2:3], in1=stat[:, 5:6], op=ALU.add), reads=[b_st], writes=[b_st])
                        P.op("dve", lambda h: h.reciprocal(out=stat[:, 3:4], in_=stat[:, 2:3]), reads=[b_st], writes=[b_st])
                        P.op("dve", lambda h, nloc=nloc: h.tensor_scalar(out=pfull[:, 0:nloc * 128], in0=pfull[:, 0:nloc * 128], scalar1=stat[:, 3:4], scalar2=None, op0=ALU.mult), reads=[b_st, b_pf], writes=[b_pf])
                        P.op("dve", lambda h: h.tensor_scalar(out=pctx, in0=pctx, scalar1=stat[:, 3:4], scalar2=None, op0=ALU.mult), reads=[b_st, b_pc], writes=[b_pc])
                    if asub < 3:
                        continue
                    ntot = nloc + nctx
                    for i in range(ntot):
                        src = pfull[:, i * 128:(i + 1) * 128] if i < nloc else pctx[:, (i - nloc) * 128:(i - nloc + 1) * 128]
                        bk = 0 if i < 8 else 1
                        ii = i if i < 8 else i - 8
                        P.op("pe", lambda h, src=src, bk=bk, ii=ii: h.transpose(out=TB[:, bk, ii * 128:(ii + 1) * 128], in_=src, identity=ident_b[:]),
                             reads=[b_pf, b_pc, b_idb], writes=[bTB[bk]])
                    n0 = min(ntot, 8)
                    P.op("act", lambda h, n0=n0: h.activation(out=ptr[:, 0:n0, :], in_=TB[:, 0, 0:n0 * 128].rearrange("p (a b) -> p a b", b=128), func=AF.Copy), reads=[bTB[0]], writes=[b_pt])
                    if ntot > 8:
                        P.op("act", lambda h: h.activation(out=ptr[:, 8, :], in_=TB[:, 1, 0:128], func=AF.Copy), reads=[bTB[1]], writes=[b_pt])
                    if asub < 4:
                        continue
                    for i in range(ntot):
                        if i < nloc:
                            lh = vh[:, ktiles[i], :]
                            rd = [b_v, b_pt]
                        else:
                            lh = cvb[:, i - nloc, :]
                            rd = [b_cvb, b_pt]
                        P.op("pe", lambda h, lh=lh, i=i, ntot=ntot: h.matmul(PV[:, 0:128], lhsT=lh, rhs=ptr[:, i, :], start=(i == 0), stop=(i == ntot - 1)), reads=rd, writes=[bPV])
                    P.op("act", lambda h, hd=hd, R=R: h.activation(out=mixT[:, 12 + hd, R * 128:(R + 1) * 128], in_=PV[:, 0:128], func=AF.Copy), reads=[bPV], writes=[bm[12 + hd][R // 4]])

        def conv_mixer(l, nseq, slen):
            (b_cc, b_u, b_y, b_cb) = new_phase(4)
            o = 0
            cc_sb, o = carve(o, [T], F32)
            u_sb, o = carve(o, [T], F32)
            y_sb, o = carve(o, [T], F32)
            cb_sb, o = carve(o, [T], F32)

            def v3(ap):
                return ap.rearrange("p (s t) -> p s t", t=slen)
            for j in range(4):
                def ev_cc(n, pa, pb):
                    P.op("act", lambda h, n=n, pa=pa: h.activation(out=cc_sb[:, hs(n)], in_=pa, func=AF.Copy), reads=[pb], writes=[b_cc])
                inproj_fm(l, 44 + j, ev_cc)

                def ev_cx(n, pa, pb):
                    P.op("dve", lambda h, n=n, pa=pa: h.tensor_tensor(out=u_sb[:, hs(n)], in0=pa, in1=cc_sb[:, hs(n)], op=ALU.mult), reads=[pb, b_cc], writes=[b_u])
                inproj_fm(l, 48 + j, ev_cx)

                def ev_cb(n, pa, pb):
                    P.op("act", lambda h, n=n, pa=pa: h.activation(out=cb_sb[:, hs(n)], in_=pa, func=AF.Copy), reads=[pb], writes=[b_cb])
                inproj_fm(l, 40 + j, ev_cb)
                c0 = (l * 3 + 0) * 4 + j
                c1 = (l * 3 + 1) * 4 + j
                c2 = (l * 3 + 2) * 4 + j
                P.op("dve", lambda h, c1=c1: h.tensor_scalar(out=y_sb, in0=u_sb, scalar1=cwT[:, c1:c1 + 1], scalar2=None, op0=ALU.mult), reads=[b_u, b_vecs], writes=[b_y])
                P.op("dve", lambda h, c0=c0: h.scalar_tensor_tensor(out=v3(y_sb)[:, :, 1:slen], in0=v3(u_sb)[:, :, 0:slen - 1], scalar=cwT[:, c0:c0 + 1], in1=v3(y_sb)[:, :, 1:slen], op0=ALU.mult, op1=ALU.add),
                     reads=[b_u, b_y, b_vecs], writes=[b_y])
                P.op("dve", lambda h, c2=c2: h.scalar_tensor_tensor(out=v3(y_sb)[:, :, 0:slen - 1], in0=v3(u_sb)[:, :, 1:slen], scalar=cwT[:, c2:c2 + 1], in1=v3(y_sb)[:, :, 0:slen - 1], op0=ALU.mult, op1=ALU.add),
                     reads=[b_u, b_y, b_vecs], writes=[b_y])
                for n in range(2):
                    P.op("dve", lambda h, n=n, j=j: h.tensor_tensor(out=mixT[:, 8 + j, hs(n)], in0=y_sb[:, hs(n)], in1=cb_sb[:, hs(n)], op=ALU.mult), reads=[b_y, b_cb], writes=[bm[8 + j][n]])

        ch_st = [P.chan(), P.chan()]
        ch_so = [P.chan() for _ in range(16)]

        def hgrn(l, kind, u):
            names = ["q", "xg", "v", "v4", "oacc", "s", "F", "kk", "Bc", "B", "tE", "qt", "kt", "kh", "khT", "U", "Sp", "at", "c0", "c1", "dec"]
            bl = new_phase(len(names) + 32)
            B_ = dict(zip(names, bl))
            bU = bl[len(names):len(names) + 16]
            bSp = bl[len(names) + 16:len(names) + 32]
            o = 0
            q_bf, o = carve(o, [T], BF16)
            xg, o = carve(o, [T], BF16)
            v_tok, o = carve(o, [NTT, 128], BF16)
            V4, o = carve(o, [4, 4 * 128], BF16)
            o_acc, o = carve(o, [T], F32)
            s_sb, o = carve(o, [T], F32)
            Fb, o = carve(o, [512], F32)
            kkb, o = carve(o, [512], BF16)
            Bc, o = carve(o, [512], F32)
            Bb, o = carve(o, [512], F32)
            tE, o = carve(o, [512], F32)
            qt, o = carve(o, [512], BF16)
            kt, o = carve(o, [512], BF16)
            kh, o = carve(o, [512], BF16)
            khT, o = carve(o, [4, 128], BF16)
            U, o = carve(o, [16, 128], F32)
            Sp, o = carve(o, [16, 128], BF16)
            at_sb, o = carve(o, [4, 128], BF16)
            carry = []
            for i in range(2):
                a_, o = carve(o, [128], F32)
                carry.append(a_)
            dec, o = carve(o, [16], F32)
            bcar = [B_["c0"], B_["c1"]]
            chunks_per_seq = (SEQ if kind == "p" else DSEQ) // 32

            for hd in range(8):
                def ev_q(n, pa, pb):
                    P.op("act", lambda h, n=n, pa=pa: h.activation(out=q_bf[:, hs(n)], in_=pa, func=AF.Copy, scale=128.0 ** -0.5), reads=[pb], writes=[B_["q"]])
                inproj_fm(l, hd, ev_q)

                def ev_g(n, pa, pb):
                    P.op("dve", lambda h, n=n, pa=pa: h.tensor_copy(out=xg[:, hs(n)], in_=pa), reads=[pb], writes=[B_["xg"]])
                inproj_fm(l, 32 + hd, ev_g)

                def ev_v(n, pa, pb):
                    P.op("act", lambda h, n=n, pa=pa: h.activation(out=v_tok[:, n * 4:(n + 1) * 4, :], in_=pa, func=AF.Copy), reads=[pb], writes=[B_["v"]])
                inproj_tm(l, 8 + hd, ev_v)

                for dr in range(2):
                    col = (dr * L + l) * 8 + hd
                    lb_ap, om_ap, nom_ap = LB[:, col:col + 1], OM[:, col:col + 1], NOM[:, col:col + 1]

                    def ev_f(n, pa, pb):
                        P.op("act", lambda h, n=n, pa=pa: h.activation(out=s_sb[:, hs(n)], in_=pa, func=AF.Sigmoid), reads=[pb], writes=[B_["s"]])
                    inproj_fm(l, (16 if dr == 0 else 24) + hd, ev_f)
                    mask = maskf if dr == 0 else maskb
                    ci = 0
                    have_state = False
                    for sg in ((0, 1) if dr == 0 else (1, 0)):
                        cs_ = hs(sg)
                        P.op("dve", lambda h, cs_=cs_, om_ap=om_ap, lb_ap=lb_ap: h.tensor_scalar(out=Fb, in0=s_sb[:, cs_], scalar1=om_ap, scalar2=lb_ap, op0=ALU.mult, op1=ALU.add), reads=[B_["s"], b_lb], writes=[B_["F"]])
                        P.op("dve", lambda h, cs_=cs_, om_ap=om_ap, nom_ap=nom_ap: h.tensor_scalar(out=kkb, in0=s_sb[:, cs_], scalar1=nom_ap, scalar2=om_ap, op0=ALU.mult, op1=ALU.add), reads=[B_["s"], b_lb], writes=[B_["kk"]])
                        P.op("act", lambda h: h.activation(out=Fb, in_=Fb, func=AF.Ln), reads=[B_["F"]], writes=[B_["F"]])
                        P.op("dve", lambda h: h.tensor_tensor_scan(out=Bc, data0=cstart[:], data1=Fb, initial=0.0, op0=ALU.mult, op1=ALU.add), reads=[B_["F"], b_const], writes=[B_["Bc"]])
                        Bc3 = Bc.rearrange("p (c j) -> p c j", j=32)
                        tot = Bc3[:, :, 31]
                        totb = tot.unsqueeze(2).to_broadcast([128, 16, 32])
                        if dr == 0:
                            Bsrc, bB = Bc, B_["Bc"]
                        else:
                            P.op("dve", lambda h: h.tensor_tensor(out=Bb.rearrange("p (c j) -> p c j", j=32), in0=totb, in1=Bc3, op=ALU.subtract), reads=[B_["Bc"]], writes=[B_["B"]])
                            P.op("dve", lambda h: h.tensor_tensor(out=Bb, in0=Bb, in1=Fb, op=ALU.add), reads=[B_["B"], B_["F"]], writes=[B_["B"]])
                            Bsrc, bB = Bb, B_["B"]
                        P.op("act", lambda h, Bsrc=Bsrc: h.activation(out=tE, in_=Bsrc, func=AF.Exp), reads=[bB], writes=[B_["tE"]])
                        P.op("dve", lambda h, cs_=cs_: h.tensor_tensor(out=qt, in0=q_bf[:, cs_], in1=tE, op=ALU.mult), reads=[B_["q"], B_["tE"]], writes=[B_["qt"]])
                        P.op("act", lambda h, Bsrc=Bsrc: h.activation(out=tE, in_=Bsrc, func=AF.Exp, scale=-1.0), reads=[bB, B_["qt"]], writes=[B_["tE"]])
                        P.op("dve", lambda h: h.tensor_tensor(out=kt, in0=kkb, in1=tE, op=ALU.mult), reads=[B_["kk"], B_["tE"]], writes=[B_["kt"]])
                        P.op("dve", lambda h, Bsrc=Bsrc: h.tensor_tensor(out=tE.rearrange("p (c j) -> p c j", j=32), in0=totb, in1=Bsrc.rearrange("p (c j) -> p c j", j=32), op=ALU.subtract),
                             reads=[bB, B_["Bc"], B_["kt"]], writes=[B_["tE"]])
                        P.op("act", lambda h: h.activation(out=tE, in_=tE, func=AF.Exp), reads=[B_["tE"]], writes=[B_["tE"]])
                        P.op("dve", lambda h: h.tensor_tensor(out=kh, in0=kkb, in1=tE, op=ALU.mult), reads=[B_["kk"], B_["tE"]], writes=[B_["kh"]])
                        P.op("act", lambda h: h.activation(out=dec, in_=tot, func=AF.Exp), reads=[B_["Bc"]], writes=[B_["dec"]])
                        for t4 in range(4):
                            P.op("pe", lambda h, t4=t4: h.transpose(out=TB[:, 0, t4 * 128:(t4 + 1) * 128], in_=kh[:, t4 * 128:(t4 + 1) * 128], identity=ident_b[:]), reads=[B_["kh"], b_idb], writes=[bTB[0]])
                        P.op("act", lambda h: h.activation(out=khT, in_=TB[:, 0, 0:512].rearrange("p (a b) -> p a b", b=128), func=AF.Copy), reads=[bTB[0]], writes=[B_["khT"]])
                        V43 = V4.rearrange("p t (c v) -> p t c v", v=128)
                        for c in range(4):
                            P.op("pool", lambda h, c=c, sg=sg: h.tensor_scalar(out=V43[:, :, c, :], in0=v_tok[:, sg * 4:(sg + 1) * 4, :], scalar1=rowsel[:, c:c + 1], scalar2=None, op0=ALU.mult),
                                 reads=[B_["v"], b_const], writes=[B_["v4"]])
                        for t4 in range(4):
                            P.op("pe", lambda h, t4=t4: h.matmul(SC[:, 0, :], lhsT=khT[:, t4, :], rhs=V4[:, t4, :], start=True, stop=True), reads=[B_["khT"], B_["v4"]], writes=[bSC[0]])
                            P.op("act", lambda h, t4=t4: h.activation(out=U[:, t4 * 4:(t4 + 1) * 4, :], in_=SC[:, 0, :].rearrange("p (a b) -> p a b", b=128), func=AF.Copy), reads=[bSC[0]], writes=bU[t4 * 4:(t4 + 1) * 4])
                        order = list(range(16)) if dr == 0 else list(range(15, -1, -1))
                        prev = None
                        prev_b = None
                        for cg in order:
                            gch = sg * 16 + cg
                            sidx = gch // chunks_per_seq
                            pos = gch % chunks_per_seq
                            seq_start = (pos == 0) if dr == 0 else (pos == chunks_per_seq - 1)
                            seq_end = (pos == chunks_per_seq - 1) if dr == 0 else (pos == 0)
                            if seq_start:
                                if kind == "p":
                                    prev, prev_b = None, None
                                else:
                                    r0 = (((u * L + l) * 2 + dr) * 8 + hd) * 128
                                    P.dma("sp", lambda h, r0=r0, ci=ci: h.dma_start(out=carry[ci], in_=st[r0:r0 + 128, :]), ch_st[ci], writes=[bcar[ci]])
                                    prev, prev_b = carry[ci], bcar[ci]
                            elif cg == order[0]:
                                prev, prev_b = carry[ci], bcar[ci]
                            if prev is None:
                                P.op("pool", lambda h, cg=cg: h.memset(Sp[:, cg, :], 0.0), writes=[bSp[cg]])
                            else:
                                P.op("pool", lambda h, cg=cg, prev=prev: h.tensor_copy(out=Sp[:, cg, :], in_=prev), reads=[prev_b], writes=[bSp[cg]])
                                P.op("dve", lambda h, cg=cg, prev=prev: h.scalar_tensor_tensor(out=U[:, cg, :], in0=prev, scalar=dec[:, cg:cg + 1], in1=U[:, cg, :], op0=ALU.mult, op1=ALU.add),
                                     reads=[prev_b, B_["dec"], bU[cg]], writes=[bU[cg]])
                            prev, prev_b = U[:, cg, :], bU[cg]
                            if seq_end and kind == "p":
                                sq_ = u * 4 + sidx
                                r0 = (((sq_ * L + l) * 2 + dr) * 8 + hd) * 128
                                P.dma("sp", lambda h, r0=r0, cg=cg: h.dma_start(out=o_ns[r0:r0 + 128, :], in_=U[:, cg, :]), ch_so[cg], reads=[bU[cg]])
                        nci = 1 - ci
                        P.op("pool", lambda h, nci=nci, cg=order[-1]: h.tensor_copy(out=carry[nci], in_=U[:, cg, :]), reads=[bU[cg]], writes=[bcar[nci]])
                        ci = nci
                        for t4 in range(4):
                            P.op("pe", lambda h, t4=t4: h.matmul(SC[:, 1, t4 * 128:(t4 + 1) * 128], lhsT=kt[:, t4 * 128:(t4 + 1) * 128], rhs=qt[:, t4 * 128:(t4 + 1) * 128], start=True, stop=True),
                                 reads=[B_["kt"], B_["qt"]], writes=[bSC[1]])
                        P.op("dve", lambda h, mask=mask: h.tensor_tensor(out=at_sb, in0=SC[:, 1, :].rearrange("p (a b) -> p a b", b=128), in1=mask[:].unsqueeze(1).to_broadcast([128, 4, 128]), op=ALU.mult),
                             reads=[bSC[1], b_const], writes=[B_["at"]])
                        for t4 in range(4):
                            P.op("pe", lambda h, t4=t4, sg=sg: h.matmul(SC[:, 2, t4 * 128:(t4 + 1) * 128], lhsT=v_tok[:, sg * 4 + t4, :], rhs=at_sb[:, t4, :], start=True, stop=False),
                                 reads=[B_["v"], B_["at"]], writes=[bSC[2]])
                            for c in range(4):
                                cg = t4 * 4 + c
                                P.op("pe", lambda h, t4=t4, c=c, cg=cg: h.matmul(SC[:, 2, t4 * 128 + c * 32:t4 * 128 + (c + 1) * 32], lhsT=Sp[:, cg, :], rhs=qt[:, t4 * 128 + c * 32:t4 * 128 + (c + 1) * 32], start=False, stop=(c == 3)),
                                     reads=[bSp[cg], B_["qt"]], writes=[bSC[2]])
                        if dr == 0:
                            P.op("act", lambda h, cs_=cs_: h.activation(out=o_acc[:, cs_], in_=SC[:, 2, :], func=AF.Copy), reads=[bSC[2]], writes=[B_["oacc"]])
                        else:
                            P.op("dve", lambda h, cs_=cs_: h.tensor_tensor(out=o_acc[:, cs_], in0=SC[:, 2, :], in1=o_acc[:, cs_], op=ALU.add), reads=[bSC[2], B_["oacc"]], writes=[B_["oacc"]])
                for n in range(2):
                    P.op("act", lambda h, n=n: h.activation(out=qt, in_=o_acc[:, hs(n)], func=AF.Square), reads=[B_["oacc"]], writes=[B_["qt"]])
                    P.op("pe", lambda h: h.matmul(PV[:], lhsT=ones_b[:], rhs=qt, start=True, stop=True), reads=[B_["qt"], b_idb], writes=[bPV])
                    P.op("act", lambda h: h.activation(out=tE, in_=PV[:], func=AF.Sqrt, scale=1.0 / 128.0, bias=eps_ap), reads=[bPV, b_eps], writes=[B_["tE"]])
                    P.op("dve", lambda h: h.reciprocal(out=tE, in_=tE), reads=[B_["tE"]], writes=[B_["tE"]])
                    P.op("dve", lambda h, n=n: h.tensor_tensor(out=Bc, in0=o_acc[:, hs(n)], in1=tE, op=ALU.mult), reads=[B_["oacc"], B_["tE"]], writes=[B_["Bc"]])
                    P.op("act", lambda h, n=n: h.activation(out=Fb, in_=xg[:, hs(n)], func=AF.Sigmoid), reads=[B_["xg"]], writes=[B_["F"]])
                    P.op("dve", lambda h, n=n: h.tensor_tensor(out=Fb, in0=Fb, in1=xg[:, hs(n)], op=ALU.mult), reads=[B_["xg"], B_["F"]], writes=[B_["F"]])
                    P.op("dve", lambda h, n=n, hd=hd: h.scalar_tensor_tensor(out=mixT[:, hd, hs(n)], in0=Bc, scalar=gnwT[:, l:l + 1], in1=Fb, op0=ALU.mult, op1=ALU.mult),
                         reads=[B_["Bc"], B_["F"], b_vecs], writes=[bm[hd][n]])

        def out_proj(l, v):
            for m in range(KC):
                slot, bsl = wload(w_out_r[(l * KC + m) * 128:(l * KC + m + 1) * 128, :], 2048, bw_out[l])
                for n in range(2):
                    bank, bb = next_bank(dense_banks4)
                    bap = bank[:] if (bank is D0 or bank is D1) else bank
                    for k in range(16):
                        P.op("pe", lambda h, k=k, n=n, bap=bap, slot=slot: h.matmul(bap, lhsT=slot[:, k * 128:(k + 1) * 128], rhs=mixT[:, k, hs(n)], start=(k == 0), stop=(k == 15)),
                             reads=[bsl, bm[k][n]], writes=[bb])
                    P.op("dve", lambda h, m=m, n=n, bap=bap: h.scalar_tensor_tensor(out=xT[:, m, hs(n)], in0=bap, scalar=mods[:, l, 2, m, v:v + 1], in1=xT[:, m, hs(n)], op0=ALU.mult, op1=ALU.add),
                         reads=[bb, bx[m][n], b_mods], writes=[bx[m][n]])

        ch_wd = [P.chan(), P.chan()]

        def ffn(l, v):
            nb_ = 6
            (b_a, b_sg0, b_sg1, b_wd0, b_wd1, b_dummy) = new_phase(nb_)
            for k in range(16):
                for n in range(2):
                    if bm[k][n].w is not None:
                        b_a.rs.append(bm[k][n].w)
                    b_a.rs.extend(bm[k][n].rs)
            o = 0
            a_ext = None
            if FC > 32:
                a_ext, o = carve(o, [FC - 32, 512], BF16)
            sg_t = []
            for i in range(2):
                a_, o = carve(o, [512], F32)
                sg_t.append(a_)
            wds = []
            for i in range(2):
                a_, o = carve(o, [FC * 128], BF16)
                wds.append(a_)
            b_sg = [b_sg0, b_sg1]
            b_wd = [b_wd0, b_wd1]
            mflat = mixT[:].rearrange("p a b -> p (a b)")

            def a_ap(j):
                if j < 32:
                    return mflat[:, j * 512:(j + 1) * 512]
                return a_ext[:, j - 32, :]
            wdi = 0
            for n in range(2):
                for j in range(FC):
                    sl_g, bg_ = wload(wg_r[(l * FC + j) * 128:(l * FC + j + 1) * 128, :], KC * 128, bw_g[l])
                    sl_u, bu_ = wload(wu_r[(l * FC + j) * 128:(l * FC + j + 1) * 128, :], KC * 128, bw_u[l])
                    bank_g, bbg = next_bank(dense_banks4)
                    bank_u, bbu = next_bank(dense_banks4)
                    bg_ap = bank_g[:] if (bank_g is D0 or bank_g is D1) else bank_g
                    bu_ap = bank_u[:] if (bank_u is D0 or bank_u is D1) else bank_u
                    for k in range(KC):
                        P.op("pe", lambda h, k=k, n=n, bg_ap=bg_ap, sl_g=sl_g: h.matmul(bg_ap, lhsT=sl_g[:, k * 128:(k + 1) * 128], rhs=hT[:, k, hs(n)], start=(k == 0), stop=(k == KC - 1)),
                             reads=[bg_, bh[k][n]], writes=[bbg])
                    for k in range(KC):
                        P.op("pe", lambda h, k=k, n=n, bu_ap=bu_ap, sl_u=sl_u: h.matmul(bu_ap, lhsT=sl_u[:, k * 128:(k + 1) * 128], rhs=hT[:, k, hs(n)], start=(k == 0), stop=(k == KC - 1)),
                             reads=[bu_, bh[k][n]], writes=[bbu])
                    s_ = j % 2
                    P.op("act", lambda h, s_=s_, bg_ap=bg_ap: h.activation(out=sg_t[s_], in_=bg_ap, func=AF.Silu), reads=[bbg], writes=[b_sg[s_]])
                    P.op("dve", lambda h, s_=s_, bu_ap=bu_ap, j=j: h.tensor_tensor(out=a_ap(j), in0=bu_ap, in1=sg_t[s_], op=ALU.mult), reads=[bbu, b_sg[s_]], writes=[b_a])
                for m in range(KC):
                    s_ = wdi % 2
                    wdi += 1
                    P.dma("sp", lambda h, s_=s_, m=m: h.dma_start(out=wds[s_], in_=wd_r[(l * KC + m) * 128:(l * KC + m + 1) * 128, :]), ch_wd[s_], reads=[bw_d[l]], writes=[b_wd[s_]])
                    bank, bb = next_bank(dense_banks4)
                    bap = bank[:] if (bank is D0 or bank is D1) else bank
                    for j in range(FC):
                        P.op("pe", lambda h, j=j, s_=s_, bap=bap: h.matmul(bap, lhsT=wds[s_][:, j * 128:(j + 1) * 128], rhs=a_ap(j), start=(j == 0), stop=(j == FC - 1)),
                             reads=[b_wd[s_], b_a], writes=[bb])
                    P.op("dve", lambda h, m=m, n=n, bap=bap: h.scalar_tensor_tensor(out=xT[:, m, hs(n)], in0=bap, scalar=mods[:, l, 5, m, v:v + 1], in1=xT[:, m, hs(n)], op0=ALU.mult, op1=ALU.add),
                         reads=[bb, bx[m][n], b_mods], writes=[bx[m][n]])
            for k in range(16):
                for n in range(2):
                    bm[k][n].w = None
                    bm[k][n].rs = list(b_a.rs) + ([b_a.w] if b_a.w is not None else [])

        def final_store(dst_rows):
            (b_q0, b_q1, b_r, b_y0, b_y1, b_hy) = new_phase(6)
            for k in range(KC):
                for n in range(2):
                    if bh[k][n].w is not None:
                        b_hy.rs.append(bh[k][n].w)
                    b_hy.rs.extend(bh[k][n].rs)
            o = 0
            sq = []
            for i in range(2):
                a_, o = carve(o, [512], BF16)
                sq.append(a_)
            rbuf, o = carve(o, [512], F32)
            ystg = []
            for i in range(2):
                a_, o = carve(o, [D], F32)
                ystg.append(a_)
            b_ys = [b_y0, b_y1]
            yT = hT[:].rearrange("p a b -> p (a b)").bitcast(F32).rearrange("p (a b) -> p a b", b=512)
            si = 0
            for n in range(2):
                norm_stats(n, rbuf, b_r, sq, [b_q0, b_q1])
                for k in range(KC):
                    P.op("dve", lambda h, k=k, n=n: h.scalar_tensor_tensor(out=yT[:, k, :], in0=xT[:, k, hs(n)], scalar=fnwT[:, k:k + 1], in1=rbuf, op0=ALU.mult, op1=ALU.mult),
                         reads=[bx[k][n], b_r, b_vecs], writes=[b_hy])
                for t4 in range(4):
                    s_ = si % 2
                    si += 1
                    for k0 in range(0, KC, 4):
                        nk_ = min(4, KC - k0)
                        for kk_ in range(nk_):
                            P.op("pe", lambda h, k0=k0, kk_=kk_, t4=t4: h.transpose(out=SC[:, 0, kk_ * 128:(kk_ + 1) * 128], in_=yT[:, k0 + kk_, t4 * 128:(t4 + 1) * 128], identity=ident_f[:]),
                                 reads=[b_hy, b_const], writes=[bSC[0]])
                        P.op("act", lambda h, s_=s_, k0=k0, nk_=nk_: h.activation(out=ystg[s_][:, k0 * 128:(k0 + nk_) * 128], in_=SC[:, 0, 0:nk_ * 128], func=AF.Copy), reads=[bSC[0]], writes=[b_ys[s_]])
                    tt = n * 4 + t4
                    P.dma("sp", lambda h, s_=s_, tt=tt: h.dma_start(out=dst_rows[tt * 128:(tt + 1) * 128, :], in_=ystg[s_]), ch_out[s_], reads=[b_ys[s_]])
            for k in range(KC):
                for n in range(2):
                    bh[k][n].w = None
                    bh[k][n].rs = list(b_hy.rs) + ([b_hy.w] if b_hy.w is not None else [])

        units = [("p", u) for u in range(NP // 4)] + [("s", u) for u in range(NS)]
        for kind, u in units:
            if kind == "p":
                src = xp[u * T:(u + 1) * T, :]
                dst = yp[u * T:(u + 1) * T, :]
                v = 0
                nseq, slen = 4, SEQ
            else:
                src = xs[u * T:(u + 1) * T, :]
                dst = ys[u * T:(u + 1) * T, :]
                v = 1 + u
                nseq, slen = 1, DSEQ
            if cfg.stop < 3:
                break
            load_x(src)
            for l in range(L):
                norm_mod(l, v, A1, 0)
                if cfg.stop >= 4 and kind in getattr(cfg, "att_kinds", "ps"):
                    attention(l, kind, u)
                if cfg.stop >= 5:
                    conv_mixer(l, nseq, slen)
                if cfg.stop >= 6:
                    hgrn(l, kind, u)
                if cfg.stop >= 7:
                    out_proj(l, v)
                    norm_mod(l, v, A2, 3)
                    ffn(l, v)
            if cfg.stop >= 8:
                final_store(dst)

        fin = Buf()
        for ch in [ch_out[0], ch_out[1], ch_misc, ch_tab] + ch_so:
            if ch.cnt:
                fin.rs.append(("d", ch, ch.cnt))
        P.wait_all("sp", [fin] + phase_bufs)
        P.emit_all()
    return nc


N_ACTIVE = 2


def make_in_maps(cfg, n_act, x_prompt, x_sample, cache_na_k, cache_na_v, state_hgrn, c, c_ctx, w_ada, b_ada, norm_mix_w, w_in,
                 hgrn_lb_raw, hgrn_gnorm_w, conv_w, na_rpb, w_out, norm_ffn_w, w_ffn_gate, w_ffn_up, w_ffn_down, final_norm_w):
    f = lambda a: np.ascontiguousarray(np.asarray(a, dtype=np.float32))
    D, L, KC, NP, NS = cfg.D, cfg.L, cfg.KC, cfg.NP, cfg.NS
    consts = host_consts()
    shared = dict(
        w_ada=f(w_ada).reshape(L * D, 6 * D), b_ada=f(b_ada).reshape(L * 6 * KC, 128), nmw=f(norm_mix_w).reshape(L * KC, 128),
        w_in=f(w_in).reshape(L * D, 8192), lbr=f(hgrn_lb_raw).reshape(2 * L * 8, 128), gnw=f(hgrn_gnorm_w).reshape(L, 128),
        cw=f(conv_w).reshape(L * 12, 128), rpb=f(na_rpb).reshape(L * 60, 31), w_out=f(w_out).reshape(L * 2048, D),
        nfw=f(norm_ffn_w).reshape(L * KC, 128), wg=f(w_ffn_gate).reshape(L * D, cfg.FH), wu=f(w_ffn_up).reshape(L * D, cfg.FH),
        wd=f(w_ffn_down).reshape(L * cfg.FH, D), fnw=f(final_norm_w).reshape(KC, 128), **consts)
    maps = []
    for ci in range(n_act):
        m = dict(shared)
        m["xp"] = f(x_prompt[ci * NP:(ci + 1) * NP]).reshape(NP * SEQ, D)
        m["xs"] = f(x_sample[ci * NS:(ci + 1) * NS]).reshape(NS * DSEQ, D)
        m["ck"] = f(cache_na_k[ci * NS:(ci + 1) * NS]).reshape(NS * L * cfg.PAST, 512)
        m["cv"] = f(cache_na_v[ci * NS:(ci + 1) * NS]).reshape(NS * L * cfg.PAST, 512)
        m["st"] = f(state_hgrn[ci * NS:(ci + 1) * NS]).reshape(NS * L * 2 * 8 * 128, 128)
        m["cvec"] = np.concatenate([f(c_ctx).reshape(1, D), f(c[ci * NS:(ci + 1) * NS]).reshape(NS, D)], axis=0)
        maps.append(m)
    return maps


def gather_outputs(cfg, res):
    D, L, NP, NS = cfg.D, cfg.L, cfg.NP, cfg.NS
    yp = np.concatenate([r["yp"].reshape(NP, SEQ, D) for r in res], axis=0)
    ys = np.concatenate([r["ys"].reshape(NS, DSEQ, D) for r in res], axis=0)
    nk = np.concatenate([r["o_nk"].reshape(NP, L, SEQ, 4, 128) for r in res], axis=0)
    nv = np.concatenate([r["o_nv"].reshape(NP, L, SEQ, 4, 128) for r in res], axis=0)
    ns = np.concatenate([r["o_ns"].reshape(NP, L, 2, 8, 128, 128) for r in res], axis=0)
    return (yp.astype(np.float32), ys.astype(np.float32), nk.astype(np.float32), nv.astype(np.float32), ns.astype(np.float32))


def kernel(**inputs):
    xpr = np.asarray(inputs["x_prompt"])
    xsa = np.asarray(inputs["x_sample"])
    n_act = N_ACTIVE
    cfg = Cfg(D=xpr.shape[2], NP=xpr.shape[0] // n_act, NS=xsa.shape[0] // n_act, L=np.asarray(inputs["w_in"]).shape[0],
              PAST=np.asarray(inputs["cache_na_k"]).shape[2])
    nc = build(cfg)
    maps = make_in_maps(cfg, n_act, **inputs)
    res = run_bass_kernel_spmd(nc, maps, core_ids=list(range(n_act)))
    return gather_outputs(cfg, res.results)
```

```python
import numpy as np
from contextlib import ExitStack
import concourse.bass as bass
import concourse.mybir as mybir
from concourse.bass_utils import run_bass_kernel_spmd

F32 = mybir.dt.float32
BF16 = mybir.dt.bfloat16
ALU = mybir.AluOpType
AF = mybir.ActivationFunctionType
AX = mybir.AxisListType
ENGS = ("pe", "act", "dve", "pool", "sp")
EPS = 1e-6
NEG = -30000.0


class Buf:
    __slots__ = ("w", "rs", "excl")

    def __init__(self, excl=False):
        self.w = None
        self.rs = []
        self.excl = excl


class Chan:
    __slots__ = ("sem", "cnt")

    def __init__(self, sem):
        self.sem = sem
        self.cnt = 0


class Prog:
    def __init__(self, nc, es):
        self.nc = nc
        self.es = es
        self.ins = {e: [] for e in ENGS}
        self.esem = {e: es.enter_context(nc.semaphore("s_" + e)) for e in ENGS}
        self.nchan = 0

    def chan(self):
        s = self.es.enter_context(self.nc.semaphore("d%d" % self.nchan))
        self.nchan += 1
        return Chan(s)

    def _deps(self, eng, reads, writes, is_dma):
        deps = []
        for r in reads:
            if r.w is not None:
                deps.append((r.w, True))
            if r.excl:
                for e in r.rs:
                    deps.append((e, False))
        for w in writes:
            if w.w is not None:
                deps.append((w.w, True))
            for e in w.rs:
                deps.append((e, False))
        out = []
        for d, iswr in deps:
            if (not is_dma) and d[0] == "e" and d[1] == eng:
                if eng == "pe":
                    continue
                if not iswr:
                    continue
            out.append(d)
        return out

    def _commit(self, ev, reads, writes):
        for r in reads:
            r.rs.append(ev)
        for w in writes:
            w.w = ev
            w.rs = []

    def op(self, eng, fn, reads=(), writes=()):
        deps = self._deps(eng, reads, writes, False)
        ev = ("e", eng, len(self.ins[eng]))
        self.ins[eng].append([fn, deps, False, None])
        self._commit(ev, reads, writes)

    def dma(self, eng, fn, chan, reads=(), writes=(), inc=16):
        deps = self._deps(eng, reads, writes, True)
        chan.cnt += inc
        ev = ("d", chan, chan.cnt)
        self.ins[eng].append([fn, deps, False, chan])
        self._commit(ev, reads, writes)

    def wait_all(self, eng, bufs):
        deps = []
        for b in bufs:
            if b.w is not None:
                deps.append(b.w)
            deps.extend(b.rs)
        self.ins[eng].append([None, deps, False, None])

    def emit_all(self):
        EPOCH = 16000
        for e in ENGS:
            for it in self.ins[e]:
                for d in it[1]:
                    if d[0] == "e":
                        self.ins[d[1]][d[2]][2] = True
        cnt = {}
        esems = {}
        for e in ENGS:
            c = 0
            arr = []
            for it in self.ins[e]:
                if it[2]:
                    c += 1
                k = max(c - 1, 0)
                arr.append((k // EPOCH, k % EPOCH + 1 if c > 0 else 0))
            cnt[e] = arr
            nep = (max(c - 1, 0)) // EPOCH + 1
            esems[e] = [self.esem[e]] + [self.es.enter_context(self.nc.semaphore("s_%s_%d" % (e, i))) for i in range(1, nep)]

        def emit(eng, h):
            seen_e = {}
            seen_d = {}
            for it_i, it in enumerate(self.ins[eng]):
                need_e = {}
                need_d = {}
                for d in it[1]:
                    if d[0] == "e":
                        ep, c = cnt[d[1]][d[2]]
                        key = (d[1], ep)
                        if c > seen_e.get(key, 0) and c > need_e.get(key, 0):
                            need_e[key] = c
                    else:
                        k = id(d[1])
                        if d[2] > seen_d.get(k, 0) and d[2] > need_d.get(k, (None, 0))[1]:
                            need_d[k] = (d[1], d[2])
                for key, c in need_e.items():
                    h.wait_ge(esems[key[0]][key[1]], c)
                    seen_e[key] = c
                for k, (ch, c) in need_d.items():
                    h.wait_ge(ch.sem, c)
                    seen_d[k] = c
                if it[0] is None:
                    continue
                ins = it[0](h)
                if it[3] is not None:
                    ins.then_inc(it[3].sem, 16)
                elif it[2]:
                    ins.then_inc(esems[eng][cnt[eng][it_i][0]], 1)

        with self.nc.Block() as block:
            @block.tensor
            def _(h):
                emit("pe", h)

            @block.scalar
            def _(h):
                emit("act", h)

            @block.vector
            def _(h):
                emit("dve", h)

            @block.gpsimd
            def _(h):
                emit("pool", h)

            @block.sync
            def _(h):
                emit("sp", h)


class Cfg:
    def __init__(self, D=2048, NP=16, NS=8, L=2, PAST=512, debug=False, stop=99):
        self.stop = stop
        self.D = D
        self.KC = D // 128
        self.FH = ((8 * D + 3 * 256 - 1) // (3 * 256)) * 256
        self.FC = self.FH // 128
        self.NP = NP
        self.NS = NS
        self.L = L
        self.PAST = PAST
        self.PT = PAST // 128
        self.NV = 1 + NS
        self.debug = debug


SEQ = 256
DSEQ = 1024
T = 1024
NTT = 8
GRID_W = 64
ROWS = 16
NA_KH = 8
NA_KW = 16


def host_consts():
    ident = np.eye(128, dtype=np.float32)
    j = np.arange(128)[:, None]
    i = np.arange(128)[None, :]
    same = (j // 32) == (i // 32)
    maskf = (same & (j <= i)).astype(np.float32)
    maskb = (same & (j >= i)).astype(np.float32)
    cstart = np.ones((128, 512), np.float32)
    cstart[:, ::32] = 0.0
    col = np.arange(GRID_W)
    cs = np.clip(col - NA_KW // 2, 0, GRID_W - NA_KW)
    valid = (col[None, :] >= cs[:, None]) & (col[None, :] < cs[:, None] + NA_KW)
    cm = np.where(valid, 0.0, NEG).astype(np.float32)
    colmask = np.concatenate([cm, cm], axis=0)
    rowsel = (np.arange(128)[:, None] // 32 == np.arange(4)[None, :]).astype(np.float32)
    return dict(ident=ident, maskf=maskf, maskb=maskb, cstart=cstart, colmask=colmask, rowsel=rowsel)


def build(cfg):
    D, KC, FH, FC, NP, NS, L, PAST, PT, NV = cfg.D, cfg.KC, cfg.FH, cfg.FC, cfg.NP, cfg.NS, cfg.L, cfg.PAST, cfg.PT, cfg.NV
    nc = bass.Bass("TRN2", target_bir_lowering=False)

    def din(name, shape, dt=F32):
        return nc.dram_tensor(name, list(shape), dt, kind="ExternalInput").ap()

    def dout(name, shape, dt=F32):
        return nc.dram_tensor(name, list(shape), dt, kind="ExternalOutput").ap()

    xp = din("xp", [max(NP, 1) * SEQ, D])
    xs = din("xs", [max(NS, 1) * DSEQ, D])
    ck = din("ck", [max(NS, 1) * L * PAST, 512])
    cv = din("cv", [max(NS, 1) * L * PAST, 512])
    st = din("st", [max(NS, 1) * L * 2 * 8 * 128, 128])
    cvec = din("cvec", [NV, D])
    w_ada = din("w_ada", [L * D, 6 * D])
    b_ada = din("b_ada", [L * 6 * KC, 128])
    nmw = din("nmw", [L * KC, 128])
    w_in = din("w_in", [L * D, 8192])
    lbr = din("lbr", [2 * L * 8, 128])
    gnw = din("gnw", [L, 128])
    cw = din("cw", [L * 12, 128])
    rpb = din("rpb", [L * 60, 31])
    w_out = din("w_out", [L * 2048, D])
    nfw = din("nfw", [L * KC, 128])
    wg = din("wg", [L * D, FH])
    wu = din("wu", [L * D, FH])
    wd = din("wd", [L * FH, D])
    fnw = din("fnw", [KC, 128])
    c_ident = din("ident", [128, 128])
    c_maskf = din("maskf", [128, 128])
    c_maskb = din("maskb", [128, 128])
    c_cstart = din("cstart", [128, 512])
    c_colmask = din("colmask", [128, 64])
    c_rowsel = din("rowsel", [128, 4])

    yp = dout("yp", [max(NP, 1) * SEQ, D])
    ys = dout("ys", [max(NS, 1) * DSEQ, D])
    o_nk = dout("o_nk", [max(NP, 1) * L * SEQ, 512])
    o_nv = dout("o_nv", [max(NP, 1) * L * SEQ, 512])
    o_ns = dout("o_ns", [max(NP, 1) * L * 2 * 8 * 128, 128])

    w_in_r = nc.dram_tensor("w_in_r", [L * 64 * 128, KC * 128], BF16).ap()
    w_out_r = nc.dram_tensor("w_out_r", [L * KC * 128, 16 * 128], BF16).ap()
    wg_r = nc.dram_tensor("wg_r", [L * FC * 128, KC * 128], BF16).ap()
    wu_r = nc.dram_tensor("wu_r", [L * FC * 128, KC * 128], BF16).ap()
    wd_r = nc.dram_tensor("wd_r", [L * KC * 128, FC * 128], BF16).ap()
    rpbp = nc.dram_tensor("rpbp", [L * 60, 127], F32).ap()
    ctab_d = nc.dram_tensor("ctab_d", [L * 128, 60 * 64], F32).ap()

    es = ExitStack()
    with es:
        P = Prog(nc, es)

        def sb(name, shape, dt):
            return es.enter_context(nc.sbuf_tensor("sb_" + name, list(shape), dt))

        def ps(name, shape, dt):
            return es.enter_context(nc.psum_tensor("ps_" + name, list(shape), dt))

        xT = sb("xT", [128, KC, T], F32)
        hT = sb("hT", [128, KC, T], BF16)
        mixT = sb("mixT", [128, 16, T], BF16)
        NSLOT = 4
        wring = [sb("wr%d" % i, [128, 2048], BF16) for i in range(NSLOT)]
        SCRB = 46 * 1024
        scr = sb("scr", [128, SCRB // 4], F32)
        ident_f = sb("ident_f", [128, 128], F32)
        ident_b = sb("ident_b", [128, 128], BF16)
        ones_b = sb("ones_b", [128, 128], BF16)
        maskf = sb("maskf", [128, 128], F32)
        maskb = sb("maskb", [128, 128], F32)
        cstart = sb("cstart", [128, 512], F32)
        colmask = sb("colmask", [128, 64], F32)
        rowsel = sb("rowsel", [128, 4], F32)
        mods = sb("mods", [128, L, 6, KC, NV], F32)
        A1 = sb("A1", [128, L, KC, NV], F32)
        A2 = sb("A2", [128, L, KC, NV], F32)
        badaT = sb("badaT", [128, L, 6 * KC], F32)
        nmwT = sb("nmwT", [128, L * KC], F32)
        nfwT = sb("nfwT", [128, L * KC], F32)
        fnwT = sb("fnwT", [128, KC], F32)
        gnwT = sb("gnwT", [128, L], F32)
        cwT = sb("cwT", [128, L * 12], F32)
        lbT = sb("lbT", [128, 2 * L * 8], F32)
        LB = sb("LB", [128, 2 * L * 8], F32)
        OM = sb("OM", [128, 2 * L * 8], F32)
        NOM = sb("NOM", [128, 2 * L * 8], F32)
        sT = sb("sT", [128, KC, NV], BF16)
        small = sb("small", [128, 64], F32)

        D0 = ps("D0", [128, 512], F32)
        D1 = ps("D1", [128, 512], F32)
        SC = ps("SC", [128, 3, 512], F32)
        TB = ps("TB", [128, 2, 1024], BF16)
        PV = ps("PV", [128, 512], F32)
        bD0, bD1, bPV = Buf(True), Buf(True), Buf(True)
        bSC = [Buf(True), Buf(True), Buf(True)]
        bTB = [Buf(True), Buf(True)]

        def carve(off, shape, dt):
            n = 1
            for s_ in shape:
                n *= s_
            nb = n * (4 if dt == F32 else 2)
            assert off % 4 == 0 and off + nb <= SCRB, (off, nb, SCRB)
            ap = scr[:, off // 4:(off + nb) // 4]
            if dt != F32:
                ap = ap.bitcast(dt)
            if len(shape) == 2:
                ap = ap.rearrange("p (a b) -> p a b", b=shape[1])
            elif len(shape) == 3:
                ap = ap.rearrange("p (a b c) -> p a b c", b=shape[1], c=shape[2])
            return ap, off + nb

        phase_bufs = []

        def new_phase(n):
            prev = []
            for b in phase_bufs:
                if b.w is not None:
                    prev.append(b.w)
                prev.extend(b.rs)
            del phase_bufs[:]
            out = []
            for _ in range(n):
                b = Buf()
                b.rs = list(prev)
                phase_bufs.append(b)
                out.append(b)
            return out

        ch_misc = P.chan()
        ch_stg = P.chan()
        ch_cv = P.chan()
        b_const = Buf()

        def load_const(dst, src):
            P.dma("sp", lambda h: h.dma_start(out=dst, in_=src), ch_misc, writes=[b_const])

        load_const(ident_f[:], c_ident)
        load_const(maskf[:], c_maskf)
        load_const(maskb[:], c_maskb)
        load_const(cstart[:], c_cstart)
        load_const(colmask[:], c_colmask)
        load_const(rowsel[:], c_rowsel)
        b_idb = Buf()
        P.op("dve", lambda h: h.tensor_copy(out=ident_b[:], in_=ident_f[:]), reads=[b_const], writes=[b_idb])
        P.op("pool", lambda h: h.memset(ones_b[:], 1.0), writes=[b_idb])

        (bs_stg, bs_ada0, bs_ada1, bs_cv, bs_z) = new_phase(5)
        o = 0
        stg_v, o = carve(o, [128], F32)
        cv_in, o = carve(o, [KC * 128], F32)
        cv_s = cv_in
        ztile, o = carve(o, [128], F32)
        ada_slot = []
        for i in range(2):
            a_, o = carve(o, [KC, 256], BF16)
            ada_slot.append(a_)
        ctab_s, o = carve(o, [60, 64], F32)

        b_vecs = Buf()

        def load_T(dst, src_rows, n):
            P.dma("sp", lambda h: h.dma_start(out=stg_v[0:n, :], in_=src_rows), ch_stg, writes=[bs_stg])
            P.op("pe", lambda h: h.transpose(out=SC[:, 0, 0:n], in_=stg_v[0:n, :], identity=ident_f[0:n, 0:n]),
                 reads=[bs_stg, b_const], writes=[bSC[0]])
            P.op("dve", lambda h: h.tensor_copy(out=dst, in_=SC[:, 0, 0:n]), reads=[bSC[0]], writes=[b_vecs])

        for l in range(L):
            load_T(badaT[:, l, :], b_ada[l * 6 * KC:(l + 1) * 6 * KC, :], 6 * KC)
        load_T(nmwT[:], nmw, L * KC)
        load_T(nfwT[:], nfw, L * KC)
        load_T(fnwT[:], fnw, KC)
        load_T(gnwT[:], gnw, L)
        load_T(cwT[:], cw, L * 12)
        load_T(lbT[:], lbr, 2 * L * 8)

        b_lb = Buf()
        P.op("act", lambda h: h.activation(out=lbT[:], in_=lbT[:], func=AF.Exp), reads=[b_vecs], writes=[b_vecs])
        for d_ in range(2):
            base = d_ * L * 8
            tot = small[:, 0:8]
            P.op("dve", lambda h, base=base: h.tensor_copy(out=small[:, 0:8], in_=lbT[:, base:base + 8]), reads=[b_vecs], writes=[b_lb])
            for l in range(1, L):
                P.op("dve", lambda h, base=base, l=l: h.tensor_tensor(out=small[:, 0:8], in0=small[:, 0:8], in1=lbT[:, base + l * 8:base + l * 8 + 8], op=ALU.add),
                     reads=[b_vecs, b_lb], writes=[b_lb])
            P.op("dve", lambda h: h.reciprocal(out=small[:, 8:16], in_=small[:, 0:8]), reads=[b_lb], writes=[b_lb])
            P.op("pool", lambda h, base=base: h.memset(LB[:, base:base + 8], 0.0), writes=[b_lb])
            for l in range(1, L):
                P.op("dve", lambda h, base=base, l=l: h.tensor_tensor(out=small[:, 16:24], in0=lbT[:, base + l * 8:base + l * 8 + 8], in1=small[:, 8:16], op=ALU.mult),
                     reads=[b_vecs, b_lb], writes=[b_lb])
                P.op("dve", lambda h, base=base, l=l: h.tensor_tensor(out=LB[:, base + l * 8:base + l * 8 + 8], in0=LB[:, base + (l - 1) * 8:base + (l - 1) * 8 + 8], in1=small[:, 16:24], op=ALU.add),
                     reads=[b_lb], writes=[b_lb])
        P.op("dve", lambda h: h.tensor_scalar(out=OM[:], in0=LB[:], scalar1=-1.0, scalar2=1.0, op0=ALU.mult, op1=ALU.add), reads=[b_lb], writes=[b_lb])
        P.op("dve", lambda h: h.tensor_scalar(out=NOM[:], in0=OM[:], scalar1=-1.0, scalar2=None, op0=ALU.mult), reads=[b_lb], writes=[b_lb])

        P.dma("sp", lambda h: h.dma_start(out=cv_in[0:NV, :], in_=cvec), ch_cv, writes=[bs_cv])
        P.op("act", lambda h: h.activation(out=cv_s[0:NV, :], in_=cv_in[0:NV, :], func=AF.Silu), reads=[bs_cv], writes=[bs_cv])
        b_sT = Buf()
        for k in range(KC):
            P.op("pe", lambda h, k=k: h.transpose(out=SC[:, 0, 0:NV], in_=cv_s[0:NV, k * 128:(k + 1) * 128], identity=ident_f[0:NV, 0:NV]),
                 reads=[bs_cv, b_const], writes=[bSC[0]])
            P.op("dve", lambda h, k=k: h.tensor_copy(out=sT[:, k, :], in_=SC[:, 0, 0:NV]), reads=[bSC[0]], writes=[b_sT])

        ch_ada = [P.chan(), P.chan()]
        b_mods = Buf()
        nblk = (6 * D) // 256
        bi = 0
        for l in range(L):
            wv = w_ada[l * D:(l + 1) * D, :].rearrange("(k p) n -> p k n", p=128)
            for blk in range(nblk):
                s_ = bi % 2
                bi += 1
                bsl = bs_ada0 if s_ == 0 else bs_ada1
                P.dma("pool", lambda h, s_=s_, blk=blk, wv=wv: h.dma_start(out=ada_slot[s_], in_=wv[:, :, blk * 256:(blk + 1) * 256]), ch_ada[s_], writes=[bsl])
                for mm in range(2):
                    mg = blk * 2 + mm
                    for k in range(KC):
                        P.op("pe", lambda h, s_=s_, mm=mm, k=k: h.matmul(PV[:, mm * NV:(mm + 1) * NV], lhsT=ada_slot[s_][:, k, mm * 128:(mm + 1) * 128], rhs=sT[:, k, :], start=(k == 0), stop=(k == KC - 1)),
                             reads=[bsl, b_sT], writes=[bPV])
                    j6, kk_ = mg // KC, mg % KC
                    P.op("dve", lambda h, l=l, mm=mm, mg=mg, j6=j6, kk_=kk_: h.tensor_scalar(out=mods[:, l, j6, kk_, :], in0=PV[:, mm * NV:(mm + 1) * NV], scalar1=badaT[:, l, mg:mg + 1], scalar2=None, op0=ALU.add),
                         reads=[bPV, b_vecs], writes=[b_mods])
        for l in range(L):
            for k in range(KC):
                P.op("dve", lambda h, l=l, k=k: h.tensor_scalar(out=A1[:, l, k, :], in0=mods[:, l, 1, k, :], scalar1=1.0, scalar2=nmwT[:, l * KC + k:l * KC + k + 1], op0=ALU.add, op1=ALU.mult),
                     reads=[b_mods, b_vecs], writes=[b_mods])
                P.op("dve", lambda h, l=l, k=k: h.tensor_scalar(out=A2[:, l, k, :], in0=mods[:, l, 4, k, :], scalar1=1.0, scalar2=nfwT[:, l * KC + k:l * KC + k + 1], op0=ALU.add, op1=ALU.mult),
                     reads=[b_mods, b_vecs], writes=[b_mods])

        ch_tab = P.chan()
        b_rpbp, b_ctabd = Buf(), Buf()
        if NS > 0:
            P.op("pool", lambda h: h.memset(ztile, 0.0), writes=[bs_z])
            P.dma("sp", lambda h: h.dma_start(out=rpbp, in_=ztile[0:L * 60, 0:127]), ch_tab, reads=[bs_z], writes=[b_rpbp])
            P.dma("sp", lambda h: h.dma_start(out=rpbp[:, 48:79], in_=rpb), ch_tab, writes=[b_rpbp])
            ch_tab2 = P.chan()
            b_ct = bs_ada0
            b_cts = Buf()
            phase_bufs.append(b_cts)
            for l in range(L):
                src = rpbp[l * 60:(l + 1) * 60, :]
                first = True
                for w_ in range(64):
                    for ro in range(2):
                        pp_ = ro * 64 + w_
                        if first:
                            P.dma("sp", lambda h, pp_=pp_, w_=w_, src=src: h.dma_start(out=ctab_s[pp_:pp_ + 1, :, :], in_=src[:, 63 - w_:127 - w_].unsqueeze(0)),
                                  ch_tab2, reads=[b_rpbp], writes=[b_cts])
                            first = False
                        else:
                            ch_tab2.cnt += 16
                            ev = ("d", ch_tab2, ch_tab2.cnt)
                            P.ins["sp"].append([lambda h, pp_=pp_, w_=w_, src=src: h.dma_start(out=ctab_s[pp_:pp_ + 1, :, :], in_=src[:, 63 - w_:127 - w_].unsqueeze(0)), [], False, ch_tab2])
                            b_cts.w = ev
                P.op("dve", lambda h: h.tensor_tensor(out=ctab_s, in0=ctab_s, in1=colmask[:].unsqueeze(1).to_broadcast([128, 60, 64]), op=ALU.add),
                     reads=[b_cts, b_const], writes=[b_cts])
                P.dma("sp", lambda h, l=l: h.dma_start(out=ctab_d[l * 128:(l + 1) * 128, :], in_=ctab_s.rearrange("p a b -> p (a b)")), ch_tab, reads=[b_cts], writes=[b_ctabd])

        def precast(dst, src, rows_in, cols_out, l, nchunk_k):
            ch = P.chan()
            b = Buf()
            sv = src[l * rows_in:(l + 1) * rows_in, :].rearrange("(k p) (m c) -> m p k c", p=128, c=128)
            nm = cols_out // 128
            ev = None
            for m in range(nm):
                r0 = (l * nm + m) * 128
                for k0 in range(0, nchunk_k, 16):
                    k1 = min(nchunk_k, k0 + 16)
                    ch.cnt += 16
                    ev = ("d", ch, ch.cnt)
                    P.ins["pool"].append([lambda h, m=m, r0=r0, sv=sv, k0=k0, k1=k1: h.dma_start(out=dst[r0:r0 + 128, :].rearrange("p (k c) -> p k c", c=128)[:, k0:k1, :], in_=sv[m][:, k0:k1, :]), [], False, ch])
            b.w = ev
            return b

        bw_in, bw_out, bw_g, bw_u, bw_d = [], [], [], [], []
        for l in range(L if cfg.stop >= 2 else 0):
            bw_in.append(precast(w_in_r, w_in, D, 8192, l, KC))
            bw_out.append(precast(w_out_r, w_out, 2048, D, l, 16))
            bw_g.append(precast(wg_r, wg, D, FH, l, KC))
            bw_u.append(precast(wu_r, wu, D, FH, l, KC))
            bw_d.append(precast(wd_r, wd, FH, D, l, FC))

        ring_ch = [P.chan() for _ in range(NSLOT)]
        ring_b = [Buf() for _ in range(NSLOT)]
        ring_i = [0]

        def wload(src_tile, width, bsrc):
            s_ = ring_i[0] % NSLOT
            ring_i[0] += 1
            P.dma("sp", lambda h, s_=s_: h.dma_start(out=wring[s_][:, 0:width], in_=src_tile), ring_ch[s_], reads=[bsrc], writes=[ring_b[s_]])
            return wring[s_], ring_b[s_]

        bx = [[Buf() for _ in range(2)] for _ in range(KC)]
        bh = [[Buf() for _ in range(2)] for _ in range(KC)]
        bm = [[Buf() for _ in range(2)] for _ in range(16)]

        dense_banks = [(D0, bD0), (D1, bD1)]
        dense_banks4 = dense_banks + [(SC[:, 0, :], bSC[0]), (SC[:, 1, :], bSC[1])]
        dbi = [0]

        def next_bank(banks):
            b = banks[dbi[0] % len(banks)]
            dbi[0] += 1
            return b

        def hs(n):
            return slice(n * 512, (n + 1) * 512)

        ch_xin = [P.chan(), P.chan()]
        ch_out = [P.chan(), P.chan()]

        def load_x(src_rows):
            (b_s0, b_s1) = new_phase(2)
            o = 0
            stg = []
            for i in range(2):
                a_, o = carve(o, [D], F32)
                stg.append(a_)
            bst = [b_s0, b_s1]
            for tt in range(NTT):
                s_ = tt % 2
                P.dma("sp", lambda h, s_=s_, tt=tt: h.dma_start(out=stg[s_], in_=src_rows[tt * 128:(tt + 1) * 128, :]), ch_xin[s_], writes=[bst[s_]])
                for k0 in range(0, KC, 4):
                    nk_ = min(4, KC - k0)
                    for kk_ in range(nk_):
                        P.op("pe", lambda h, s_=s_, k0=k0, kk_=kk_: h.transpose(out=SC[:, 0, kk_ * 128:(kk_ + 1) * 128], in_=stg[s_][:, (k0 + kk_) * 128:(k0 + kk_ + 1) * 128], identity=ident_f[:]),
                             reads=[bst[s_], b_const], writes=[bSC[0]])
                    P.op("act", lambda h, k0=k0, nk_=nk_, tt=tt: h.activation(out=xT[:, k0:k0 + nk_, tt * 128:(tt + 1) * 128], in_=SC[:, 0, 0:nk_ * 128].rearrange("p (a b) -> p a b", b=128), func=AF.Copy),
                         reads=[bSC[0]], writes=[bx[k][tt // 4] for k in range(k0, k0 + nk_)])

        def norm_stats(n, rbuf_ap, b_r, sq, b_sq):
            for k in range(KC):
                s_ = k % 2
                P.op("act", lambda h, k=k, s_=s_: h.activation(out=sq[s_], in_=xT[:, k, hs(n)], func=AF.Square), reads=[bx[k][n]], writes=[b_sq[s_]])
                P.op("pe", lambda h, k=k, s_=s_: h.matmul(PV[:], lhsT=ones_b[:], rhs=sq[s_], start=(k == 0), stop=(k == KC - 1)), reads=[b_sq[s_], b_idb], writes=[bPV])
            P.op("act", lambda h: h.activation(out=rbuf_ap, in_=PV[:], func=AF.Sqrt, scale=1.0 / D, bias=eps_ap), reads=[bPV, b_eps], writes=[b_r])
            P.op("dve", lambda h: h.reciprocal(out=rbuf_ap, in_=rbuf_ap), reads=[b_r], writes=[b_r])

        eps_ap = small[:, 32:33]
        b_eps = Buf()
        P.op("pool", lambda h: h.memset(small[:, 32:33], EPS), writes=[b_eps])

        def norm_mod(l, v, Aap, shj):
            (b_q0, b_q1, b_r, b_t0, b_t1) = new_phase(5)
            o = 0
            sq = []
            for i in range(2):
                a_, o = carve(o, [512], BF16)
                sq.append(a_)
            rbuf, o = carve(o, [512], F32)
            tt_ = []
            for i in range(2):
                a_, o = carve(o, [512], F32)
                tt_.append(a_)
            b_t = [b_t0, b_t1]
            for n in range(2):
                norm_stats(n, rbuf, b_r, sq, [b_q0, b_q1])
                for k in range(KC):
                    s_ = k % 2
                    P.op("dve", lambda h, k=k, s_=s_, n=n: h.scalar_tensor_tensor(out=tt_[s_], in0=xT[:, k, hs(n)], scalar=Aap[:, l, k, v:v + 1], in1=rbuf, op0=ALU.mult, op1=ALU.mult),
                         reads=[bx[k][n], b_r, b_mods], writes=[b_t[s_]])
                    P.op("act", lambda h, k=k, s_=s_, n=n: h.activation(out=hT[:, k, hs(n)], in_=tt_[s_], func=AF.Identity, bias=mods[:, l, shj, k, v:v + 1], scale=1.0),
                         reads=[b_t[s_], b_mods], writes=[bh[k][n]])

        def inproj_fm(l, m, evac, halves=(0, 1)):
            slot, bsl = wload(w_in_r[(l * 64 + m) * 128:(l * 64 + m + 1) * 128, :], KC * 128, bw_in[l])
            for n in halves:
                bank, bb = next_bank(dense_banks)
                for k in range(KC):
                    P.op("pe", lambda h, k=k, n=n, bank=bank, slot=slot: h.matmul(bank[:] if bank is D0 or bank is D1 else bank, lhsT=slot[:, k * 128:(k + 1) * 128], rhs=hT[:, k, hs(n)], start=(k == 0), stop=(k == KC - 1)),
                         reads=[bsl, bh[k][n]], writes=[bb])
                evac(n, bank[:] if bank is D0 or bank is D1 else bank, bb)

        def inproj_tm(l, m, evac):
            slot, bsl = wload(w_in_r[(l * 64 + m) * 128:(l * 64 + m + 1) * 128, :], KC * 128, bw_in[l])
            for n in range(2):
                bank, bb = next_bank(dense_banks)
                bap = bank[:] if (bank is D0 or bank is D1) else bank
                for t4 in range(4):
                    tt = n * 4 + t4
                    for k in range(KC):
                        P.op("pe", lambda h, k=k, tt=tt, t4=t4, bap=bap, slot=slot: h.matmul(bap[:, t4 * 128:(t4 + 1) * 128], lhsT=hT[:, k, tt * 128:(tt + 1) * 128], rhs=slot[:, k * 128:(k + 1) * 128], start=(k == 0), stop=(k == KC - 1)),
                             reads=[bsl, bh[k][n]], writes=[bb])
                evac(n, bap.rearrange("p (a b) -> p a b", b=128), bb)

        ch_ctx = P.chan()
        ch_ctx2 = P.chan()
        ch_tabl = P.chan()

        def attention(l, kind, u):
            nb_ = 14
            bb_ = new_phase(nb_)
            (b_q, b_k, b_v, b_ckb, b_ckT, b_cvb, b_sl, b_pf, b_pc, b_pt, b_st, b_og0, b_og1, b_tab) = bb_
            o = 0
            qTh, o = carve(o, [T], BF16)
            kTh, o = carve(o, [T], BF16)
            vh, o = carve(o, [NTT, 128], BF16)
            ckb, o = carve(o, [PT, 128], BF16)
            ckT, o = carve(o, [PAST], BF16)
            cvb, o = carve(o, [PT, 128], BF16)
            sloc, o = carve(o, [512], F32)
            pfull, o = carve(o, [640], BF16)
            pctx, o = carve(o, [512], BF16)
            ptr, o = carve(o, [9, 128], BF16)
            stat, o = carve(o, [16], F32)
            ostg = []
            for i in range(2):
                a_, o = carve(o, [4, 128], F32)
                ostg.append(a_)
            b_og = [b_og0, b_og1]
            if kind == "s":
                tab, o = carve(o, [60, 64], F32)
                P.dma("sp", lambda h: h.dma_start(out=tab, in_=ctab_d[l * 128:(l + 1) * 128, :].rearrange("p (a b) -> p a b", b=64)), ch_tabl, reads=[b_ctabd], writes=[b_tab])
            ogi = [0]
            for hd in range(4):
                def ev_q(n, pa, pb):
                    P.op("act", lambda h, n=n, pa=pa: h.activation(out=qTh[:, hs(n)], in_=pa, func=AF.Copy, scale=128.0 ** -0.5), reads=[pb], writes=[b_q])
                inproj_fm(l, 52 + hd, ev_q)

                def ev_k(n, pa, pb):
                    P.op("dve", lambda h, n=n, pa=pa: h.tensor_copy(out=kTh[:, hs(n)], in_=pa), reads=[pb], writes=[b_k])
                if getattr(cfg, "att_sub", 9) >= -1:
                    inproj_fm(l, 56 + hd, ev_k)

                def ev_v(n, pa, pb, hd=hd):
                    P.op("act", lambda h, n=n, pa=pa: h.activation(out=vh[:, n * 4:(n + 1) * 4, :], in_=pa, func=AF.Copy), reads=[pb], writes=[b_v])
                    if kind == "p" and getattr(cfg, "att_sub", 9) >= 1:
                        s_ = ogi[0] % 2
                        ogi[0] += 1
                        P.op("dve", lambda h, pa=pa, s_=s_: h.tensor_copy(out=ostg[s_], in_=pa), reads=[pb], writes=[b_og[s_]])
                        for t4 in range(4):
                            sq_ = u * 4 + n * 2 + t4 // 2
                            r0 = (sq_ * L + l) * SEQ + (t4 % 2) * 128
                            P.dma("sp", lambda h, s_=s_, t4=t4, r0=r0, hd=hd: h.dma_start(out=o_nv[r0:r0 + 128, hd * 128:(hd + 1) * 128], in_=ostg[s_][:, t4, :]),
                                  ch_out[s_], reads=[b_og[s_]])
                if getattr(cfg, "att_sub", 9) >= 0:
                    inproj_tm(l, 60 + hd, ev_v)
                if getattr(cfg, "att_sub", 9) < 1:
                    continue
                if kind == "p":
                    def ev_ko(n, pa, pb, hd=hd):
                        s_ = ogi[0] % 2
                        ogi[0] += 1
                        P.op("dve", lambda h, pa=pa, s_=s_: h.tensor_copy(out=ostg[s_], in_=pa), reads=[pb], writes=[b_og[s_]])
                        for t4 in range(4):
                            sq_ = u * 4 + n * 2 + t4 // 2
                            r0 = (sq_ * L + l) * SEQ + (t4 % 2) * 128
                            P.dma("sp", lambda h, s_=s_, t4=t4, r0=r0, hd=hd: h.dma_start(out=o_nk[r0:r0 + 128, hd * 128:(hd + 1) * 128], in_=ostg[s_][:, t4, :]),
                                  ch_out[s_], reads=[b_og[s_]])
                    inproj_tm(l, 56 + hd, ev_ko)
                else:
                    r0 = (u * L + l) * PAST
                    P.dma("pool", lambda h, r0=r0, hd=hd: h.dma_start(out=ckb, in_=ck[r0:r0 + PAST, hd * 128:(hd + 1) * 128].rearrange("(a p) c -> p a c", p=128)), ch_ctx, writes=[b_ckb])
                    P.dma("pool", lambda h, r0=r0, hd=hd: h.dma_start(out=cvb, in_=cv[r0:r0 + PAST, hd * 128:(hd + 1) * 128].rearrange("(a p) c -> p a c", p=128)), ch_ctx2, writes=[b_cvb])
                    for lt in range(PT):
                        P.op("pe", lambda h, lt=lt: h.transpose(out=TB[:, 1, lt * 128:(lt + 1) * 128], in_=ckb[:, lt, :], identity=ident_b[:]), reads=[b_ckb, b_idb], writes=[bTB[1]])
                    P.op("act", lambda h: h.activation(out=ckT, in_=TB[:, 1, 0:PAST], func=AF.Copy), reads=[bTB[1]], writes=[b_ckT])
                asub = getattr(cfg, "att_sub", 9)
                for R in range(NTT if asub >= 2 else 0):
                    if kind == "p":
                        sq_ = R // 2
                        ktiles = [2 * sq_, 2 * sq_ + 1]
                        nloc = 2
                        P.op("pe", lambda h, R=R, sq_=sq_: h.matmul(SC[:, 0, 0:256], lhsT=qTh[:, R * 128:(R + 1) * 128], rhs=kTh[:, sq_ * 256:(sq_ + 1) * 256], start=True, stop=True),
                             reads=[b_q, b_k], writes=[bSC[0]])
                        P.op("dve", lambda h: h.tensor_reduce(out=stat[:, 0:1], in_=SC[:, 0, 0:256], axis=AX.X, op=ALU.max), reads=[bSC[0]], writes=[b_st])
                        P.op("dve", lambda h: h.tensor_scalar(out=stat[:, 1:2], in0=stat[:, 0:1], scalar1=-1.0, scalar2=None, op0=ALU.mult), reads=[b_st], writes=[b_st])
                        P.op("act", lambda h: h.activation(out=pfull[:, 0:256], in_=SC[:, 0, 0:256], func=AF.Exp, bias=stat[:, 1:2], scale=1.0, accum_out=stat[:, 2:3]),
                             reads=[bSC[0], b_st], writes=[b_pf, b_st])
                        P.op("dve", lambda h: h.reciprocal(out=stat[:, 3:4], in_=stat[:, 2:3]), reads=[b_st], writes=[b_st])
                        P.op("dve", lambda h: h.tensor_scalar(out=pfull[:, 0:256], in0=pfull[:, 0:256], scalar1=stat[:, 3:4], scalar2=None, op0=ALU.mult), reads=[b_st, b_pf], writes=[b_pf])
                        nctx = 0
                    else:
                        r_a, r_b = 2 * R, 2 * R + 1
                        rs_a = min(max(r_a - NA_KH // 2, 0), ROWS - NA_KH)
                        rs_b = min(max(r_b - NA_KH // 2, 0), ROWS - NA_KH)
                        kt0 = rs_a // 2
                        kt1 = (rs_b + NA_KH - 1) // 2
                        nloc = kt1 - kt0 + 1
                        ktiles = list(range(kt0, kt0 + nloc))
                        nctx = PT
                        w1 = min(nloc, 4) * 128
                        P.op("pe", lambda h, R=R, kt0=kt0, w1=w1: h.matmul(SC[:, 0, 0:w1], lhsT=qTh[:, R * 128:(R + 1) * 128], rhs=kTh[:, kt0 * 128:kt0 * 128 + w1], start=True, stop=True),
                             reads=[b_q, b_k], writes=[bSC[0]])
                        if nloc > 4:
                            P.op("pe", lambda h, R=R, kt0=kt0: h.matmul(SC[:, 1, 0:128], lhsT=qTh[:, R * 128:(R + 1) * 128], rhs=kTh[:, (kt0 + 4) * 128:(kt0 + 5) * 128], start=True, stop=True),
                                 reads=[b_q, b_k], writes=[bSC[1]])
                        P.op("pe", lambda h, R=R: h.matmul(SC[:, 2, 0:PAST], lhsT=qTh[:, R * 128:(R + 1) * 128], rhs=ckT, start=True, stop=True),
                             reads=[b_q, b_ckT], writes=[bSC[2]])
                        scflat = SC[:].rearrange("p a b -> p (a b)")
                        c0s = []
                        for ro, (r_, rs_) in enumerate(((r_a, rs_a), (r_b, rs_b))):
                            c0 = (rs_ - 2 * kt0) * 64
                            c0s.append(c0)
                            dr0 = rs_ - r_ + 7
                            psl = slice(ro * 64, ro * 64 + 64)
                            P.op("dve", lambda h, psl=psl, c0=c0, dr0=dr0, hd=hd: h.tensor_tensor(out=sloc[psl, :], in0=scflat[psl, c0:c0 + 512], in1=tab[psl, hd * 15 + dr0:hd * 15 + dr0 + 8, :].rearrange("p a b -> p (a b)"), op=ALU.add),
                                 reads=[bSC[0], bSC[1], b_tab], writes=[b_sl])
                        P.op("dve", lambda h: h.tensor_reduce(out=stat[:, 0:1], in_=sloc, axis=AX.X, op=ALU.max), reads=[b_sl], writes=[b_st])
                        P.op("dve", lambda h: h.tensor_reduce(out=stat[:, 4:5], in_=SC[:, 2, 0:PAST], axis=AX.X, op=ALU.max), reads=[bSC[2]], writes=[b_st])
                        P.op("dve", lambda h: h.tensor_tensor(out=stat[:, 0:1], in0=stat[:, 0:1], in1=stat[:, 4:5], op=ALU.max), reads=[b_st], writes=[b_st])
                        P.op("dve", lambda h: h.tensor_scalar(out=stat[:, 1:2], in0=stat[:, 0:1], scalar1=-1.0, scalar2=None, op0=ALU.mult), reads=[b_st], writes=[b_st])
                        P.op("pool", lambda h: h.memset(pfull, 0.0), writes=[b_pf])
                        for ro in range(2):
                            psl = slice(ro * 64, ro * 64 + 64)
                            c0 = c0s[ro]
                            P.op("act", lambda h, psl=psl, c0=c0: h.activation(out=pfull[psl, c0:c0 + 512], in_=sloc[psl, :], func=AF.Exp, bias=stat[psl, 1:2], scale=1.0, accum_out=stat[psl, 2:3]),
                                 reads=[b_sl, b_st], writes=[b_pf, b_st])
                        P.op("act", lambda h: h.activation(out=pctx, in_=SC[:, 2, 0:PAST], func=AF.Exp, bias=stat[:, 1:2], scale=1.0, accum_out=stat[:, 5:6]),
                             reads=[bSC[2], b_st], writes=[b_pc, b_st])
                        P.op("dve", lambda h: h.tensor_tensor(out=stat[:, 2:3], in0=stat[:, 2:3], in1=stat[:, 5:6], op=ALU.add), reads=[b_st], writes=[b_st])
                        P.op("dve", lambda h: h.reciprocal(out=stat[:, 3:4], in_=stat[:, 2:3]), reads=[b_st], writes=[b_st])
                        P.op("dve", lambda h, nloc=nloc: h.tensor_scalar(out=pfull[:, 0:nloc * 128], in0=pfull[:, 0:nloc * 128], scalar1=stat[:, 3:4], scalar2=None, op0=ALU.mult), reads=[b_st, b_pf], writes=[b_pf])
                        P.op("dve", lambda h: h.tensor_scalar(out=pctx, in0=pctx, scalar1=stat[:, 3:4], scalar2=None, op0=ALU.mult), reads=[b_st, b_pc], writes=[b_pc])
                    if asub < 3:
                        continue
                    ntot = nloc + nctx
                    for i in range(ntot):
                        src = pfull[:, i * 128:(i + 1) * 128] if i < nloc else pctx[:, (i - nloc) * 128:(i - nloc + 1) * 128]
                        bk = 0 if i < 8 else 1
                        ii = i if i < 8 else i - 8
                        P.op("pe", lambda h, src=src, bk=bk, ii=ii: h.transpose(out=TB[:, bk, ii * 128:(ii + 1) * 128], in_=src, identity=ident_b[:]),
                             reads=[b_pf, b_pc, b_idb], writes=[bTB[bk]])
                    n0 = min(ntot, 8)
                    P.op("act", lambda h, n0=n0: h.activation(out=ptr[:, 0:n0, :], in_=TB[:, 0, 0:n0 * 128].rearrange("p (a b) -> p a b", b=128), func=AF.Copy), reads=[bTB[0]], writes=[b_pt])
                    if ntot > 8:
                        P.op("act", lambda h: h.activation(out=ptr[:, 8, :], in_=TB[:, 1, 0:128], func=AF.Copy), reads=[bTB[1]], writes=[b_pt])
                    if asub < 4:
                        continue
                    for i in range(ntot):
                        if i < nloc:
                            lh = vh[:, ktiles[i], :]
                            rd = [b_v, b_pt]
                        else:
                            lh = cvb[:, i - nloc, :]
                            rd = [b_cvb, b_pt]
                        P.op("pe", lambda h, lh=lh, i=i, ntot=ntot: h.matmul(PV[:, 0:128], lhsT=lh, rhs=ptr[:, i, :], start=(i == 0), stop=(i == ntot - 1)), reads=rd, writes=[bPV])
                    P.op("act", lambda h, hd=hd, R=R: h.activation(out=mixT[:, 12 + hd, R * 128:(R + 1) * 128], in_=PV[:, 0:128], func=AF.Copy), reads=[bPV], writes=[bm[12 + hd][R // 4]])

        def conv_mixer(l, nseq, slen):
            (b_cc, b_u, b_y, b_cb) = new_phase(4)
            o = 0
            cc_sb, o = carve(o, [T], F32)
            u_sb, o = carve(o, [T], F32)
            y_sb, o = carve(o, [T], F32)
            cb_sb, o = carve(o, [T], F32)

            def v3(ap):
                return ap.rearrange("p (s t) -> p s t", t=slen)
            for j in range(4):
                def ev_cc(n, pa, pb):
                    P.op("act", lambda h, n=n, pa=pa: h.activation(out=cc_sb[:, hs(n)], in_=pa, func=AF.Copy), reads=[pb], writes=[b_cc])
                inproj_fm(l, 44 + j, ev_cc)

                def ev_cx(n, pa, pb):
                    P.op("dve", lambda h, n=n, pa=pa: h.tensor_tensor(out=u_sb[:, hs(n)], in0=pa, in1=cc_sb[:, hs(n)], op=ALU.mult), reads=[pb, b_cc], writes=[b_u])
                inproj_fm(l, 48 + j, ev_cx)

                def ev_cb(n, pa, pb):
                    P.op("act", lambda h, n=n, pa=pa: h.activation(out=cb_sb[:, hs(n)], in_=pa, func=AF.Copy), reads=[pb], writes=[b_cb])
                inproj_fm(l, 40 + j, ev_cb)
                c0 = (l * 3 + 0) * 4 + j
                c1 = (l * 3 + 1) * 4 + j
                c2 = (l * 3 + 2) * 4 + j
                P.op("dve", lambda h, c1=c1: h.tensor_scalar(out=y_sb, in0=u_sb, scalar1=cwT[:, c1:c1 + 1], scalar2=None, op0=ALU.mult), reads=[b_u, b_vecs], writes=[b_y])
                P.op("dve", lambda h, c0=c0: h.scalar_tensor_tensor(out=v3(y_sb)[:, :, 1:slen], in0=v3(u_sb)[:, :, 0:slen - 1], scalar=cwT[:, c0:c0 + 1], in1=v3(y_sb)[:, :, 1:slen], op0=ALU.mult, op1=ALU.add),
                     reads=[b_u, b_y, b_vecs], writes=[b_y])
                P.op("dve", lambda h, c2=c2: h.scalar_tensor_tensor(out=v3(y_sb)[:, :, 0:slen - 1], in0=v3(u_sb)[:, :, 1:slen], scalar=cwT[:, c2:c2 + 1], in1=v3(y_sb)[:, :, 0:slen - 1], op0=ALU.mult, op1=ALU.add),
                     reads=[b_u, b_y, b_vecs], writes=[b_y])
                for n in range(2):
                    P.op("dve", lambda h, n=n, j=j: h.tensor_tensor(out=mixT[:, 8 + j, hs(n)], in0=y_sb[:, hs(n)], in1=cb_sb[:, hs(n)], op=ALU.mult), reads=[b_y, b_cb], writes=[bm[8 + j][n]])

        ch_st = [P.chan(), P.chan()]
        ch_so = [P.chan() for _ in range(16)]

        def hgrn(l, kind, u):
            names = ["q", "xg", "v", "v4", "oacc", "s", "F", "kk", "Bc", "B", "tE", "qt", "kt", "kh", "khT", "U", "Sp", "at", "c0", "c1", "dec"]
            bl = new_phase(len(names) + 32)
            B_ = dict(zip(names, bl))
            bU = bl[len(names):len(names) + 16]
            bSp = bl[len(names) + 16:len(names) + 32]
            o = 0
            q_bf, o = carve(o, [T], BF16)
            xg, o = carve(o, [T], BF16)
            v_tok, o = carve(o, [NTT, 128], BF16)
            V4, o = carve(o, [4, 4 * 128], BF16)
            o_acc, o = carve(o, [T], F32)
            s_sb, o = carve(o, [T], F32)
            Fb, o = carve(o, [512], F32)
            kkb, o = carve(o, [512], BF16)
            Bc, o = carve(o, [512], F32)
            Bb, o = carve(o, [512], F32)
            tE, o = carve(o, [512], F32)
            qt, o = carve(o, [512], BF16)
            kt, o = carve(o, [512], BF16)
            kh, o = carve(o, [512], BF16)
            khT, o = carve(o, [4, 128], BF16)
            U, o = carve(o, [16, 128], F32)
            Sp, o = carve(o, [16, 128], BF16)
            at_sb, o = carve(o, [4, 128], BF16)
            carry = []
            for i in range(2):
                a_, o = carve(o, [128], F32)
                carry.append(a_)
            dec, o = carve(o, [16], F32)
            bcar = [B_["c0"], B_["c1"]]
            chunks_per_seq = (SEQ if kind == "p" else DSEQ) // 32

            for hd in range(8):
                def ev_q(n, pa, pb):
                    P.op("act", lambda h, n=n, pa=pa: h.activation(out=q_bf[:, hs(n)], in_=pa, func=AF.Copy, scale=128.0 ** -0.5), reads=[pb], writes=[B_["q"]])
                inproj_fm(l, hd, ev_q)

                def ev_g(n, pa, pb):
                    P.op("dve", lambda h, n=n, pa=pa: h.tensor_copy(out=xg[:, hs(n)], in_=pa), reads=[pb], writes=[B_["xg"]])
                inproj_fm(l, 32 + hd, ev_g)

                def ev_v(n, pa, pb):
                    P.op("act", lambda h, n=n, pa=pa: h.activation(out=v_tok[:, n * 4:(n + 1) * 4, :], in_=pa, func=AF.Copy), reads=[pb], writes=[B_["v"]])
                inproj_tm(l, 8 + hd, ev_v)

                for dr in range(2):
                    col = (dr * L + l) * 8 + hd
                    lb_ap, om_ap, nom_ap = LB[:, col:col + 1], OM[:, col:col + 1], NOM[:, col:col + 1]

                    def ev_f(n, pa, pb):
                        P.op("act", lambda h, n=n, pa=pa: h.activation(out=s_sb[:, hs(n)], in_=pa, func=AF.Sigmoid), reads=[pb], writes=[B_["s"]])
                    inproj_fm(l, (16 if dr == 0 else 24) + hd, ev_f)
                    mask = maskf if dr == 0 else maskb
                    ci = 0
                    have_state = False
                    for sg in ((0, 1) if dr == 0 else (1, 0)):
                        cs_ = hs(sg)
                        P.op("dve", lambda h, cs_=cs_, om_ap=om_ap, lb_ap=lb_ap: h.tensor_scalar(out=Fb, in0=s_sb[:, cs_], scalar1=om_ap, scalar2=lb_ap, op0=ALU.mult, op1=ALU.add), reads=[B_["s"], b_lb], writes=[B_["F"]])
                        P.op("dve", lambda h, cs_=cs_, om_ap=om_ap, nom_ap=nom_ap: h.tensor_scalar(out=kkb, in0=s_sb[:, cs_], scalar1=nom_ap, scalar2=om_ap, op0=ALU.mult, op1=ALU.add), reads=[B_["s"], b_lb], writes=[B_["kk"]])
                        P.op("act", lambda h: h.activation(out=Fb, in_=Fb, func=AF.Ln), reads=[B_["F"]], writes=[B_["F"]])
                        P.op("dve", lambda h: h.tensor_tensor_scan(out=Bc, data0=cstart[:], data1=Fb, initial=0.0, op0=ALU.mult, op1=ALU.add), reads=[B_["F"], b_const], writes=[B_["Bc"]])
                        Bc3 = Bc.rearrange("p (c j) -> p c j", j=32)
                        tot = Bc3[:, :, 31]
                        totb = tot.unsqueeze(2).to_broadcast([128, 16, 32])
                        if dr == 0:
                            Bsrc, bB = Bc, B_["Bc"]
                        else:
                            P.op("dve", lambda h: h.tensor_tensor(out=Bb.rearrange("p (c j) -> p c j", j=32), in0=totb, in1=Bc3, op=ALU.subtract), reads=[B_["Bc"]], writes=[B_["B"]])
                            P.op("dve", lambda h: h.tensor_tensor(out=Bb, in0=Bb, in1=Fb, op=ALU.add), reads=[B_["B"], B_["F"]], writes=[B_["B"]])
                            Bsrc, bB = Bb, B_["B"]
                        P.op("act", lambda h, Bsrc=Bsrc: h.activation(out=tE, in_=Bsrc, func=AF.Exp), reads=[bB], writes=[B_["tE"]])
                        P.op("dve", lambda h, cs_=cs_: h.tensor_tensor(out=qt, in0=q_bf[:, cs_], in1=tE, op=ALU.mult), reads=[B_["q"], B_["tE"]], writes=[B_["qt"]])
                        P.op("act", lambda h, Bsrc=Bsrc: h.activation(out=tE, in_=Bsrc, func=AF.Exp, scale=-1.0), reads=[bB, B_["qt"]], writes=[B_["tE"]])
                        P.op("dve", lambda h: h.tensor_tensor(out=kt, in0=kkb, in1=tE, op=ALU.mult), reads=[B_["kk"], B_["tE"]], writes=[B_["kt"]])
                        P.op("dve", lambda h, Bsrc=Bsrc: h.tensor_tensor(out=tE.rearrange("p (c j) -> p c j", j=32), in0=totb, in1=Bsrc.rearrange("p (c j) -> p c j", j=32), op=ALU.subtract),
                             reads=[bB, B_["Bc"], B_["kt"]], writes=[B_["tE"]])
                        P.op("act", lambda h: h.activation(out=tE, in_=tE, func=AF.Exp), reads=[B_["tE"]], writes=[B_["tE"]])
                        P.op("dve", lambda h: h.tensor_tensor(out=kh, in0=kkb, in1=tE, op=ALU.mult), reads=[B_["kk"], B_["tE"]], writes=[B_["kh"]])
                        P.op("act", lambda h: h.activation(out=dec, in_=tot, func=AF.Exp), reads=[B_["Bc"]], writes=[B_["dec"]])
                        for t4 in range(4):
                            P.op("pe", lambda h, t4=t4: h.transpose(out=TB[:, 0, t4 * 128:(t4 + 1) * 128], in_=kh[:, t4 * 128:(t4 + 1) * 128], identity=ident_b[:]), reads=[B_["kh"], b_idb], writes=[bTB[0]])
                        P.op("act", lambda h: h.activation(out=khT, in_=TB[:, 0, 0:512].rearrange("p (a b) -> p a b", b=128), func=AF.Copy), reads=[bTB[0]], writes=[B_["khT"]])
                        V43 = V4.rearrange("p t (c v) -> p t c v", v=128)
                        for c in range(4):
                            P.op("pool", lambda h, c=c, sg=sg: h.tensor_scalar(out=V43[:, :, c, :], in0=v_tok[:, sg * 4:(sg + 1) * 4, :], scalar1=rowsel[:, c:c + 1], scalar2=None, op0=ALU.mult),
                                 reads=[B_["v"], b_const], writes=[B_["v4"]])
                        for t4 in range(4):
                            P.op("pe", lambda h, t4=t4: h.matmul(SC[:, 0, :], lhsT=khT[:, t4, :], rhs=V4[:, t4, :], start=True, stop=True), reads=[B_["khT"], B_["v4"]], writes=[bSC[0]])
                            P.op("act", lambda h, t4=t4: h.activation(out=U[:, t4 * 4:(t4 + 1) * 4, :], in_=SC[:, 0, :].rearrange("p (a b) -> p a b", b=128), func=AF.Copy), reads=[bSC[0]], writes=bU[t4 * 4:(t4 + 1) * 4])
                        order = list(range(16)) if dr == 0 else list(range(15, -1, -1))
                        prev = None
                        prev_b = None
                        for cg in order:
                            gch = sg * 16 + cg
                            sidx = gch // chunks_per_seq
                            pos = gch % chunks_per_seq
                            seq_start = (pos == 0) if dr == 0 else (pos == chunks_per_seq - 1)
                            seq_end = (pos == chunks_per_seq - 1) if dr == 0 else (pos == 0)
                            if seq_start:
                                if kind == "p":
                                    prev, prev_b = None, None
                                else:
                                    r0 = (((u * L + l) * 2 + dr) * 8 + hd) * 128
                                    P.dma("sp", lambda h, r0=r0, ci=ci: h.dma_start(out=carry[ci], in_=st[r0:r0 + 128, :]), ch_st[ci], writes=[bcar[ci]])
                                    prev, prev_b = carry[ci], bcar[ci]
                            elif cg == order[0]:
                                prev, prev_b = carry[ci], bcar[ci]
                            if prev is None:
                                P.op("pool", lambda h, cg=cg: h.memset(Sp[:, cg, :], 0.0), writes=[bSp[cg]])
                            else:
                                P.op("pool", lambda h, cg=cg, prev=prev: h.tensor_copy(out=Sp[:, cg, :], in_=prev), reads=[prev_b], writes=[bSp[cg]])
                                P.op("dve", lambda h, cg=cg, prev=prev: h.scalar_tensor_tensor(out=U[:, cg, :], in0=prev, scalar=dec[:, cg:cg + 1], in1=U[:, cg, :], op0=ALU.mult, op1=ALU.add),
                                     reads=[prev_b, B_["dec"], bU[cg]], writes=[bU[cg]])
                            prev, prev_b = U[:, cg, :], bU[cg]
                            if seq_end and kind == "p":
                                sq_ = u * 4 + sidx
                                r0 = (((sq_ * L + l) * 2 + dr) * 8 + hd) * 128
                                P.dma("sp", lambda h, r0=r0, cg=cg: h.dma_start(out=o_ns[r0:r0 + 128, :], in_=U[:, cg, :]), ch_so[cg], reads=[bU[cg]])
                        nci = 1 - ci
                        P.op("pool", lambda h, nci=nci, cg=order[-1]: h.tensor_copy(out=carry[nci], in_=U[:, cg, :]), reads=[bU[cg]], writes=[bcar[nci]])
                        ci = nci
                        for t4 in range(4):
                            P.op("pe", lambda h, t4=t4: h.matmul(SC[:, 1, t4 * 128:(t4 + 1) * 128], lhsT=kt[:, t4 * 128:(t4 + 1) * 128], rhs=qt[:, t4 * 128:(t4 + 1) * 128], start=True, stop=True),
                                 reads=[B_["kt"], B_["qt"]], writes=[bSC[1]])
                        P.op("dve", lambda h, mask=mask: h.tensor_tensor(out=at_sb, in0=SC[:, 1, :].rearrange("p (a b) -> p a b", b=128), in1=mask[:].unsqueeze(1).to_broadcast([128, 4, 128]), op=ALU.mult),
                             reads=[bSC[1], b_const], writes=[B_["at"]])
                        for t4 in range(4):
                            P.op("pe", lambda h, t4=t4, sg=sg: h.matmul(SC[:, 2, t4 * 128:(t4 + 1) * 128], lhsT=v_tok[:, sg * 4 + t4, :], rhs=at_sb[:, t4, :], start=True, stop=False),
                                 reads=[B_["v"], B_["at"]], writes=[bSC[2]])
                            for c in range(4):
                                cg = t4 * 4 + c
                                P.op("pe", lambda h, t4=t4, c=c, cg=cg: h.matmul(SC[:, 2, t4 * 128 + c * 32:t4 * 128 + (c + 1) * 32], lhsT=Sp[:, cg, :], rhs=qt[:, t4 * 128 + c * 32:t4 * 128 + (c + 1) * 32], start=False, stop=(c == 3)),
                                     reads=[bSp[cg], B_["qt"]], writes=[bSC[2]])
                        if dr == 0:
                            P.op("act", lambda h, cs_=cs_: h.activation(out=o_acc[:, cs_], in_=SC[:, 2, :], func=AF.Copy), reads=[bSC[2]], writes=[B_["oacc"]])
                        else:
                            P.op("dve", lambda h, cs_=cs_: h.tensor_tensor(out=o_acc[:, cs_], in0=SC[:, 2, :], in1=o_acc[:, cs_], op=ALU.add), reads=[bSC[2], B_["oacc"]], writes=[B_["oacc"]])
                for n in range(2):
                    P.op("act", lambda h, n=n: h.activation(out=qt, in_=o_acc[:, hs(n)], func=AF.Square), reads=[B_["oacc"]], writes=[B_["qt"]])
                    P.op("pe", lambda h: h.matmul(PV[:], lhsT=ones_b[:], rhs=qt, start=True, stop=True), reads=[B_["qt"], b_idb], writes=[bPV])
                    P.op("act", lambda h: h.activation(out=tE, in_=PV[:], func=AF.Sqrt, scale=1.0 / 128.0, bias=eps_ap), reads=[bPV, b_eps], writes=[B_["tE"]])
                    P.op("dve", lambda h: h.reciprocal(out=tE, in_=tE), reads=[B_["tE"]], writes=[B_["tE"]])
                    P.op("dve", lambda h, n=n: h.tensor_tensor(out=Bc, in0=o_acc[:, hs(n)], in1=tE, op=ALU.mult), reads=[B_["oacc"], B_["tE"]], writes=[B_["Bc"]])
                    P.op("act", lambda h, n=n: h.activation(out=Fb, in_=xg[:, hs(n)], func=AF.Sigmoid), reads=[B_["xg"]], writes=[B_["F"]])
                    P.op("dve", lambda h, n=n: h.tensor_tensor(out=Fb, in0=Fb, in1=xg[:, hs(n)], op=ALU.mult), reads=[B_["xg"], B_["F"]], writes=[B_["F"]])
                    P.op("dve", lambda h, n=n, hd=hd: h.scalar_tensor_tensor(out=mixT[:, hd, hs(n)], in0=Bc, scalar=gnwT[:, l:l + 1], in1=Fb, op0=ALU.mult, op1=ALU.mult),
                         reads=[B_["Bc"], B_["F"], b_vecs], writes=[bm[hd][n]])

        def out_proj(l, v):
            for m in range(KC):
                slot, bsl = wload(w_out_r[(l * KC + m) * 128:(l * KC + m + 1) * 128, :], 2048, bw_out[l])
                for n in range(2):
                    bank, bb = next_bank(dense_banks4)
                    bap = bank[:] if (bank is D0 or bank is D1) else bank
                    for k in range(16):
                        P.op("pe", lambda h, k=k, n=n, bap=bap, slot=slot: h.matmul(bap, lhsT=slot[:, k * 128:(k + 1) * 128], rhs=mixT[:, k, hs(n)], start=(k == 0), stop=(k == 15)),
                             reads=[bsl, bm[k][n]], writes=[bb])
                    P.op("dve", lambda h, m=m, n=n, bap=bap: h.scalar_tensor_tensor(out=xT[:, m, hs(n)], in0=bap, scalar=mods[:, l, 2, m, v:v + 1], in1=xT[:, m, hs(n)], op0=ALU.mult, op1=ALU.add),
                         reads=[bb, bx[m][n], b_mods], writes=[bx[m][n]])

        ch_wd = [P.chan(), P.chan()]

        def ffn(l, v):
            nb_ = 6
            (b_a, b_sg0, b_sg1, b_wd0, b_wd1, b_dummy) = new_phase(nb_)
            for k in range(16):
                for n in range(2):
                    if bm[k][n].w is not None:
                        b_a.rs.append(bm[k][n].w)
                    b_a.rs.extend(bm[k][n].rs)
            o = 0
            a_ext = None
            if FC > 32:
                a_ext, o = carve(o, [FC - 32, 512], BF16)
            sg_t = []
            for i in range(2):
                a_, o = carve(o, [512], F32)
                sg_t.append(a_)
            wds = []
            for i in range(2):
                a_, o = carve(o, [FC * 128], BF16)
                wds.append(a_)
            b_sg = [b_sg0, b_sg1]
            b_wd = [b_wd0, b_wd1]
            mflat = mixT[:].rearrange("p a b -> p (a b)")

            def a_ap(j):
                if j < 32:
                    return mflat[:, j * 512:(j + 1) * 512]
                return a_ext[:, j - 32, :]
            wdi = 0
            for n in range(2):
                for j in range(FC):
                    sl_g, bg_ = wload(wg_r[(l * FC + j) * 128:(l * FC + j + 1) * 128, :], KC * 128, bw_g[l])
                    sl_u, bu_ = wload(wu_r[(l * FC + j) * 128:(l * FC + j + 1) * 128, :], KC * 128, bw_u[l])
                    bank_g, bbg = next_bank(dense_banks4)
                    bank_u, bbu = next_bank(dense_banks4)
                    bg_ap = bank_g[:] if (bank_g is D0 or bank_g is D1) else bank_g
                    bu_ap = bank_u[:] if (bank_u is D0 or bank_u is D1) else bank_u
                    for k in range(KC):
                        P.op("pe", lambda h, k=k, n=n, bg_ap=bg_ap, sl_g=sl_g: h.matmul(bg_ap, lhsT=sl_g[:, k * 128:(k + 1) * 128], rhs=hT[:, k, hs(n)], start=(k == 0), stop=(k == KC - 1)),
                             reads=[bg_, bh[k][n]], writes=[bbg])
                    for k in range(KC):
                        P.op("pe", lambda h, k=k, n=n, bu_ap=bu_ap, sl_u=sl_u: h.matmul(bu_ap, lhsT=sl_u[:, k * 128:(k + 1) * 128], rhs=hT[:, k, hs(n)], start=(k == 0), stop=(k == KC - 1)),
                             reads=[bu_, bh[k][n]], writes=[bbu])
                    s_ = j % 2
                    P.op("act", lambda h, s_=s_, bg_ap=bg_ap: h.activation(out=sg_t[s_], in_=bg_ap, func=AF.Silu), reads=[bbg], writes=[b_sg[s_]])
                    P.op("dve", lambda h, s_=s_, bu_ap=bu_ap, j=j: h.tensor_tensor(out=a_ap(j), in0=bu_ap, in1=sg_t[s_], op=ALU.mult), reads=[bbu, b_sg[s_]], writes=[b_a])
                for m in range(KC):
                    s_ = wdi % 2
                    wdi += 1
                    P.dma("sp", lambda h, s_=s_, m=m: h.dma_start(out=wds[s_], in_=wd_r[(l * KC + m) * 128:(l * KC + m + 1) * 128, :]), ch_wd[s_], reads=[bw_d[l]], writes=[b_wd[s_]])
                    bank, bb = next_bank(dense_banks4)
                    bap = bank[:] if (bank is D0 or bank is D1) else bank
                    for j in range(FC):
                        P.op("pe", lambda h, j=j, s_=s_, bap=bap: h.matmul(bap, lhsT=wds[s_][:, j * 128:(j + 1) * 128], rhs=a_ap(j), start=(j == 0), stop=(j == FC - 1)),
                             reads=[b_wd[s_], b_a], writes=[bb])
                    P.op("dve", lambda h, m=m, n=n, bap=bap: h.scalar_tensor_tensor(out=xT[:, m, hs(n)], in0=bap, scalar=mods[:, l, 5, m, v:v + 1], in1=xT[:, m, hs(n)], op0=ALU.mult, op1=ALU.add),
                         reads=[bb, bx[m][n], b_mods], writes=[bx[m][n]])
            for k in range(16):
                for n in range(2):
                    bm[k][n].w = None
                    bm[k][n].rs = list(b_a.rs) + ([b_a.w] if b_a.w is not None else [])

        def final_store(dst_rows):
            (b_q0, b_q1, b_r, b_y0, b_y1, b_hy) = new_phase(6)
            for k in range(KC):
                for n in range(2):
                    if bh[k][n].w is not None:
                        b_hy.rs.append(bh[k][n].w)
                    b_hy.rs.extend(bh[k][n].rs)
            o = 0
            sq = []
            for i in range(2):
                a_, o = carve(o, [512], BF16)
                sq.append(a_)
            rbuf, o = carve(o, [512], F32)
            ystg = []
            for i in range(2):
                a_, o = carve(o, [D], F32)
                ystg.append(a_)
            b_ys = [b_y0, b_y1]
            yT = hT[:].rearrange("p a b -> p (a b)").bitcast(F32).rearrange("p (a b) -> p a b", b=512)
            si = 0
            for n in range(2):
                norm_stats(n, rbuf, b_r, sq, [b_q0, b_q1])
                for k in range(KC):
                    P.op("dve", lambda h, k=k, n=n: h.scalar_tensor_tensor(out=yT[:, k, :], in0=xT[:, k, hs(n)], scalar=fnwT[:, k:k + 1], in1=rbuf, op0=ALU.mult, op1=ALU.mult),
                         reads=[bx[k][n], b_r, b_vecs], writes=[b_hy])
                for t4 in range(4):
                    s_ = si % 2
                    si += 1
                    for k0 in range(0, KC, 4):
                        nk_ = min(4, KC - k0)
                        for kk_ in range(nk_):
                            P.op("pe", lambda h, k0=k0, kk_=kk_, t4=t4: h.transpose(out=SC[:, 0, kk_ * 128:(kk_ + 1) * 128], in_=yT[:, k0 + kk_, t4 * 128:(t4 + 1) * 128], identity=ident_f[:]),
                                 reads=[b_hy, b_const], writes=[bSC[0]])
                        P.op("act", lambda h, s_=s_, k0=k0, nk_=nk_: h.activation(out=ystg[s_][:, k0 * 128:(k0 + nk_) * 128], in_=SC[:, 0, 0:nk_ * 128], func=AF.Copy), reads=[bSC[0]], writes=[b_ys[s_]])
                    tt = n * 4 + t4
                    P.dma("sp", lambda h, s_=s_, tt=tt: h.dma_start(out=dst_rows[tt * 128:(tt + 1) * 128, :], in_=ystg[s_]), ch_out[s_], reads=[b_ys[s_]])
            for k in range(KC):
                for n in range(2):
                    bh[k][n].w = None
                    bh[k][n].rs = list(b_hy.rs) + ([b_hy.w] if b_hy.w is not None else [])

        units = [("p", u) for u in range(NP // 4)] + [("s", u) for u in range(NS)]
        for kind, u in units:
            if kind == "p":
                src = xp[u * T:(u + 1) * T, :]
                dst = yp[u * T:(u + 1) * T, :]
                v = 0
                nseq, slen = 4, SEQ
            else:
                src = xs[u * T:(u + 1) * T, :]
                dst = ys[u * T:(u + 1) * T, :]
                v = 1 + u
                nseq, slen = 1, DSEQ
            if cfg.stop < 3:
                break
            load_x(src)
            for l in range(L):
                norm_mod(l, v, A1, 0)
                if cfg.stop >= 4 and kind in getattr(cfg, "att_kinds", "ps"):
                    attention(l, kind, u)
                if cfg.stop >= 5:
                    conv_mixer(l, nseq, slen)
                if cfg.stop >= 6:
                    hgrn(l, kind, u)
                if cfg.stop >= 7:
                    out_proj(l, v)
                    norm_mod(l, v, A2, 3)
                    ffn(l, v)
            if cfg.stop >= 8:
                final_store(dst)

        fin = Buf()
        for ch in [ch_out[0], ch_out[1], ch_misc, ch_tab] + ch_so:
            if ch.cnt:
                fin.rs.append(("d", ch, ch.cnt))
        P.wait_all("sp", [fin] + phase_bufs)
        P.emit_all()
    return nc


N_ACTIVE = 4


def make_in_maps(cfg, n_act, x_prompt, x_sample, cache_na_k, cache_na_v, state_hgrn, c, c_ctx, w_ada, b_ada, norm_mix_w, w_in,
                 hgrn_lb_raw, hgrn_gnorm_w, conv_w, na_rpb, w_out, norm_ffn_w, w_ffn_gate, w_ffn_up, w_ffn_down, final_norm_w):
    f = lambda a: np.ascontiguousarray(np.asarray(a, dtype=np.float32))
    D, L, KC, NP, NS = cfg.D, cfg.L, cfg.KC, cfg.NP, cfg.NS
    consts = host_consts()
    shared = dict(
        w_ada=f(w_ada).reshape(L * D, 6 * D), b_ada=f(b_ada).reshape(L * 6 * KC, 128), nmw=f(norm_mix_w).reshape(L * KC, 128),
        w_in=f(w_in).reshape(L * D, 8192), lbr=f(hgrn_lb_raw).reshape(2 * L * 8, 128), gnw=f(hgrn_gnorm_w).reshape(L, 128),
        cw=f(conv_w).reshape(L * 12, 128), rpb=f(na_rpb).reshape(L * 60, 31), w_out=f(w_out).reshape(L * 2048, D),
        nfw=f(norm_ffn_w).reshape(L * KC, 128), wg=f(w_ffn_gate).reshape(L * D, cfg.FH), wu=f(w_ffn_up).reshape(L * D, cfg.FH),
        wd=f(w_ffn_down).reshape(L * cfg.FH, D), fnw=f(final_norm_w).reshape(KC, 128), **consts)
    maps = []
    for ci in range(n_act):
        m = dict(shared)
        m["xp"] = f(x_prompt[ci * NP:(ci + 1) * NP]).reshape(NP * SEQ, D)
        m["xs"] = f(x_sample[ci * NS:(ci + 1) * NS]).reshape(NS * DSEQ, D)
        m["ck"] = f(cache_na_k[ci * NS:(ci + 1) * NS]).reshape(NS * L * cfg.PAST, 512)
        m["cv"] = f(cache_na_v[ci * NS:(ci + 1) * NS]).reshape(NS * L * cfg.PAST, 512)
        m["st"] = f(state_hgrn[ci * NS:(ci + 1) * NS]).reshape(NS * L * 2 * 8 * 128, 128)
        m["cvec"] = np.concatenate([f(c_ctx).reshape(1, D), f(c[ci * NS:(ci + 1) * NS]).reshape(NS, D)], axis=0)
        maps.append(m)
    return maps


def gather_outputs(cfg, res):
    D, L, NP, NS = cfg.D, cfg.L, cfg.NP, cfg.NS
    yp = np.concatenate([r["yp"].reshape(NP, SEQ, D) for r in res], axis=0)
    ys = np.concatenate([r["ys"].reshape(NS, DSEQ, D) for r in res], axis=0)
    nk = np.concatenate([r["o_nk"].reshape(NP, L, SEQ, 4, 128) for r in res], axis=0)
    nv = np.concatenate([r["o_nv"].reshape(NP, L, SEQ, 4, 128) for r in res], axis=0)
    ns = np.concatenate([r["o_ns"].reshape(NP, L, 2, 8, 128, 128) for r in res], axis=0)
    return (yp.astype(np.float32), ys.astype(np.float32), nk.astype(np.float32), nv.astype(np.float32), ns.astype(np.float32))


def kernel(**inputs):
    xpr = np.asarray(inputs["x_prompt"])
    xsa = np.asarray(inputs["x_sample"])
    n_act = N_ACTIVE
    cfg = Cfg(D=xpr.shape[2], NP=xpr.shape[0] // n_act, NS=xsa.shape[0] // n_act, L=np.asarray(inputs["w_in"]).shape[0],
              PAST=np.asarray(inputs["cache_na_k"]).shape[2])
    nc = build(cfg)
    maps = make_in_maps(cfg, n_act, **inputs)
    res = run_bass_kernel_spmd(nc, maps, core_ids=list(range(n_act)))
    return gather_outputs(cfg, res.results)
```

```python
import numpy as np
from contextlib import ExitStack
import concourse.bass as bass
import concourse.mybir as mybir
from concourse.bass_utils import run_bass_kernel_spmd

F32 = mybir.dt.float32
BF16 = mybir.dt.bfloat16
ALU = mybir.AluOpType
AF = mybir.ActivationFunctionType
AX = mybir.AxisListType
ENGS = ("pe", "act", "dve", "pool", "sp")
EPS = 1e-6
NEG = -30000.0


class Buf:
    __slots__ = ("w", "rs", "excl")

    def __init__(self, excl=False):
        self.w = None
        self.rs = []
        self.excl = excl


class Chan:
    __slots__ = ("sem", "cnt")

    def __init__(self, sem):
        self.sem = sem
        self.cnt = 0


class Prog:
    def __init__(self, nc, es):
        self.nc = nc
        self.es = es
        self.ins = {e: [] for e in ENGS}
        self.esem = {e: es.enter_context(nc.semaphore("s_" + e)) for e in ENGS}
        self.nchan = 0

    def chan(self):
        s = self.es.enter_context(self.nc.semaphore("d%d" % self.nchan))
        self.nchan += 1
        return Chan(s)

    def _deps(self, eng, reads, writes, is_dma):
        deps = []
        for r in reads:
            if r.w is not None:
                deps.append((r.w, True))
            if r.excl:
                for e in r.rs:
                    deps.append((e, False))
        for w in writes:
            if w.w is not None:
                deps.append((w.w, True))
            for e in w.rs:
                deps.append((e, False))
        out = []
        for d, iswr in deps:
            if (not is_dma) and d[0] == "e" and d[1] == eng:
                if eng == "pe":
                    continue
                if not iswr:
                    continue
            out.append(d)
        return out

    def _commit(self, ev, reads, writes):
        for r in reads:
            r.rs.append(ev)
        for w in writes:
            w.w = ev
            w.rs = []

    def op(self, eng, fn, reads=(), writes=()):
        deps = self._deps(eng, reads, writes, False)
        ev = ("e", eng, len(self.ins[eng]))
        self.ins[eng].append([fn, deps, False, None])
        self._commit(ev, reads, writes)

    def dma(self, eng, fn, chan, reads=(), writes=(), inc=16):
        deps = self._deps(eng, reads, writes, True)
        chan.cnt += inc
        ev = ("d", chan, chan.cnt)
        self.ins[eng].append([fn, deps, False, chan])
        self._commit(ev, reads, writes)

    def wait_all(self, eng, bufs):
        deps = []
        for b in bufs:
            if b.w is not None:
                deps.append(b.w)
            deps.extend(b.rs)
        self.ins[eng].append([None, deps, False, None])

    def emit_all(self):
        EPOCH = 16000
        for e in ENGS:
            for it in self.ins[e]:
                for d in it[1]:
                    if d[0] == "e":
                        self.ins[d[1]][d[2]][2] = True
        cnt = {}
        esems = {}
        for e in ENGS:
            c = 0
            arr = []
            for it in self.ins[e]:
                if it[2]:
                    c += 1
                k = max(c - 1, 0)
                arr.append((k // EPOCH, k % EPOCH + 1 if c > 0 else 0))
            cnt[e] = arr
            nep = (max(c - 1, 0)) // EPOCH + 1
            esems[e] = [self.esem[e]] + [self.es.enter_context(self.nc.semaphore("s_%s_%d" % (e, i))) for i in range(1, nep)]

        def emit(eng, h):
            seen_e = {}
            seen_d = {}
            for it_i, it in enumerate(self.ins[eng]):
                need_e = {}
                need_d = {}
                for d in it[1]:
                    if d[0] == "e":
                        ep, c = cnt[d[1]][d[2]]
                        key = (d[1], ep)
                        if c > seen_e.get(key, 0) and c > need_e.get(key, 0):
                            need_e[key] = c
                    else:
                        k = id(d[1])
                        if d[2] > seen_d.get(k, 0) and d[2] > need_d.get(k, (None, 0))[1]:
                            need_d[k] = (d[1], d[2])
                for key, c in need_e.items():
                    h.wait_ge(esems[key[0]][key[1]], c)
                    seen_e[key] = c
                for k, (ch, c) in need_d.items():
                    h.wait_ge(ch.sem, c)
                    seen_d[k] = c
                if it[0] is None:
                    continue
                ins = it[0](h)
                if it[3] is not None:
                    ins.then_inc(it[3].sem, 16)
                elif it[2]:
                    ins.then_inc(esems[eng][cnt[eng][it_i][0]], 1)

        with self.nc.Block() as block:
            @block.tensor
            def _(h):
                emit("pe", h)

            @block.scalar
            def _(h):
                emit("act", h)

            @block.vector
            def _(h):
                emit("dve", h)

            @block.gpsimd
            def _(h):
                emit("pool", h)

            @block.sync
            def _(h):
                emit("sp", h)


class Cfg:
    def __init__(self, D=2048, NP=16, NS=8, L=2, PAST=512, debug=False, stop=99):
        self.stop = stop
        self.D = D
        self.KC = D // 128
        self.FH = ((8 * D + 3 * 256 - 1) // (3 * 256)) * 256
        self.FC = self.FH // 128
        self.NP = NP
        self.NS = NS
        self.L = L
        self.PAST = PAST
        self.PT = PAST // 128
        self.NV = 1 + NS
        self.debug = debug


SEQ = 256
DSEQ = 1024
T = 1024
NTT = 8
GRID_W = 64
ROWS = 16
NA_KH = 8
NA_KW = 16


def host_consts():
    ident = np.eye(128, dtype=np.float32)
    j = np.arange(128)[:, None]
    i = np.arange(128)[None, :]
    same = (j // 32) == (i // 32)
    maskf = (same & (j <= i)).astype(np.float32)
    maskb = (same & (j >= i)).astype(np.float32)
    cstart = np.ones((128, 512), np.float32)
    cstart[:, ::32] = 0.0
    col = np.arange(GRID_W)
    cs = np.clip(col - NA_KW // 2, 0, GRID_W - NA_KW)
    valid = (col[None, :] >= cs[:, None]) & (col[None, :] < cs[:, None] + NA_KW)
    cm = np.where(valid, 0.0, NEG).astype(np.float32)
    colmask = np.concatenate([cm, cm], axis=0)
    rowsel = (np.arange(128)[:, None] // 32 == np.arange(4)[None, :]).astype(np.float32)
    return dict(ident=ident, maskf=maskf, maskb=maskb, cstart=cstart, colmask=colmask, rowsel=rowsel)


def build(cfg):
    D, KC, FH, FC, NP, NS, L, PAST, PT, NV = cfg.D, cfg.KC, cfg.FH, cfg.FC, cfg.NP, cfg.NS, cfg.L, cfg.PAST, cfg.PT, cfg.NV
    nc = bass.Bass("TRN2", target_bir_lowering=False)

    def din(name, shape, dt=F32):
        return nc.dram_tensor(name, list(shape), dt, kind="ExternalInput").ap()

    def dout(name, shape, dt=F32):
        return nc.dram_tensor(name, list(shape), dt, kind="ExternalOutput").ap()

    xp = din("xp", [max(NP, 1) * SEQ, D])
    xs = din("xs", [max(NS, 1) * DSEQ, D])
    ck = din("ck", [max(NS, 1) * L * PAST, 512])
    cv = din("cv", [max(NS, 1) * L * PAST, 512])
    st = din("st", [max(NS, 1) * L * 2 * 8 * 128, 128])
    cvec = din("cvec", [NV, D])
    w_ada = din("w_ada", [L * D, 6 * D])
    b_ada = din("b_ada", [L * 6 * KC, 128])
    nmw = din("nmw", [L * KC, 128])
    w_in = din("w_in", [L * D, 8192])
    lbr = din("lbr", [2 * L * 8, 128])
    gnw = din("gnw", [L, 128])
    cw = din("cw", [L * 12, 128])
    rpb = din("rpb", [L * 60, 31])
    w_out = din("w_out", [L * 2048, D])
    nfw = din("nfw", [L * KC, 128])
    wg = din("wg", [L * D, FH])
    wu = din("wu", [L * D, FH])
    wd = din("wd", [L * FH, D])
    fnw = din("fnw", [KC, 128])
    c_ident = din("ident", [128, 128])
    c_maskf = din("maskf", [128, 128])
    c_maskb = din("maskb", [128, 128])
    c_cstart = din("cstart", [128, 512])
    c_colmask = din("colmask", [128, 64])
    c_rowsel = din("rowsel", [128, 4])

    yp = dout("yp", [max(NP, 1) * SEQ, D])
    ys = dout("ys", [max(NS, 1) * DSEQ, D])
    o_nk = dout("o_nk", [max(NP, 1) * L * SEQ, 512])
    o_nv = dout("o_nv", [max(NP, 1) * L * SEQ, 512])
    o_ns = dout("o_ns", [max(NP, 1) * L * 2 * 8 * 128, 128])

    w_in_r = nc.dram_tensor("w_in_r", [L * 64 * 128, KC * 128], BF16).ap()
    w_out_r = nc.dram_tensor("w_out_r", [L * KC * 128, 16 * 128], BF16).ap()
    wg_r = nc.dram_tensor("wg_r", [L * FC * 128, KC * 128], BF16).ap()
    wu_r = nc.dram_tensor("wu_r", [L * FC * 128, KC * 128], BF16).ap()
    wd_r = nc.dram_tensor("wd_r", [L * KC * 128, FC * 128], BF16).ap()
    rpbp = nc.dram_tensor("rpbp", [L * 60, 127], F32).ap()
    ctab_d = nc.dram_tensor("ctab_d", [L * 128, 60 * 64], F32).ap()

    es = ExitStack()
    with es:
        P = Prog(nc, es)

        def sb(name, shape, dt):
            return es.enter_context(nc.sbuf_tensor("sb_" + name, list(shape), dt))

        def ps(name, shape, dt):
            return es.enter_context(nc.psum_tensor("ps_" + name, list(shape), dt))

        xT = sb("xT", [128, KC, T], F32)
        hT = sb("hT", [128, KC, T], BF16)
        mixT = sb("mixT", [128, 16, T], BF16)
        NSLOT = 4
        wring = [sb("wr%d" % i, [128, 2048], BF16) for i in range(NSLOT)]
        SCRB = 46 * 1024
        scr = sb("scr", [128, SCRB // 4], F32)
        ident_f = sb("ident_f", [128, 128], F32)
        ident_b = sb("ident_b", [128, 128], BF16)
        ones_b = sb("ones_b", [128, 128], BF16)
        maskf = sb("maskf", [128, 128], F32)
        maskb = sb("maskb", [128, 128], F32)
        cstart = sb("cstart", [128, 512], F32)
        colmask = sb("colmask", [128, 64], F32)
        rowsel = sb("rowsel", [128, 4], F32)
        mods = sb("mods", [128, L, 6, KC, NV], F32)
        A1 = sb("A1", [128, L, KC, NV], F32)
        A2 = sb("A2", [128, L, KC, NV], F32)
        badaT = sb("badaT", [128, L, 6 * KC], F32)
        nmwT = sb("nmwT", [128, L * KC], F32)
        nfwT = sb("nfwT", [128, L * KC], F32)
        fnwT = sb("fnwT", [128, KC], F32)
        gnwT = sb("gnwT", [128, L], F32)
        cwT = sb("cwT", [128, L * 12], F32)
        lbT = sb("lbT", [128, 2 * L * 8], F32)
        LB = sb("LB", [128, 2 * L * 8], F32)
        OM = sb("OM", [128, 2 * L * 8], F32)
        NOM = sb("NOM", [128, 2 * L * 8], F32)
        sT = sb("sT", [128, KC, NV], BF16)
        small = sb("small", [128, 64], F32)

        D0 = ps("D0", [128, 512], F32)
        D1 = ps("D1", [128, 512], F32)
        SC = ps("SC", [128, 3, 512], F32)
        TB = ps("TB", [128, 2, 1024], BF16)
        PV = ps("PV", [128, 512], F32)
        bD0, bD1, bPV = Buf(True), Buf(True), Buf(True)
        bSC = [Buf(True), Buf(True), Buf(True)]
        bTB = [Buf(True), Buf(True)]

        def carve(off, shape, dt):
            n = 1
            for s_ in shape:
                n *= s_
            nb = n * (4 if dt == F32 else 2)
            assert off % 4 == 0 and off + nb <= SCRB, (off, nb, SCRB)
            ap = scr[:, off // 4:(off + nb) // 4]
            if dt != F32:
                ap = ap.bitcast(dt)
            if len(shape) == 2:
                ap = ap.rearrange("p (a b) -> p a b", b=shape[1])
            elif len(shape) == 3:
                ap = ap.rearrange("p (a b c) -> p a b c", b=shape[1], c=shape[2])
            return ap, off + nb

        phase_bufs = []

        def new_phase(n):
            prev = []
            for b in phase_bufs:
                if b.w is not None:
                    prev.append(b.w)
                prev.extend(b.rs)
            del phase_bufs[:]
            out = []
            for _ in range(n):
                b = Buf()
                b.rs = list(prev)
                phase_bufs.append(b)
                out.append(b)
            return out

        ch_misc = P.chan()
        ch_stg = P.chan()
        ch_cv = P.chan()
        b_const = Buf()

        def load_const(dst, src):
            P.dma("sp", lambda h: h.dma_start(out=dst, in_=src), ch_misc, writes=[b_const])

        load_const(ident_f[:], c_ident)
        load_const(maskf[:], c_maskf)
        load_const(maskb[:], c_maskb)
        load_const(cstart[:], c_cstart)
        load_const(colmask[:], c_colmask)
        load_const(rowsel[:], c_rowsel)
        b_idb = Buf()
        P.op("dve", lambda h: h.tensor_copy(out=ident_b[:], in_=ident_f[:]), reads=[b_const], writes=[b_idb])
        P.op("pool", lambda h: h.memset(ones_b[:], 1.0), writes=[b_idb])

        (bs_stg, bs_ada0, bs_ada1, bs_cv, bs_z) = new_phase(5)
        o = 0
        stg_v, o = carve(o, [128], F32)
        cv_in, o = carve(o, [KC * 128], F32)
        cv_s = cv_in
        ztile, o = carve(o, [128], F32)
        ada_slot = []
        for i in range(2):
            a_, o = carve(o, [KC, 256], BF16)
            ada_slot.append(a_)
        ctab_s, o = carve(o, [60, 64], F32)

        b_vecs = Buf()

        def load_T(dst, src_rows, n):
            P.dma("sp", lambda h: h.dma_start(out=stg_v[0:n, :], in_=src_rows), ch_stg, writes=[bs_stg])
            P.op("pe", lambda h: h.transpose(out=SC[:, 0, 0:n], in_=stg_v[0:n, :], identity=ident_f[0:n, 0:n]),
                 reads=[bs_stg, b_const], writes=[bSC[0]])
            P.op("dve", lambda h: h.tensor_copy(out=dst, in_=SC[:, 0, 0:n]), reads=[bSC[0]], writes=[b_vecs])

        for l in range(L):
            load_T(badaT[:, l, :], b_ada[l * 6 * KC:(l + 1) * 6 * KC, :], 6 * KC)
        load_T(nmwT[:], nmw, L * KC)
        load_T(nfwT[:], nfw, L * KC)
        load_T(fnwT[:], fnw, KC)
        load_T(gnwT[:], gnw, L)
        load_T(cwT[:], cw, L * 12)
        load_T(lbT[:], lbr, 2 * L * 8)

        b_lb = Buf()
        P.op("act", lambda h: h.activation(out=lbT[:], in_=lbT[:], func=AF.Exp), reads=[b_vecs], writes=[b_vecs])
        for d_ in range(2):
            base = d_ * L * 8
            tot = small[:, 0:8]
            P.op("dve", lambda h, base=base: h.tensor_copy(out=small[:, 0:8], in_=lbT[:, base:base + 8]), reads=[b_vecs], writes=[b_lb])
            for l in range(1, L):
                P.op("dve", lambda h, base=base, l=l: h.tensor_tensor(out=small[:, 0:8], in0=small[:, 0:8], in1=lbT[:, base + l * 8:base + l * 8 + 8], op=ALU.add),
                     reads=[b_vecs, b_lb], writes=[b_lb])
            P.op("dve", lambda h: h.reciprocal(out=small[:, 8:16], in_=small[:, 0:8]), reads=[b_lb], writes=[b_lb])
            P.op("pool", lambda h, base=base: h.memset(LB[:, base:base + 8], 0.0), writes=[b_lb])
            for l in range(1, L):
                P.op("dve", lambda h, base=base, l=l: h.tensor_tensor(out=small[:, 16:24], in0=lbT[:, base + l * 8:base + l * 8 + 8], in1=small[:, 8:16], op=ALU.mult),
                     reads=[b_vecs, b_lb], writes=[b_lb])
                P.op("dve", lambda h, base=base, l=l: h.tensor_tensor(out=LB[:, base + l * 8:base + l * 8 + 8], in0=LB[:, base + (l - 1) * 8:base + (l - 1) * 8 + 8], in1=small[:, 16:24], op=ALU.add),
                     reads=[b_lb], writes=[b_lb])
        P.op("dve", lambda h: h.tensor_scalar(out=OM[:], in0=LB[:], scalar1=-1.0, scalar2=1.0, op0=ALU.mult, op1=ALU.add), reads=[b_lb], writes=[b_lb])
        P.op("dve", lambda h: h.tensor_scalar(out=NOM[:], in0=OM[:], scalar1=-1.0, scalar2=None, op0=ALU.mult), reads=[b_lb], writes=[b_lb])

        P.dma("sp", lambda h: h.dma_start(out=cv_in[0:NV, :], in_=cvec), ch_cv, writes=[bs_cv])
        P.op("act", lambda h: h.activation(out=cv_s[0:NV, :], in_=cv_in[0:NV, :], func=AF.Silu), reads=[bs_cv], writes=[bs_cv])
        b_sT = Buf()
        for k in range(KC):
            P.op("pe", lambda h, k=k: h.transpose(out=SC[:, 0, 0:NV], in_=cv_s[0:NV, k * 128:(k + 1) * 128], identity=ident_f[0:NV, 0:NV]),
                 reads=[bs_cv, b_const], writes=[bSC[0]])
            P.op("dve", lambda h, k=k: h.tensor_copy(out=sT[:, k, :], in_=SC[:, 0, 0:NV]), reads=[bSC[0]], writes=[b_sT])

        ch_ada = [P.chan(), P.chan()]
        b_mods = Buf()
        nblk = (6 * D) // 256
        bi = 0
        for l in range(L):
            wv = w_ada[l * D:(l + 1) * D, :].rearrange("(k p) n -> p k n", p=128)
            for blk in range(nblk):
                s_ = bi % 2
                bi += 1
                bsl = bs_ada0 if s_ == 0 else bs_ada1
                P.dma("pool", lambda h, s_=s_, blk=blk, wv=wv: h.dma_start(out=ada_slot[s_], in_=wv[:, :, blk * 256:(blk + 1) * 256]), ch_ada[s_], writes=[bsl])
                for mm in range(2):
                    mg = blk * 2 + mm
                    for k in range(KC):
                        P.op("pe", lambda h, s_=s_, mm=mm, k=k: h.matmul(PV[:, mm * NV:(mm + 1) * NV], lhsT=ada_slot[s_][:, k, mm * 128:(mm + 1) * 128], rhs=sT[:, k, :], start=(k == 0), stop=(k == KC - 1)),
                             reads=[bsl, b_sT], writes=[bPV])
                    j6, kk_ = mg // KC, mg % KC
                    P.op("dve", lambda h, l=l, mm=mm, mg=mg, j6=j6, kk_=kk_: h.tensor_scalar(out=mods[:, l, j6, kk_, :], in0=PV[:, mm * NV:(mm + 1) * NV], scalar1=badaT[:, l, mg:mg + 1], scalar2=None, op0=ALU.add),
                         reads=[bPV, b_vecs], writes=[b_mods])
        for l in range(L):
            for k in range(KC):
                P.op("dve", lambda h, l=l, k=k: h.tensor_scalar(out=A1[:, l, k, :], in0=mods[:, l, 1, k, :], scalar1=1.0, scalar2=nmwT[:, l * KC + k:l * KC + k + 1], op0=ALU.add, op1=ALU.mult),
                     reads=[b_mods, b_vecs], writes=[b_mods])
                P.op("dve", lambda h, l=l, k=k: h.tensor_scalar(out=A2[:, l, k, :], in0=mods[:, l, 4, k, :], scalar1=1.0, scalar2=nfwT[:, l * KC + k:l * KC + k + 1], op0=ALU.add, op1=ALU.mult),
                     reads=[b_mods, b_vecs], writes=[b_mods])

        ch_tab = P.chan()
        b_rpbp, b_ctabd = Buf(), Buf()
        if NS > 0:
            P.op("pool", lambda h: h.memset(ztile, 0.0), writes=[bs_z])
            P.dma("sp", lambda h: h.dma_start(out=rpbp, in_=ztile[0:L * 60, 0:127]), ch_tab, reads=[bs_z], writes=[b_rpbp])
            P.dma("sp", lambda h: h.dma_start(out=rpbp[:, 48:79], in_=rpb), ch_tab, writes=[b_rpbp])
            ch_tab2 = P.chan()
            b_ct = bs_ada0
            b_cts = Buf()
            phase_bufs.append(b_cts)
            for l in range(L):
                src = rpbp[l * 60:(l + 1) * 60, :]
                first = True
                for w_ in range(64):
                    for ro in range(2):
                        pp_ = ro * 64 + w_
                        if first:
                            P.dma("sp", lambda h, pp_=pp_, w_=w_, src=src: h.dma_start(out=ctab_s[pp_:pp_ + 1, :, :], in_=src[:, 63 - w_:127 - w_].unsqueeze(0)),
                                  ch_tab2, reads=[b_rpbp], writes=[b_cts])
                            first = False
                        else:
                            ch_tab2.cnt += 16
                            ev = ("d", ch_tab2, ch_tab2.cnt)
                            P.ins["sp"].append([lambda h, pp_=pp_, w_=w_, src=src: h.dma_start(out=ctab_s[pp_:pp_ + 1, :, :], in_=src[:, 63 - w_:127 - w_].unsqueeze(0)), [], False, ch_tab2])
                            b_cts.w = ev
                P.op("dve", lambda h: h.tensor_tensor(out=ctab_s, in0=ctab_s, in1=colmask[:].unsqueeze(1).to_broadcast([128, 60, 64]), op=ALU.add),
                     reads=[b_cts, b_const], writes=[b_cts])
                P.dma("sp", lambda h, l=l: h.dma_start(out=ctab_d[l * 128:(l + 1) * 128, :], in_=ctab_s.rearrange("p a b -> p (a b)")), ch_tab, reads=[b_cts], writes=[b_ctabd])

        def precast(dst, src, rows_in, cols_out, l, nchunk_k):
            ch = P.chan()
            b = Buf()
            sv = src[l * rows_in:(l + 1) * rows_in, :].rearrange("(k p) (m c) -> m p k c", p=128, c=128)
            nm = cols_out // 128
            ev = None
            for m in range(nm):
                r0 = (l * nm + m) * 128
                for k0 in range(0, nchunk_k, 16):
                    k1 = min(nchunk_k, k0 + 16)
                    ch.cnt += 16
                    ev = ("d", ch, ch.cnt)
                    P.ins["pool"].append([lambda h, m=m, r0=r0, sv=sv, k0=k0, k1=k1: h.dma_start(out=dst[r0:r0 + 128, :].rearrange("p (k c) -> p k c", c=128)[:, k0:k1, :], in_=sv[m][:, k0:k1, :]), [], False, ch])
            b.w = ev
            return b

        bw_in, bw_out, bw_g, bw_u, bw_d = [], [], [], [], []
        for l in range(L if cfg.stop >= 2 else 0):
            bw_in.append(precast(w_in_r, w_in, D, 8192, l, KC))
            bw_out.append(precast(w_out_r, w_out, 2048, D, l, 16))
            bw_g.append(precast(wg_r, wg, D, FH, l, KC))
            bw_u.append(precast(wu_r, wu, D, FH, l, KC))
            bw_d.append(precast(wd_r, wd, FH, D, l, FC))

        ring_ch = [P.chan() for _ in range(NSLOT)]
        ring_b = [Buf() for _ in range(NSLOT)]
        ring_i = [0]

        def wload(src_tile, width, bsrc):
            s_ = ring_i[0] % NSLOT
            ring_i[0] += 1
            P.dma("sp", lambda h, s_=s_: h.dma_start(out=wring[s_][:, 0:width], in_=src_tile), ring_ch[s_], reads=[bsrc], writes=[ring_b[s_]])
            return wring[s_], ring_b[s_]

        bx = [[Buf() for _ in range(2)] for _ in range(KC)]
        bh = [[Buf() for _ in range(2)] for _ in range(KC)]
        bm = [[Buf() for _ in range(2)] for _ in range(16)]

        dense_banks = [(D0, bD0), (D1, bD1)]
        dense_banks4 = dense_banks + [(SC[:, 0, :], bSC[0]), (SC[:, 1, :], bSC[1])]
        dbi = [0]

        def next_bank(banks):
            b = banks[dbi[0] % len(banks)]
            dbi[0] += 1
            return b

        def hs(n):
            return slice(n * 512, (n + 1) * 512)

        nreal = [4]
        ch_xin = [P.chan(), P.chan()]
        ch_out = [P.chan(), P.chan()]

        def load_x(src_rows):
            (b_s0, b_s1) = new_phase(2)
            o = 0
            stg = []
            for i in range(2):
                a_, o = carve(o, [D], F32)
                stg.append(a_)
            bst = [b_s0, b_s1]
            ntile_real = nreal[0] * 2 if nreal[0] < 4 else NTT
            for tt in range(NTT):
                s_ = tt % 2
                ts_ = tt % ntile_real
                P.dma("sp", lambda h, s_=s_, ts_=ts_: h.dma_start(out=stg[s_], in_=src_rows[ts_ * 128:(ts_ + 1) * 128, :]), ch_xin[s_], writes=[bst[s_]])
                for k0 in range(0, KC, 4):
                    nk_ = min(4, KC - k0)
                    for kk_ in range(nk_):
                        P.op("pe", lambda h, s_=s_, k0=k0, kk_=kk_: h.transpose(out=SC[:, 0, kk_ * 128:(kk_ + 1) * 128], in_=stg[s_][:, (k0 + kk_) * 128:(k0 + kk_ + 1) * 128], identity=ident_f[:]),
                             reads=[bst[s_], b_const], writes=[bSC[0]])
                    P.op("act", lambda h, k0=k0, nk_=nk_, tt=tt: h.activation(out=xT[:, k0:k0 + nk_, tt * 128:(tt + 1) * 128], in_=SC[:, 0, 0:nk_ * 128].rearrange("p (a b) -> p a b", b=128), func=AF.Copy),
                         reads=[bSC[0]], writes=[bx[k][tt // 4] for k in range(k0, k0 + nk_)])

        def norm_stats(n, rbuf_ap, b_r, sq, b_sq):
            for k in range(KC):
                s_ = k % 2
                P.op("act", lambda h, k=k, s_=s_: h.activation(out=sq[s_], in_=xT[:, k, hs(n)], func=AF.Square), reads=[bx[k][n]], writes=[b_sq[s_]])
                P.op("pe", lambda h, k=k, s_=s_: h.matmul(PV[:], lhsT=ones_b[:], rhs=sq[s_], start=(k == 0), stop=(k == KC - 1)), reads=[b_sq[s_], b_idb], writes=[bPV])
            P.op("act", lambda h: h.activation(out=rbuf_ap, in_=PV[:], func=AF.Sqrt, scale=1.0 / D, bias=eps_ap), reads=[bPV, b_eps], writes=[b_r])
            P.op("dve", lambda h: h.reciprocal(out=rbuf_ap, in_=rbuf_ap), reads=[b_r], writes=[b_r])

        eps_ap = small[:, 32:33]
        b_eps = Buf()
        P.op("pool", lambda h: h.memset(small[:, 32:33], EPS), writes=[b_eps])

        def norm_mod(l, v, Aap, shj):
            (b_q0, b_q1, b_r, b_t0, b_t1) = new_phase(5)
            o = 0
            sq = []
            for i in range(2):
                a_, o = carve(o, [512], BF16)
                sq.append(a_)
            rbuf, o = carve(o, [512], F32)
            tt_ = []
            for i in range(2):
                a_, o = carve(o, [512], F32)
                tt_.append(a_)
            b_t = [b_t0, b_t1]
            for n in range(2):
                norm_stats(n, rbuf, b_r, sq, [b_q0, b_q1])
                for k in range(KC):
                    s_ = k % 2
                    P.op("dve", lambda h, k=k, s_=s_, n=n: h.scalar_tensor_tensor(out=tt_[s_], in0=xT[:, k, hs(n)], scalar=Aap[:, l, k, v:v + 1], in1=rbuf, op0=ALU.mult, op1=ALU.mult),
                         reads=[bx[k][n], b_r, b_mods], writes=[b_t[s_]])
                    P.op("act", lambda h, k=k, s_=s_, n=n: h.activation(out=hT[:, k, hs(n)], in_=tt_[s_], func=AF.Identity, bias=mods[:, l, shj, k, v:v + 1], scale=1.0),
                         reads=[b_t[s_], b_mods], writes=[bh[k][n]])

        def inproj_fm(l, m, evac, halves=(0, 1)):
            slot, bsl = wload(w_in_r[(l * 64 + m) * 128:(l * 64 + m + 1) * 128, :], KC * 128, bw_in[l])
            for n in halves:
                bank, bb = next_bank(dense_banks)
                for k in range(KC):
                    P.op("pe", lambda h, k=k, n=n, bank=bank, slot=slot: h.matmul(bank[:] if bank is D0 or bank is D1 else bank, lhsT=slot[:, k * 128:(k + 1) * 128], rhs=hT[:, k, hs(n)], start=(k == 0), stop=(k == KC - 1)),
                         reads=[bsl, bh[k][n]], writes=[bb])
                evac(n, bank[:] if bank is D0 or bank is D1 else bank, bb)

        def inproj_tm(l, m, evac):
            slot, bsl = wload(w_in_r[(l * 64 + m) * 128:(l * 64 + m + 1) * 128, :], KC * 128, bw_in[l])
            for n in range(2):
                bank, bb = next_bank(dense_banks)
                bap = bank[:] if (bank is D0 or bank is D1) else bank
                for t4 in range(4):
                    tt = n * 4 + t4
                    for k in range(KC):
                        P.op("pe", lambda h, k=k, tt=tt, t4=t4, bap=bap, slot=slot: h.matmul(bap[:, t4 * 128:(t4 + 1) * 128], lhsT=hT[:, k, tt * 128:(tt + 1) * 128], rhs=slot[:, k * 128:(k + 1) * 128], start=(k == 0), stop=(k == KC - 1)),
                             reads=[bsl, bh[k][n]], writes=[bb])
                evac(n, bap.rearrange("p (a b) -> p a b", b=128), bb)

        ch_ctx = P.chan()
        ch_ctx2 = P.chan()
        ch_tabl = P.chan()

        def attention(l, kind, u):
            nb_ = 14
            bb_ = new_phase(nb_)
            (b_q, b_k, b_v, b_ckb, b_ckT, b_cvb, b_sl, b_pf, b_pc, b_pt, b_st, b_og0, b_og1, b_tab) = bb_
            o = 0
            qTh, o = carve(o, [T], BF16)
            kTh, o = carve(o, [T], BF16)
            vh, o = carve(o, [NTT, 128], BF16)
            ckb, o = carve(o, [PT, 128], BF16)
            ckT, o = carve(o, [PAST], BF16)
            cvb, o = carve(o, [PT, 128], BF16)
            sloc, o = carve(o, [512], F32)
            pfull, o = carve(o, [640], BF16)
            pctx, o = carve(o, [512], BF16)
            ptr, o = carve(o, [9, 128], BF16)
            stat, o = carve(o, [16], F32)
            ostg = []
            for i in range(2):
                a_, o = carve(o, [4, 128], F32)
                ostg.append(a_)
            b_og = [b_og0, b_og1]
            if kind == "s":
                tab, o = carve(o, [60, 64], F32)
                P.dma("sp", lambda h: h.dma_start(out=tab, in_=ctab_d[l * 128:(l + 1) * 128, :].rearrange("p (a b) -> p a b", b=64)), ch_tabl, reads=[b_ctabd], writes=[b_tab])
            ogi = [0]
            for hd in range(4):
                def ev_q(n, pa, pb):
                    P.op("act", lambda h, n=n, pa=pa: h.activation(out=qTh[:, hs(n)], in_=pa, func=AF.Copy, scale=128.0 ** -0.5), reads=[pb], writes=[b_q])
                inproj_fm(l, 52 + hd, ev_q)

                def ev_k(n, pa, pb):
                    P.op("dve", lambda h, n=n, pa=pa: h.tensor_copy(out=kTh[:, hs(n)], in_=pa), reads=[pb], writes=[b_k])
                if getattr(cfg, "att_sub", 9) >= -1:
                    inproj_fm(l, 56 + hd, ev_k)

                def ev_v(n, pa, pb, hd=hd):
                    P.op("act", lambda h, n=n, pa=pa: h.activation(out=vh[:, n * 4:(n + 1) * 4, :], in_=pa, func=AF.Copy), reads=[pb], writes=[b_v])
                    if kind == "p" and getattr(cfg, "att_sub", 9) >= 1 and n * 2 < nreal[0]:
                        s_ = ogi[0] % 2
                        ogi[0] += 1
                        P.op("dve", lambda h, pa=pa, s_=s_: h.tensor_copy(out=ostg[s_], in_=pa), reads=[pb], writes=[b_og[s_]])
                        for t4 in range(4):
                            if n * 2 + t4 // 2 >= nreal[0]:
                                continue
                            sq_ = u * 4 + n * 2 + t4 // 2
                            r0 = (sq_ * L + l) * SEQ + (t4 % 2) * 128
                            P.dma("sp", lambda h, s_=s_, t4=t4, r0=r0, hd=hd: h.dma_start(out=o_nv[r0:r0 + 128, hd * 128:(hd + 1) * 128], in_=ostg[s_][:, t4, :]),
                                  ch_out[s_], reads=[b_og[s_]])
                if getattr(cfg, "att_sub", 9) >= 0:
                    inproj_tm(l, 60 + hd, ev_v)
                if getattr(cfg, "att_sub", 9) < 1:
                    continue
                if kind == "p":
                    def ev_ko(n, pa, pb, hd=hd):
                        if n * 2 >= nreal[0]:
                            return
                        s_ = ogi[0] % 2
                        ogi[0] += 1
                        P.op("dve", lambda h, pa=pa, s_=s_: h.tensor_copy(out=ostg[s_], in_=pa), reads=[pb], writes=[b_og[s_]])
                        for t4 in range(4):
                            if n * 2 + t4 // 2 >= nreal[0]:
                                continue
                            sq_ = u * 4 + n * 2 + t4 // 2
                            r0 = (sq_ * L + l) * SEQ + (t4 % 2) * 128
                            P.dma("sp", lambda h, s_=s_, t4=t4, r0=r0, hd=hd: h.dma_start(out=o_nk[r0:r0 + 128, hd * 128:(hd + 1) * 128], in_=ostg[s_][:, t4, :]),
                                  ch_out[s_], reads=[b_og[s_]])
                    inproj_tm(l, 56 + hd, ev_ko)
                else:
                    r0 = (u * L + l) * PAST
                    P.dma("pool", lambda h, r0=r0, hd=hd: h.dma_start(out=ckb, in_=ck[r0:r0 + PAST, hd * 128:(hd + 1) * 128].rearrange("(a p) c -> p a c", p=128)), ch_ctx, writes=[b_ckb])
                    P.dma("pool", lambda h, r0=r0, hd=hd: h.dma_start(out=cvb, in_=cv[r0:r0 + PAST, hd * 128:(hd + 1) * 128].rearrange("(a p) c -> p a c", p=128)), ch_ctx2, writes=[b_cvb])
                    for lt in range(PT):
                        P.op("pe", lambda h, lt=lt: h.transpose(out=TB[:, 1, lt * 128:(lt + 1) * 128], in_=ckb[:, lt, :], identity=ident_b[:]), reads=[b_ckb, b_idb], writes=[bTB[1]])
                    P.op("act", lambda h: h.activation(out=ckT, in_=TB[:, 1, 0:PAST], func=AF.Copy), reads=[bTB[1]], writes=[b_ckT])
                asub = getattr(cfg, "att_sub", 9)
                for R in range(NTT if asub >= 2 else 0):
                    if kind == "p":
                        sq_ = R // 2
                        ktiles = [2 * sq_, 2 * sq_ + 1]
                        nloc = 2
                        P.op("pe", lambda h, R=R, sq_=sq_: h.matmul(SC[:, 0, 0:256], lhsT=qTh[:, R * 128:(R + 1) * 128], rhs=kTh[:, sq_ * 256:(sq_ + 1) * 256], start=True, stop=True),
                             reads=[b_q, b_k], writes=[bSC[0]])
                        P.op("dve", lambda h: h.tensor_reduce(out=stat[:, 0:1], in_=SC[:, 0, 0:256], axis=AX.X, op=ALU.max), reads=[bSC[0]], writes=[b_st])
                        P.op("dve", lambda h: h.tensor_scalar(out=stat[:, 1:2], in0=stat[:, 0:1], scalar1=-1.0, scalar2=None, op0=ALU.mult), reads=[b_st], writes=[b_st])
                        P.op("act", lambda h: h.activation(out=pfull[:, 0:256], in_=SC[:, 0, 0:256], func=AF.Exp, bias=stat[:, 1:2], scale=1.0, accum_out=stat[:, 2:3]),
                             reads=[bSC[0], b_st], writes=[b_pf, b_st])
                        P.op("dve", lambda h: h.reciprocal(out=stat[:, 3:4], in_=stat[:, 2:3]), reads=[b_st], writes=[b_st])
                        P.op("dve", lambda h: h.tensor_scalar(out=pfull[:, 0:256], in0=pfull[:, 0:256], scalar1=stat[:, 3:4], scalar2=None, op0=ALU.mult), reads=[b_st, b_pf], writes=[b_pf])
                        nctx = 0
                    else:
                        r_a, r_b = 2 * R, 2 * R + 1
                        rs_a = min(max(r_a - NA_KH // 2, 0), ROWS - NA_KH)
                        rs_b = min(max(r_b - NA_KH // 2, 0), ROWS - NA_KH)
                        kt0 = rs_a // 2
                        kt1 = (rs_b + NA_KH - 1) // 2
                        nloc = kt1 - kt0 + 1
                        ktiles = list(range(kt0, kt0 + nloc))
                        nctx = PT
                        w1 = min(nloc, 4) * 128
                        P.op("pe", lambda h, R=R, kt0=kt0, w1=w1: h.matmul(SC[:, 0, 0:w1], lhsT=qTh[:, R * 128:(R + 1) * 128], rhs=kTh[:, kt0 * 128:kt0 * 128 + w1], start=True, stop=True),
                             reads=[b_q, b_k], writes=[bSC[0]])
                        if nloc > 4:
                            P.op("pe", lambda h, R=R, kt0=kt0: h.matmul(SC[:, 1, 0:128], lhsT=qTh[:, R * 128:(R + 1) * 128], rhs=kTh[:, (kt0 + 4) * 128:(kt0 + 5) * 128], start=True, stop=True),
                                 reads=[b_q, b_k], writes=[bSC[1]])
                        P.op("pe", lambda h, R=R: h.matmul(SC[:, 2, 0:PAST], lhsT=qTh[:, R * 128:(R + 1) * 128], rhs=ckT, start=True, stop=True),
                             reads=[b_q, b_ckT], writes=[bSC[2]])
                        scflat = SC[:].rearrange("p a b -> p (a b)")
                        c0s = []
                        for ro, (r_, rs_) in enumerate(((r_a, rs_a), (r_b, rs_b))):
                            c0 = (rs_ - 2 * kt0) * 64
                            c0s.append(c0)
                            dr0 = rs_ - r_ + 7
                            psl = slice(ro * 64, ro * 64 + 64)
                            P.op("dve", lambda h, psl=psl, c0=c0, dr0=dr0, hd=hd: h.tensor_tensor(out=sloc[psl, :], in0=scflat[psl, c0:c0 + 512], in1=tab[psl, hd * 15 + dr0:hd * 15 + dr0 + 8, :].rearrange("p a b -> p (a b)"), op=ALU.add),
                                 reads=[bSC[0], bSC[1], b_tab], writes=[b_sl])
                        P.op("dve", lambda h: h.tensor_reduce(out=stat[:, 0:1], in_=sloc, axis=AX.X, op=ALU.max), reads=[b_sl], writes=[b_st])
                        P.op("dve", lambda h: h.tensor_reduce(out=stat[:, 4:5], in_=SC[:, 2, 0:PAST], axis=AX.X, op=ALU.max), reads=[bSC[2]], writes=[b_st])
                        P.op("dve", lambda h: h.tensor_tensor(out=stat[:, 0:1], in0=stat[:, 0:1], in1=stat[:, 4:5], op=ALU.max), reads=[b_st], writes=[b_st])
                        P.op("dve", lambda h: h.tensor_scalar(out=stat[:, 1:2], in0=stat[:, 0:1], scalar1=-1.0, scalar2=None, op0=ALU.mult), reads=[b_st], writes=[b_st])
                        P.op("pool", lambda h: h.memset(pfull, 0.0), writes=[b_pf])
                        for ro in range(2):
                            psl = slice(ro * 64, ro * 64 + 64)
                            c0 = c0s[ro]
                            P.op("act", lambda h, psl=psl, c0=c0: h.activation(out=pfull[psl, c0:c0 + 512], in_=sloc[psl, :], func=AF.Exp, bias=stat[psl, 1:2], scale=1.0, accum_out=stat[psl, 2:3]),
                                 reads=[b_sl, b_st], writes=[b_pf, b_st])
                        P.op("act", lambda h: h.activation(out=pctx, in_=SC[:, 2, 0:PAST], func=AF.Exp, bias=stat[:, 1:2], scale=1.0, accum_out=stat[:, 5:6]),
                             reads=[bSC[2], b_st], writes=[b_pc, b_st])
                        P.op("dve", lambda h: h.tensor_tensor(out=stat[:, 2:3], in0=stat[:, 2:3], in1=stat[:, 5:6], op=ALU.add), reads=[b_st], writes=[b_st])
                        P.op("dve", lambda h: h.reciprocal(out=stat[:, 3:4], in_=stat[:, 2:3]), reads=[b_st], writes=[b_st])
                        P.op("dve", lambda h, nloc=nloc: h.tensor_scalar(out=pfull[:, 0:nloc * 128], in0=pfull[:, 0:nloc * 128], scalar1=stat[:, 3:4], scalar2=None, op0=ALU.mult), reads=[b_st, b_pf], writes=[b_pf])
                        P.op("dve", lambda h: h.tensor_scalar(out=pctx, in0=pctx, scalar1=stat[:, 3:4], scalar2=None, op0=ALU.mult), reads=[b_st, b_pc], writes=[b_pc])
                    if asub < 3:
                        continue
                    ntot = nloc + nctx
                    for i in range(ntot):
                        src = pfull[:, i * 128:(i + 1) * 128] if i < nloc else pctx[:, (i - nloc) * 128:(i - nloc + 1) * 128]
                        bk = 0 if i < 8 else 1
                        ii = i if i < 8 else i - 8
                        P.op("pe", lambda h, src=src, bk=bk, ii=ii: h.transpose(out=TB[:, bk, ii * 128:(ii + 1) * 128], in_=src, identity=ident_b[:]),
                             reads=[b_pf, b_pc, b_idb], writes=[bTB[bk]])
                    n0 = min(ntot, 8)
                    P.op("act", lambda h, n0=n0: h.activation(out=ptr[:, 0:n0, :], in_=TB[:, 0, 0:n0 * 128].rearrange("p (a b) -> p a b", b=128), func=AF.Copy), reads=[bTB[0]], writes=[b_pt])
                    if ntot > 8:
                        P.op("act", lambda h: h.activation(out=ptr[:, 8, :], in_=TB[:, 1, 0:128], func=AF.Copy), reads=[bTB[1]], writes=[b_pt])
                    if asub < 4:
                        continue
                    for i in range(ntot):
                        if i < nloc:
                            lh = vh[:, ktiles[i], :]
                            rd = [b_v, b_pt]
                        else:
                            lh = cvb[:, i - nloc, :]
                            rd = [b_cvb, b_pt]
                        P.op("pe", lambda h, lh=lh, i=i, ntot=ntot: h.matmul(PV[:, 0:128], lhsT=lh, rhs=ptr[:, i, :], start=(i == 0), stop=(i == ntot - 1)), reads=rd, writes=[bPV])
                    P.op("act", lambda h, hd=hd, R=R: h.activation(out=mixT[:, 12 + hd, R * 128:(R + 1) * 128], in_=PV[:, 0:128], func=AF.Copy), reads=[bPV], writes=[bm[12 + hd][R // 4]])

        def conv_mixer(l, nseq, slen):
            (b_cc, b_u, b_y, b_cb) = new_phase(4)
            o = 0
            cc_sb, o = carve(o, [T], F32)
            u_sb, o = carve(o, [T], F32)
            y_sb, o = carve(o, [T], F32)
            cb_sb, o = carve(o, [T], F32)

            def v3(ap):
                return ap.rearrange("p (s t) -> p s t", t=slen)
            for j in range(4):
                def ev_cc(n, pa, pb):
                    P.op("act", lambda h, n=n, pa=pa: h.activation(out=cc_sb[:, hs(n)], in_=pa, func=AF.Copy), reads=[pb], writes=[b_cc])
                inproj_fm(l, 44 + j, ev_cc)

                def ev_cx(n, pa, pb):
                    P.op("dve", lambda h, n=n, pa=pa: h.tensor_tensor(out=u_sb[:, hs(n)], in0=pa, in1=cc_sb[:, hs(n)], op=ALU.mult), reads=[pb, b_cc], writes=[b_u])
                inproj_fm(l, 48 + j, ev_cx)

                def ev_cb(n, pa, pb):
                    P.op("act", lambda h, n=n, pa=pa: h.activation(out=cb_sb[:, hs(n)], in_=pa, func=AF.Copy), reads=[pb], writes=[b_cb])
                inproj_fm(l, 40 + j, ev_cb)
                c0 = (l * 3 + 0) * 4 + j
                c1 = (l * 3 + 1) * 4 + j
                c2 = (l * 3 + 2) * 4 + j
                P.op("dve", lambda h, c1=c1: h.tensor_scalar(out=y_sb, in0=u_sb, scalar1=cwT[:, c1:c1 + 1], scalar2=None, op0=ALU.mult), reads=[b_u, b_vecs], writes=[b_y])
                P.op("dve", lambda h, c0=c0: h.scalar_tensor_tensor(out=v3(y_sb)[:, :, 1:slen], in0=v3(u_sb)[:, :, 0:slen - 1], scalar=cwT[:, c0:c0 + 1], in1=v3(y_sb)[:, :, 1:slen], op0=ALU.mult, op1=ALU.add),
                     reads=[b_u, b_y, b_vecs], writes=[b_y])
                P.op("dve", lambda h, c2=c2: h.scalar_tensor_tensor(out=v3(y_sb)[:, :, 0:slen - 1], in0=v3(u_sb)[:, :, 1:slen], scalar=cwT[:, c2:c2 + 1], in1=v3(y_sb)[:, :, 0:slen - 1], op0=ALU.mult, op1=ALU.add),
                     reads=[b_u, b_y, b_vecs], writes=[b_y])
                for n in range(2):
                    P.op("dve", lambda h, n=n, j=j: h.tensor_tensor(out=mixT[:, 8 + j, hs(n)], in0=y_sb[:, hs(n)], in1=cb_sb[:, hs(n)], op=ALU.mult), reads=[b_y, b_cb], writes=[bm[8 + j][n]])

        ch_st = [P.chan(), P.chan()]
        ch_so = [P.chan() for _ in range(16)]

        def hgrn(l, kind, u):
            names = ["q", "xg", "v", "v4", "oacc", "s", "F", "kk", "Bc", "B", "tE", "qt", "kt", "kh", "khT", "U", "Sp", "at", "c0", "c1", "dec"]
            bl = new_phase(len(names) + 32)
            B_ = dict(zip(names, bl))
            bU = bl[len(names):len(names) + 16]
            bSp = bl[len(names) + 16:len(names) + 32]
            o = 0
            q_bf, o = carve(o, [T], BF16)
            xg, o = carve(o, [T], BF16)
            v_tok, o = carve(o, [NTT, 128], BF16)
            V4, o = carve(o, [4, 4 * 128], BF16)
            o_acc, o = carve(o, [T], F32)
            s_sb, o = carve(o, [T], F32)
            Fb, o = carve(o, [512], F32)
            kkb, o = carve(o, [512], BF16)
            Bc, o = carve(o, [512], F32)
            Bb, o = carve(o, [512], F32)
            tE, o = carve(o, [512], F32)
            qt, o = carve(o, [512], BF16)
            kt, o = carve(o, [512], BF16)
            kh, o = carve(o, [512], BF16)
            khT, o = carve(o, [4, 128], BF16)
            U, o = carve(o, [16, 128], F32)
            Sp, o = carve(o, [16, 128], BF16)
            at_sb, o = carve(o, [4, 128], BF16)
            carry = []
            for i in range(2):
                a_, o = carve(o, [128], F32)
                carry.append(a_)
            dec, o = carve(o, [16], F32)
            bcar = [B_["c0"], B_["c1"]]
            chunks_per_seq = (SEQ if kind == "p" else DSEQ) // 32

            for hd in range(8):
                def ev_q(n, pa, pb):
                    P.op("act", lambda h, n=n, pa=pa: h.activation(out=q_bf[:, hs(n)], in_=pa, func=AF.Copy, scale=128.0 ** -0.5), reads=[pb], writes=[B_["q"]])
                inproj_fm(l, hd, ev_q)

                def ev_g(n, pa, pb):
                    P.op("dve", lambda h, n=n, pa=pa: h.tensor_copy(out=xg[:, hs(n)], in_=pa), reads=[pb], writes=[B_["xg"]])
                inproj_fm(l, 32 + hd, ev_g)

                def ev_v(n, pa, pb):
                    P.op("act", lambda h, n=n, pa=pa: h.activation(out=v_tok[:, n * 4:(n + 1) * 4, :], in_=pa, func=AF.Copy), reads=[pb], writes=[B_["v"]])
                inproj_tm(l, 8 + hd, ev_v)

                for dr in range(2):
                    col = (dr * L + l) * 8 + hd
                    lb_ap, om_ap, nom_ap = LB[:, col:col + 1], OM[:, col:col + 1], NOM[:, col:col + 1]

                    def ev_f(n, pa, pb):
                        P.op("act", lambda h, n=n, pa=pa: h.activation(out=s_sb[:, hs(n)], in_=pa, func=AF.Sigmoid), reads=[pb], writes=[B_["s"]])
                    inproj_fm(l, (16 if dr == 0 else 24) + hd, ev_f)
                    mask = maskf if dr == 0 else maskb
                    ci = 0
                    have_state = False
                    for sg in ((0, 1) if dr == 0 else (1, 0)):
                        cs_ = hs(sg)
                        P.op("dve", lambda h, cs_=cs_, om_ap=om_ap, lb_ap=lb_ap: h.tensor_scalar(out=Fb, in0=s_sb[:, cs_], scalar1=om_ap, scalar2=lb_ap, op0=ALU.mult, op1=ALU.add), reads=[B_["s"], b_lb], writes=[B_["F"]])
                        P.op("dve", lambda h, cs_=cs_, om_ap=om_ap, nom_ap=nom_ap: h.tensor_scalar(out=kkb, in0=s_sb[:, cs_], scalar1=nom_ap, scalar2=om_ap, op0=ALU.mult, op1=ALU.add), reads=[B_["s"], b_lb], writes=[B_["kk"]])
                        P.op("act", lambda h: h.activation(out=Fb, in_=Fb, func=AF.Ln), reads=[B_["F"]], writes=[B_["F"]])
                        P.op("dve", lambda h: h.tensor_tensor_scan(out=Bc, data0=cstart[:], data1=Fb, initial=0.0, op0=ALU.mult, op1=ALU.add), reads=[B_["F"], b_const], writes=[B_["Bc"]])
                        Bc3 = Bc.rearrange("p (c j) -> p c j", j=32)
                        tot = Bc3[:, :, 31]
                        totb = tot.unsqueeze(2).to_broadcast([128, 16, 32])
                        if dr == 0:
                            Bsrc, bB = Bc, B_["Bc"]
                        else:
                            P.op("dve", lambda h: h.tensor_tensor(out=Bb.rearrange("p (c j) -> p c j", j=32), in0=totb, in1=Bc3, op=ALU.subtract), reads=[B_["Bc"]], writes=[B_["B"]])
                            P.op("dve", lambda h: h.tensor_tensor(out=Bb, in0=Bb, in1=Fb, op=ALU.add), reads=[B_["B"], B_["F"]], writes=[B_["B"]])
                            Bsrc, bB = Bb, B_["B"]
                        P.op("act", lambda h, Bsrc=Bsrc: h.activation(out=tE, in_=Bsrc, func=AF.Exp), reads=[bB], writes=[B_["tE"]])
                        P.op("dve", lambda h, cs_=cs_: h.tensor_tensor(out=qt, in0=q_bf[:, cs_], in1=tE, op=ALU.mult), reads=[B_["q"], B_["tE"]], writes=[B_["qt"]])
                        P.op("act", lambda h, Bsrc=Bsrc: h.activation(out=tE, in_=Bsrc, func=AF.Exp, scale=-1.0), reads=[bB, B_["qt"]], writes=[B_["tE"]])
                        P.op("dve", lambda h: h.tensor_tensor(out=kt, in0=kkb, in1=tE, op=ALU.mult), reads=[B_["kk"], B_["tE"]], writes=[B_["kt"]])
                        P.op("dve", lambda h, Bsrc=Bsrc: h.tensor_tensor(out=tE.rearrange("p (c j) -> p c j", j=32), in0=totb, in1=Bsrc.rearrange("p (c j) -> p c j", j=32), op=ALU.subtract),
                             reads=[bB, B_["Bc"], B_["kt"]], writes=[B_["tE"]])
                        P.op("act", lambda h: h.activation(out=tE, in_=tE, func=AF.Exp), reads=[B_["tE"]], writes=[B_["tE"]])
                        P.op("dve", lambda h: h.tensor_tensor(out=kh, in0=kkb, in1=tE, op=ALU.mult), reads=[B_["kk"], B_["tE"]], writes=[B_["kh"]])
                        P.op("act", lambda h: h.activation(out=dec, in_=tot, func=AF.Exp), reads=[B_["Bc"]], writes=[B_["dec"]])
                        for t4 in range(4):
                            P.op("pe", lambda h, t4=t4: h.transpose(out=TB[:, 0, t4 * 128:(t4 + 1) * 128], in_=kh[:, t4 * 128:(t4 + 1) * 128], identity=ident_b[:]), reads=[B_["kh"], b_idb], writes=[bTB[0]])
                        P.op("act", lambda h: h.activation(out=khT, in_=TB[:, 0, 0:512].rearrange("p (a b) -> p a b", b=128), func=AF.Copy), reads=[bTB[0]], writes=[B_["khT"]])
                        V43 = V4.rearrange("p t (c v) -> p t c v", v=128)
                        for c in range(4):
                            P.op("pool", lambda h, c=c, sg=sg: h.tensor_scalar(out=V43[:, :, c, :], in0=v_tok[:, sg * 4:(sg + 1) * 4, :], scalar1=rowsel[:, c:c + 1], scalar2=None, op0=ALU.mult),
                                 reads=[B_["v"], b_const], writes=[B_["v4"]])
                        for t4 in range(4):
                            P.op("pe", lambda h, t4=t4: h.matmul(SC[:, 0, :], lhsT=khT[:, t4, :], rhs=V4[:, t4, :], start=True, stop=True), reads=[B_["khT"], B_["v4"]], writes=[bSC[0]])
                            P.op("act", lambda h, t4=t4: h.activation(out=U[:, t4 * 4:(t4 + 1) * 4, :], in_=SC[:, 0, :].rearrange("p (a b) -> p a b", b=128), func=AF.Copy), reads=[bSC[0]], writes=bU[t4 * 4:(t4 + 1) * 4])
                        order = list(range(16)) if dr == 0 else list(range(15, -1, -1))
                        prev = None
                        prev_b = None
                        for cg in order:
                            gch = sg * 16 + cg
                            sidx = gch // chunks_per_seq
                            pos = gch % chunks_per_seq
                            seq_start = (pos == 0) if dr == 0 else (pos == chunks_per_seq - 1)
                            seq_end = (pos == chunks_per_seq - 1) if dr == 0 else (pos == 0)
                            if seq_start:
                                if kind == "p":
                                    prev, prev_b = None, None
                                else:
                                    r0 = (((u * L + l) * 2 + dr) * 8 + hd) * 128
                                    P.dma("sp", lambda h, r0=r0, ci=ci: h.dma_start(out=carry[ci], in_=st[r0:r0 + 128, :]), ch_st[ci], writes=[bcar[ci]])
                                    prev, prev_b = carry[ci], bcar[ci]
                            elif cg == order[0]:
                                prev, prev_b = carry[ci], bcar[ci]
                            if prev is None:
                                P.op("pool", lambda h, cg=cg: h.memset(Sp[:, cg, :], 0.0), writes=[bSp[cg]])
                            else:
                                P.op("pool", lambda h, cg=cg, prev=prev: h.tensor_copy(out=Sp[:, cg, :], in_=prev), reads=[prev_b], writes=[bSp[cg]])
                                P.op("dve", lambda h, cg=cg, prev=prev: h.scalar_tensor_tensor(out=U[:, cg, :], in0=prev, scalar=dec[:, cg:cg + 1], in1=U[:, cg, :], op0=ALU.mult, op1=ALU.add),
                                     reads=[prev_b, B_["dec"], bU[cg]], writes=[bU[cg]])
                            prev, prev_b = U[:, cg, :], bU[cg]
                            if seq_end and kind == "p" and sidx < nreal[0]:
                                sq_ = u * 4 + sidx
                                r0 = (((sq_ * L + l) * 2 + dr) * 8 + hd) * 128
                                P.dma("sp", lambda h, r0=r0, cg=cg: h.dma_start(out=o_ns[r0:r0 + 128, :], in_=U[:, cg, :]), ch_so[cg], reads=[bU[cg]])
                        nci = 1 - ci
                        P.op("pool", lambda h, nci=nci, cg=order[-1]: h.tensor_copy(out=carry[nci], in_=U[:, cg, :]), reads=[bU[cg]], writes=[bcar[nci]])
                        ci = nci
                        for t4 in range(4):
                            P.op("pe", lambda h, t4=t4: h.matmul(SC[:, 1, t4 * 128:(t4 + 1) * 128], lhsT=kt[:, t4 * 128:(t4 + 1) * 128], rhs=qt[:, t4 * 128:(t4 + 1) * 128], start=True, stop=True),
                                 reads=[B_["kt"], B_["qt"]], writes=[bSC[1]])
                        P.op("dve", lambda h, mask=mask: h.tensor_tensor(out=at_sb, in0=SC[:, 1, :].rearrange("p (a b) -> p a b", b=128), in1=mask[:].unsqueeze(1).to_broadcast([128, 4, 128]), op=ALU.mult),
                             reads=[bSC[1], b_const], writes=[B_["at"]])
                        for t4 in range(4):
                            P.op("pe", lambda h, t4=t4, sg=sg: h.matmul(SC[:, 2, t4 * 128:(t4 + 1) * 128], lhsT=v_tok[:, sg * 4 + t4, :], rhs=at_sb[:, t4, :], start=True, stop=False),
                                 reads=[B_["v"], B_["at"]], writes=[bSC[2]])
                            for c in range(4):
                                cg = t4 * 4 + c
                                P.op("pe", lambda h, t4=t4, c=c, cg=cg: h.matmul(SC[:, 2, t4 * 128 + c * 32:t4 * 128 + (c + 1) * 32], lhsT=Sp[:, cg, :], rhs=qt[:, t4 * 128 + c * 32:t4 * 128 + (c + 1) * 32], start=False, stop=(c == 3)),
                                     reads=[bSp[cg], B_["qt"]], writes=[bSC[2]])
                        if dr == 0:
                            P.op("act", lambda h, cs_=cs_: h.activation(out=o_acc[:, cs_], in_=SC[:, 2, :], func=AF.Copy), reads=[bSC[2]], writes=[B_["oacc"]])
                        else:
                            P.op("dve", lambda h, cs_=cs_: h.tensor_tensor(out=o_acc[:, cs_], in0=SC[:, 2, :], in1=o_acc[:, cs_], op=ALU.add), reads=[bSC[2], B_["oacc"]], writes=[B_["oacc"]])
                for n in range(2):
                    P.op("act", lambda h, n=n: h.activation(out=qt, in_=o_acc[:, hs(n)], func=AF.Square), reads=[B_["oacc"]], writes=[B_["qt"]])
                    P.op("pe", lambda h: h.matmul(PV[:], lhsT=ones_b[:], rhs=qt, start=True, stop=True), reads=[B_["qt"], b_idb], writes=[bPV])
                    P.op("act", lambda h: h.activation(out=tE, in_=PV[:], func=AF.Sqrt, scale=1.0 / 128.0, bias=eps_ap), reads=[bPV, b_eps], writes=[B_["tE"]])
                    P.op("dve", lambda h: h.reciprocal(out=tE, in_=tE), reads=[B_["tE"]], writes=[B_["tE"]])
                    P.op("dve", lambda h, n=n: h.tensor_tensor(out=Bc, in0=o_acc[:, hs(n)], in1=tE, op=ALU.mult), reads=[B_["oacc"], B_["tE"]], writes=[B_["Bc"]])
                    P.op("act", lambda h, n=n: h.activation(out=Fb, in_=xg[:, hs(n)], func=AF.Sigmoid), reads=[B_["xg"]], writes=[B_["F"]])
                    P.op("dve", lambda h, n=n: h.tensor_tensor(out=Fb, in0=Fb, in1=xg[:, hs(n)], op=ALU.mult), reads=[B_["xg"], B_["F"]], writes=[B_["F"]])
                    P.op("dve", lambda h, n=n, hd=hd: h.scalar_tensor_tensor(out=mixT[:, hd, hs(n)], in0=Bc, scalar=gnwT[:, l:l + 1], in1=Fb, op0=ALU.mult, op1=ALU.mult),
                         reads=[B_["Bc"], B_["F"], b_vecs], writes=[bm[hd][n]])

        def out_proj(l, v):
            for m in range(KC):
                slot, bsl = wload(w_out_r[(l * KC + m) * 128:(l * KC + m + 1) * 128, :], 2048, bw_out[l])
                for n in range(2):
                    bank, bb = next_bank(dense_banks4)
                    bap = bank[:] if (bank is D0 or bank is D1) else bank
                    for k in range(16):
                        P.op("pe", lambda h, k=k, n=n, bap=bap, slot=slot: h.matmul(bap, lhsT=slot[:, k * 128:(k + 1) * 128], rhs=mixT[:, k, hs(n)], start=(k == 0), stop=(k == 15)),
                             reads=[bsl, bm[k][n]], writes=[bb])
                    P.op("dve", lambda h, m=m, n=n, bap=bap: h.scalar_tensor_tensor(out=xT[:, m, hs(n)], in0=bap, scalar=mods[:, l, 2, m, v:v + 1], in1=xT[:, m, hs(n)], op0=ALU.mult, op1=ALU.add),
                         reads=[bb, bx[m][n], b_mods], writes=[bx[m][n]])

        ch_wd = [P.chan(), P.chan()]

        def ffn(l, v):
            nb_ = 6
            (b_a, b_sg0, b_sg1, b_wd0, b_wd1, b_dummy) = new_phase(nb_)
            for k in range(16):
                for n in range(2):
                    if bm[k][n].w is not None:
                        b_a.rs.append(bm[k][n].w)
                    b_a.rs.extend(bm[k][n].rs)
            o = 0
            a_ext = None
            if FC > 32:
                a_ext, o = carve(o, [FC - 32, 512], BF16)
            sg_t = []
            for i in range(2):
                a_, o = carve(o, [512], F32)
                sg_t.append(a_)
            wds = []
            for i in range(2):
                a_, o = carve(o, [FC * 128], BF16)
                wds.append(a_)
            b_sg = [b_sg0, b_sg1]
            b_wd = [b_wd0, b_wd1]
            mflat = mixT[:].rearrange("p a b -> p (a b)")

            def a_ap(j):
                if j < 32:
                    return mflat[:, j * 512:(j + 1) * 512]
                return a_ext[:, j - 32, :]
            wdi = 0
            for n in range(2):
                for j in range(FC):
                    sl_g, bg_ = wload(wg_r[(l * FC + j) * 128:(l * FC + j + 1) * 128, :], KC * 128, bw_g[l])
                    sl_u, bu_ = wload(wu_r[(l * FC + j) * 128:(l * FC + j + 1) * 128, :], KC * 128, bw_u[l])
                    bank_g, bbg = next_bank(dense_banks4)
                    bank_u, bbu = next_bank(dense_banks4)
                    bg_ap = bank_g[:] if (bank_g is D0 or bank_g is D1) else bank_g
                    bu_ap = bank_u[:] if (bank_u is D0 or bank_u is D1) else bank_u
                    for k in range(KC):
                        P.op("pe", lambda h, k=k, n=n, bg_ap=bg_ap, sl_g=sl_g: h.matmul(bg_ap, lhsT=sl_g[:, k * 128:(k + 1) * 128], rhs=hT[:, k, hs(n)], start=(k == 0), stop=(k == KC - 1)),
                             reads=[bg_, bh[k][n]], writes=[bbg])
                    for k in range(KC):
                        P.op("pe", lambda h, k=k, n=n, bu_ap=bu_ap, sl_u=sl_u: h.matmul(bu_ap, lhsT=sl_u[:, k * 128:(k + 1) * 128], rhs=hT[:, k, hs(n)], start=(k == 0), stop=(k == KC - 1)),
                             reads=[bu_, bh[k][n]], writes=[bbu])
                    s_ = j % 2
                    P.op("act", lambda h, s_=s_, bg_ap=bg_ap: h.activation(out=sg_t[s_], in_=bg_ap, func=AF.Silu), reads=[bbg], writes=[b_sg[s_]])
                    P.op("dve", lambda h, s_=s_, bu_ap=bu_ap, j=j: h.tensor_tensor(out=a_ap(j), in0=bu_ap, in1=sg_t[s_], op=ALU.mult), reads=[bbu, b_sg[s_]], writes=[b_a])
                for m in range(KC):
                    s_ = wdi % 2
                    wdi += 1
                    P.dma("sp", lambda h, s_=s_, m=m: h.dma_start(out=wds[s_], in_=wd_r[(l * KC + m) * 128:(l * KC + m + 1) * 128, :]), ch_wd[s_], reads=[bw_d[l]], writes=[b_wd[s_]])
                    bank, bb = next_bank(dense_banks4)
                    bap = bank[:] if (bank is D0 or bank is D1) else bank
                    for j in range(FC):
                        P.op("pe", lambda h, j=j, s_=s_, bap=bap: h.matmul(bap, lhsT=wds[s_][:, j * 128:(j + 1) * 128], rhs=a_ap(j), start=(j == 0), stop=(j == FC - 1)),
                             reads=[b_wd[s_], b_a], writes=[bb])
                    P.op("dve", lambda h, m=m, n=n, bap=bap: h.scalar_tensor_tensor(out=xT[:, m, hs(n)], in0=bap, scalar=mods[:, l, 5, m, v:v + 1], in1=xT[:, m, hs(n)], op0=ALU.mult, op1=ALU.add),
                         reads=[bb, bx[m][n], b_mods], writes=[bx[m][n]])
            for k in range(16):
                for n in range(2):
                    bm[k][n].w = None
                    bm[k][n].rs = list(b_a.rs) + ([b_a.w] if b_a.w is not None else [])

        def final_store(dst_rows):
            (b_q0, b_q1, b_r, b_y0, b_y1, b_hy) = new_phase(6)
            for k in range(KC):
                for n in range(2):
                    if bh[k][n].w is not None:
                        b_hy.rs.append(bh[k][n].w)
                    b_hy.rs.extend(bh[k][n].rs)
            o = 0
            sq = []
            for i in range(2):
                a_, o = carve(o, [512], BF16)
                sq.append(a_)
            rbuf, o = carve(o, [512], F32)
            ystg = []
            for i in range(2):
                a_, o = carve(o, [D], F32)
                ystg.append(a_)
            b_ys = [b_y0, b_y1]
            yT = hT[:].rearrange("p a b -> p (a b)").bitcast(F32).rearrange("p (a b) -> p a b", b=512)
            si = 0
            for n in range(2):
                norm_stats(n, rbuf, b_r, sq, [b_q0, b_q1])
                for k in range(KC):
                    P.op("dve", lambda h, k=k, n=n: h.scalar_tensor_tensor(out=yT[:, k, :], in0=xT[:, k, hs(n)], scalar=fnwT[:, k:k + 1], in1=rbuf, op0=ALU.mult, op1=ALU.mult),
                         reads=[bx[k][n], b_r, b_vecs], writes=[b_hy])
                for t4 in range(4):
                    if (n * 4 + t4) >= (nreal[0] * 2 if nreal[0] < 4 else NTT):
                        continue
                    s_ = si % 2
                    si += 1
                    for k0 in range(0, KC, 4):
                        nk_ = min(4, KC - k0)
                        for kk_ in range(nk_):
                            P.op("pe", lambda h, k0=k0, kk_=kk_, t4=t4: h.transpose(out=SC[:, 0, kk_ * 128:(kk_ + 1) * 128], in_=yT[:, k0 + kk_, t4 * 128:(t4 + 1) * 128], identity=ident_f[:]),
                                 reads=[b_hy, b_const], writes=[bSC[0]])
                        P.op("act", lambda h, s_=s_, k0=k0, nk_=nk_: h.activation(out=ystg[s_][:, k0 * 128:(k0 + nk_) * 128], in_=SC[:, 0, 0:nk_ * 128], func=AF.Copy), reads=[bSC[0]], writes=[b_ys[s_]])
                    tt = n * 4 + t4
                    P.dma("sp", lambda h, s_=s_, tt=tt: h.dma_start(out=dst_rows[tt * 128:(tt + 1) * 128, :], in_=ystg[s_]), ch_out[s_], reads=[b_ys[s_]])
            for k in range(KC):
                for n in range(2):
                    bh[k][n].w = None
                    bh[k][n].rs = list(b_hy.rs) + ([b_hy.w] if b_hy.w is not None else [])

        units = [("p", u) for u in range((NP + 3) // 4)] + [("s", u) for u in range(NS)]
        for kind, u in units:
            nreal[0] = 4
            if kind == "p":
                nreal[0] = min(4, NP - 4 * u)
                nrow = nreal[0] * SEQ
                src = xp[u * T:u * T + nrow, :]
                dst = yp[u * T:u * T + nrow, :]
                v = 0
                nseq, slen = 4, SEQ
            else:
                src = xs[u * T:(u + 1) * T, :]
                dst = ys[u * T:(u + 1) * T, :]
                v = 1 + u
                nseq, slen = 1, DSEQ
            if cfg.stop < 3:
                break
            load_x(src)
            for l in range(L):
                norm_mod(l, v, A1, 0)
                if cfg.stop >= 4 and kind in getattr(cfg, "att_kinds", "ps"):
                    attention(l, kind, u)
                if cfg.stop >= 5:
                    conv_mixer(l, nseq, slen)
                if cfg.stop >= 6:
                    hgrn(l, kind, u)
                if cfg.stop >= 7:
                    out_proj(l, v)
                    norm_mod(l, v, A2, 3)
                    ffn(l, v)
            if cfg.stop >= 8:
                final_store(dst)

        fin = Buf()
        for ch in [ch_out[0], ch_out[1], ch_misc, ch_tab] + ch_so:
            if ch.cnt:
                fin.rs.append(("d", ch, ch.cnt))
        P.wait_all("sp", [fin] + phase_bufs)
        P.emit_all()
    return nc


N_ACTIVE = 8


def make_in_maps(cfg, n_act, x_prompt, x_sample, cache_na_k, cache_na_v, state_hgrn, c, c_ctx, w_ada, b_ada, norm_mix_w, w_in,
                 hgrn_lb_raw, hgrn_gnorm_w, conv_w, na_rpb, w_out, norm_ffn_w, w_ffn_gate, w_ffn_up, w_ffn_down, final_norm_w):
    f = lambda a: np.ascontiguousarray(np.asarray(a, dtype=np.float32))
    D, L, KC, NP, NS = cfg.D, cfg.L, cfg.KC, cfg.NP, cfg.NS
    consts = host_consts()
    shared = dict(
        w_ada=f(w_ada).reshape(L * D, 6 * D), b_ada=f(b_ada).reshape(L * 6 * KC, 128), nmw=f(norm_mix_w).reshape(L * KC, 128),
        w_in=f(w_in).reshape(L * D, 8192), lbr=f(hgrn_lb_raw).reshape(2 * L * 8, 128), gnw=f(hgrn_gnorm_w).reshape(L, 128),
        cw=f(conv_w).reshape(L * 12, 128), rpb=f(na_rpb).reshape(L * 60, 31), w_out=f(w_out).reshape(L * 2048, D),
        nfw=f(norm_ffn_w).reshape(L * KC, 128), wg=f(w_ffn_gate).reshape(L * D, cfg.FH), wu=f(w_ffn_up).reshape(L * D, cfg.FH),
        wd=f(w_ffn_down).reshape(L * cfg.FH, D), fnw=f(final_norm_w).reshape(KC, 128), **consts)
    maps = []
    for ci in range(n_act):
        m = dict(shared)
        m["xp"] = f(x_prompt[ci * NP:(ci + 1) * NP]).reshape(NP * SEQ, D)
        m["xs"] = f(x_sample[ci * NS:(ci + 1) * NS]).reshape(NS * DSEQ, D)
        m["ck"] = f(cache_na_k[ci * NS:(ci + 1) * NS]).reshape(NS * L * cfg.PAST, 512)
        m["cv"] = f(cache_na_v[ci * NS:(ci + 1) * NS]).reshape(NS * L * cfg.PAST, 512)
        m["st"] = f(state_hgrn[ci * NS:(ci + 1) * NS]).reshape(NS * L * 2 * 8 * 128, 128)
        m["cvec"] = np.concatenate([f(c_ctx).reshape(1, D), f(c[ci * NS:(ci + 1) * NS]).reshape(NS, D)], axis=0)
        maps.append(m)
    return maps


def gather_outputs(cfg, res):
    D, L, NP, NS = cfg.D, cfg.L, cfg.NP, cfg.NS
    yp = np.concatenate([r["yp"].reshape(NP, SEQ, D) for r in res], axis=0)
    ys = np.concatenate([r["ys"].reshape(NS, DSEQ, D) for r in res], axis=0)
    nk = np.concatenate([r["o_nk"].reshape(NP, L, SEQ, 4, 128) for r in res], axis=0)
    nv = np.concatenate([r["o_nv"].reshape(NP, L, SEQ, 4, 128) for r in res], axis=0)
    ns = np.concatenate([r["o_ns"].reshape(NP, L, 2, 8, 128, 128) for r in res], axis=0)
    return (yp.astype(np.float32), ys.astype(np.float32), nk.astype(np.float32), nv.astype(np.float32), ns.astype(np.float32))


def kernel(**inputs):
    xpr = np.asarray(inputs["x_prompt"])
    xsa = np.asarray(inputs["x_sample"])
    n_act = N_ACTIVE
    cfg = Cfg(D=xpr.shape[2], NP=xpr.shape[0] // n_act, NS=xsa.shape[0] // n_act, L=np.asarray(inputs["w_in"]).shape[0],
              PAST=np.asarray(inputs["cache_na_k"]).shape[2])
    nc = build(cfg)
    maps = make_in_maps(cfg, n_act, **inputs)
    res = run_bass_kernel_spmd(nc, maps, core_ids=list(range(n_act)))
    return gather_outputs(cfg, res.results)
```

```python
import numpy as np
from contextlib import ExitStack
import concourse.bass as bass
import concourse.mybir as mybir
from concourse.bass_utils import run_bass_kernel_spmd

F32 = mybir.dt.float32
BF16 = mybir.dt.bfloat16
ALU = mybir.AluOpType
AF = mybir.ActivationFunctionType
AX = mybir.AxisListType
ENGS = ("pe", "act", "dve", "pool", "sp")
EPS = 1e-6
NEG = -30000.0


class Buf:
    __slots__ = ("w", "rs", "excl")

    def __init__(self, excl=False):
        self.w = None
        self.rs = []
        self.excl = excl


class Chan:
    __slots__ = ("sem", "cnt")

    def __init__(self, sem):
        self.sem = sem
        self.cnt = 0


class Prog:
    def __init__(self, nc, es):
        self.nc = nc
        self.es = es
        self.ins = {e: [] for e in ENGS}
        self.esem = {e: es.enter_context(nc.semaphore("s_" + e)) for e in ENGS}
        self.nchan = 0

    def chan(self):
        s = self.es.enter_context(self.nc.semaphore("d%d" % self.nchan))
        self.nchan += 1
        return Chan(s)

    def _deps(self, eng, reads, writes, is_dma):
        deps = []
        for r in reads:
            if r.w is not None:
                deps.append((r.w, True))
            if r.excl:
                for e in r.rs:
                    deps.append((e, False))
        for w in writes:
            if w.w is not None:
                deps.append((w.w, True))
            for e in w.rs:
                deps.append((e, False))
        out = []
        for d, iswr in deps:
            if (not is_dma) and d[0] == "e" and d[1] == eng:
                if eng == "pe":
                    continue
                if not iswr:
                    continue
            out.append(d)
        return out

    def _commit(self, ev, reads, writes):
        for r in reads:
            r.rs.append(ev)
        for w in writes:
            w.w = ev
            w.rs = []

    def op(self, eng, fn, reads=(), writes=()):
        deps = self._deps(eng, reads, writes, False)
        ev = ("e", eng, len(self.ins[eng]))
        self.ins[eng].append([fn, deps, False, None])
        self._commit(ev, reads, writes)

    def dma(self, eng, fn, chan, reads=(), writes=(), inc=16):
        deps = self._deps(eng, reads, writes, True)
        chan.cnt += inc
        ev = ("d", chan, chan.cnt)
        self.ins[eng].append([fn, deps, False, chan])
        self._commit(ev, reads, writes)

    def wait_all(self, eng, bufs):
        deps = []
        for b in bufs:
            if b.w is not None:
                deps.append(b.w)
            deps.extend(b.rs)
        self.ins[eng].append([None, deps, False, None])

    def emit_all(self):
        EPOCH = 16000
        for e in ENGS:
            for it in self.ins[e]:
                for d in it[1]:
                    if d[0] == "e":
                        self.ins[d[1]][d[2]][2] = True
        cnt = {}
        esems = {}
        for e in ENGS:
            c = 0
            arr = []
            for it in self.ins[e]:
                if it[2]:
                    c += 1
                k = max(c - 1, 0)
                arr.append((k // EPOCH, k % EPOCH + 1 if c > 0 else 0))
            cnt[e] = arr
            nep = (max(c - 1, 0)) // EPOCH + 1
            esems[e] = [self.esem[e]] + [self.es.enter_context(self.nc.semaphore("s_%s_%d" % (e, i))) for i in range(1, nep)]

        def emit(eng, h):
            seen_e = {}
            seen_d = {}
            for it_i, it in enumerate(self.ins[eng]):
                need_e = {}
                need_d = {}
                for d in it[1]:
                    if d[0] == "e":
                        ep, c = cnt[d[1]][d[2]]
                        key = (d[1], ep)
                        if c > seen_e.get(key, 0) and c > need_e.get(key, 0):
                            need_e[key] = c
                    else:
                        k = id(d[1])
                        if d[2] > seen_d.get(k, 0) and d[2] > need_d.get(k, (None, 0))[1]:
                            need_d[k] = (d[1], d[2])
                for key, c in need_e.items():
                    h.wait_ge(esems[key[0]][key[1]], c)
                    seen_e[key] = c
                for k, (ch, c) in need_d.items():
                    h.wait_ge(ch.sem, c)
                    seen_d[k] = c
                if it[0] is None:
                    continue
                ins = it[0](h)
                if it[3] is not None:
                    ins.then_inc(it[3].sem, 16)
                elif it[2]:
                    ins.then_inc(esems[eng][cnt[eng][it_i][0]], 1)

        with self.nc.Block() as block:
            @block.tensor
            def _(h):
                emit("pe", h)

            @block.scalar
            def _(h):
                emit("act", h)

            @block.vector
            def _(h):
                emit("dve", h)

            @block.gpsimd
            def _(h):
                emit("pool", h)

            @block.sync
            def _(h):
                emit("sp", h)


class Cfg:
    def __init__(self, D=2048, NP=16, NS=8, L=2, PAST=512, debug=False, stop=99):
        self.stop = stop
        self.D = D
        self.KC = D // 128
        self.FH = ((8 * D + 3 * 256 - 1) // (3 * 256)) * 256
        self.FC = self.FH // 128
        self.NP = NP
        self.NS = NS
        self.L = L
        self.PAST = PAST
        self.PT = PAST // 128
        self.NV = 1 + NS
        self.debug = debug


SEQ = 256
DSEQ = 1024
T = 1024
NTT = 8
GRID_W = 64
ROWS = 16
NA_KH = 8
NA_KW = 16


def host_consts():
    ident = np.eye(128, dtype=np.float32)
    j = np.arange(128)[:, None]
    i = np.arange(128)[None, :]
    same = (j // 32) == (i // 32)
    maskf = (same & (j <= i)).astype(np.float32)
    maskb = (same & (j >= i)).astype(np.float32)
    cstart = np.ones((128, 512), np.float32)
    cstart[:, ::32] = 0.0
    col = np.arange(GRID_W)
    cs = np.clip(col - NA_KW // 2, 0, GRID_W - NA_KW)
    valid = (col[None, :] >= cs[:, None]) & (col[None, :] < cs[:, None] + NA_KW)
    cm = np.where(valid, 0.0, NEG).astype(np.float32)
    colmask = np.concatenate([cm, cm], axis=0)
    rowsel = (np.arange(128)[:, None] // 32 == np.arange(4)[None, :]).astype(np.float32)
    return dict(ident=ident, maskf=maskf, maskb=maskb, cstart=cstart, colmask=colmask, rowsel=rowsel)


def build(cfg):
    D, KC, FH, FC, NP, NS, L, PAST, PT, NV = cfg.D, cfg.KC, cfg.FH, cfg.FC, cfg.NP, cfg.NS, cfg.L, cfg.PAST, cfg.PT, cfg.NV
    nc = bass.Bass("TRN2", target_bir_lowering=False)

    def din(name, shape, dt=F32):
        return nc.dram_tensor(name, list(shape), dt, kind="ExternalInput").ap()

    def dout(name, shape, dt=F32):
        return nc.dram_tensor(name, list(shape), dt, kind="ExternalOutput").ap()

    xp = din("xp", [max(NP, 1) * SEQ, D])
    xs = din("xs", [max(NS, 1) * DSEQ, D])
    ck = din("ck", [max(NS, 1) * L * PAST, 512])
    cv = din("cv", [max(NS, 1) * L * PAST, 512])
    st = din("st", [max(NS, 1) * L * 2 * 8 * 128, 128])
    cvec = din("cvec", [NV, D])
    w_ada = din("w_ada", [L * D, 6 * D])
    b_ada = din("b_ada", [L * 6 * KC, 128])
    nmw = din("nmw", [L * KC, 128])
    w_in = din("w_in", [L * D, 8192])
    lbr = din("lbr", [2 * L * 8, 128])
    gnw = din("gnw", [L, 128])
    cw = din("cw", [L * 12, 128])
    rpb = din("rpb", [L * 60, 31])
    w_out = din("w_out", [L * 2048, D])
    nfw = din("nfw", [L * KC, 128])
    wg = din("wg", [L * D, FH])
    wu = din("wu", [L * D, FH])
    wd = din("wd", [L * FH, D])
    fnw = din("fnw", [KC, 128])
    c_ident = din("ident", [128, 128])
    c_maskf = din("maskf", [128, 128])
    c_maskb = din("maskb", [128, 128])
    c_cstart = din("cstart", [128, 512])
    c_colmask = din("colmask", [128, 64])
    c_rowsel = din("rowsel", [128, 4])

    yp = dout("yp", [max(NP, 1) * SEQ, D])
    ys = dout("ys", [max(NS, 1) * DSEQ, D])
    o_nk = dout("o_nk", [max(NP, 1) * L * SEQ, 512])
    o_nv = dout("o_nv", [max(NP, 1) * L * SEQ, 512])
    o_ns = dout("o_ns", [max(NP, 1) * L * 2 * 8 * 128, 128])

    w_in_r = nc.dram_tensor("w_in_r", [L * 64 * 128, KC * 128], BF16).ap()
    w_out_r = nc.dram_tensor("w_out_r", [L * KC * 128, 16 * 128], BF16).ap()
    wg_r = nc.dram_tensor("wg_r", [L * FC * 128, KC * 128], BF16).ap()
    wu_r = nc.dram_tensor("wu_r", [L * FC * 128, KC * 128], BF16).ap()
    wd_r = nc.dram_tensor("wd_r", [L * KC * 128, FC * 128], BF16).ap()
    rpbp = nc.dram_tensor("rpbp", [L * 60, 127], F32).ap()
    ctab_d = nc.dram_tensor("ctab_d", [L * 128, 60 * 64], F32).ap()

    es = ExitStack()
    with es:
        P = Prog(nc, es)

        def sb(name, shape, dt):
            return es.enter_context(nc.sbuf_tensor("sb_" + name, list(shape), dt))

        def ps(name, shape, dt):
            return es.enter_context(nc.psum_tensor("ps_" + name, list(shape), dt))

        xT = sb("xT", [128, KC, T], F32)
        hT = sb("hT", [128, KC, T], BF16)
        mixT = sb("mixT", [128, 16, T], BF16)
        NSLOT = 4
        wring = [sb("wr%d" % i, [128, 2048], BF16) for i in range(NSLOT)]
        SCRB = 46 * 1024
        scr = sb("scr", [128, SCRB // 4], F32)
        ident_f = sb("ident_f", [128, 128], F32)
        ident_b = sb("ident_b", [128, 128], BF16)
        ones_b = sb("ones_b", [128, 128], BF16)
        maskf = sb("maskf", [128, 128], F32)
        maskb = sb("maskb", [128, 128], F32)
        cstart = sb("cstart", [128, 512], F32)
        colmask = sb("colmask", [128, 64], F32)
        rowsel = sb("rowsel", [128, 4], F32)
        mods = sb("mods", [128, L, 6, KC, NV], F32)
        A1 = sb("A1", [128, L, KC, NV], F32)
        A2 = sb("A2", [128, L, KC, NV], F32)
        badaT = sb("badaT", [128, L, 6 * KC], F32)
        nmwT = sb("nmwT", [128, L * KC], F32)
        nfwT = sb("nfwT", [128, L * KC], F32)
        fnwT = sb("fnwT", [128, KC], F32)
        gnwT = sb("gnwT", [128, L], F32)
        cwT = sb("cwT", [128, L * 12], F32)
        lbT = sb("lbT", [128, 2 * L * 8], F32)
        LB = sb("LB", [128, 2 * L * 8], F32)
        OM = sb("OM", [128, 2 * L * 8], F32)
        NOM = sb("NOM", [128, 2 * L * 8], F32)
        sT = sb("sT", [128, KC, NV], BF16)
        small = sb("small", [128, 64], F32)

        D0 = ps("D0", [128, 512], F32)
        D1 = ps("D1", [128, 512], F32)
        SC = ps("SC", [128, 3, 512], F32)
        TB = ps("TB", [128, 2, 1024], BF16)
        PV = ps("PV", [128, 512], F32)
        bD0, bD1, bPV = Buf(True), Buf(True), Buf(True)
        bSC = [Buf(True), Buf(True), Buf(True)]
        bTB = [Buf(True), Buf(True)]

        def carve(off, shape, dt):
            n = 1
            for s_ in shape:
                n *= s_
            nb = n * (4 if dt == F32 else 2)
            assert off % 4 == 0 and off + nb <= SCRB, (off, nb, SCRB)
            ap = scr[:, off // 4:(off + nb) // 4]
            if dt != F32:
                ap = ap.bitcast(dt)
            if len(shape) == 2:
                ap = ap.rearrange("p (a b) -> p a b", b=shape[1])
            elif len(shape) == 3:
                ap = ap.rearrange("p (a b c) -> p a b c", b=shape[1], c=shape[2])
            return ap, off + nb

        phase_bufs = []

        def new_phase(n):
            prev = []
            for b in phase_bufs:
                if b.w is not None:
                    prev.append(b.w)
                prev.extend(b.rs)
            del phase_bufs[:]
            out = []
            for _ in range(n):
                b = Buf()
                b.rs = list(prev)
                phase_bufs.append(b)
                out.append(b)
            return out

        ch_misc = P.chan()
        ch_stg = P.chan()
        ch_cv = P.chan()
        b_const = Buf()

        def load_const(dst, src):
            P.dma("sp", lambda h: h.dma_start(out=dst, in_=src), ch_misc, writes=[b_const])

        load_const(ident_f[:], c_ident)
        load_const(maskf[:], c_maskf)
        load_const(maskb[:], c_maskb)
        load_const(cstart[:], c_cstart)
        load_const(colmask[:], c_colmask)
        load_const(rowsel[:], c_rowsel)
        b_idb = Buf()
        P.op("dve", lambda h: h.tensor_copy(out=ident_b[:], in_=ident_f[:]), reads=[b_const], writes=[b_idb])
        P.op("pool", lambda h: h.memset(ones_b[:], 1.0), writes=[b_idb])

        (bs_stg, bs_ada0, bs_ada1, bs_cv, bs_z) = new_phase(5)
        o = 0
        stg_v, o = carve(o, [128], F32)
        cv_in, o = carve(o, [KC * 128], F32)
        cv_s = cv_in
        ztile, o = carve(o, [128], F32)
        ada_slot = []
        for i in range(2):
            a_, o = carve(o, [KC, 256], BF16)
            ada_slot.append(a_)
        ctab_s, o = carve(o, [60, 64], F32)

        b_vecs = Buf()

        def load_T(dst, src_rows, n):
            P.dma("sp", lambda h: h.dma_start(out=stg_v[0:n, :], in_=src_rows), ch_stg, writes=[bs_stg])
            P.op("pe", lambda h: h.transpose(out=SC[:, 0, 0:n], in_=stg_v[0:n, :], identity=ident_f[0:n, 0:n]),
                 reads=[bs_stg, b_const], writes=[bSC[0]])
            P.op("dve", lambda h: h.tensor_copy(out=dst, in_=SC[:, 0, 0:n]), reads=[bSC[0]], writes=[b_vecs])

        for l in range(L):
            load_T(badaT[:, l, :], b_ada[l * 6 * KC:(l + 1) * 6 * KC, :], 6 * KC)
        load_T(nmwT[:], nmw, L * KC)
        load_T(nfwT[:], nfw, L * KC)
        load_T(fnwT[:], fnw, KC)
        load_T(gnwT[:], gnw, L)
        load_T(cwT[:], cw, L * 12)
        load_T(lbT[:], lbr, 2 * L * 8)

        b_lb = Buf()
        P.op("act", lambda h: h.activation(out=lbT[:], in_=lbT[:], func=AF.Exp), reads=[b_vecs], writes=[b_vecs])
        for d_ in range(2):
            base = d_ * L * 8
            tot = small[:, 0:8]
            P.op("dve", lambda h, base=base: h.tensor_copy(out=small[:, 0:8], in_=lbT[:, base:base + 8]), reads=[b_vecs], writes=[b_lb])
            for l in range(1, L):
                P.op("dve", lambda h, base=base, l=l: h.tensor_tensor(out=small[:, 0:8], in0=small[:, 0:8], in1=lbT[:, base + l * 8:base + l * 8 + 8], op=ALU.add),
                     reads=[b_vecs, b_lb], writes=[b_lb])
            P.op("dve", lambda h: h.reciprocal(out=small[:, 8:16], in_=small[:, 0:8]), reads=[b_lb], writes=[b_lb])
            P.op("pool", lambda h, base=base: h.memset(LB[:, base:base + 8], 0.0), writes=[b_lb])
            for l in range(1, L):
                P.op("dve", lambda h, base=base, l=l: h.tensor_tensor(out=small[:, 16:24], in0=lbT[:, base + l * 8:base + l * 8 + 8], in1=small[:, 8:16], op=ALU.mult),
                     reads=[b_vecs, b_lb], writes=[b_lb])
                P.op("dve", lambda h, base=base, l=l: h.tensor_tensor(out=LB[:, base + l * 8:base + l * 8 + 8], in0=LB[:, base + (l - 1) * 8:base + (l - 1) * 8 + 8], in1=small[:, 16:24], op=ALU.add),
                     reads=[b_lb], writes=[b_lb])
        P.op("dve", lambda h: h.tensor_scalar(out=OM[:], in0=LB[:], scalar1=-1.0, scalar2=1.0, op0=ALU.mult, op1=ALU.add), reads=[b_lb], writes=[b_lb])
        P.op("dve", lambda h: h.tensor_scalar(out=NOM[:], in0=OM[:], scalar1=-1.0, scalar2=None, op0=ALU.mult), reads=[b_lb], writes=[b_lb])

        P.dma("sp", lambda h: h.dma_start(out=cv_in[0:NV, :], in_=cvec), ch_cv, writes=[bs_cv])
        P.op("act", lambda h: h.activation(out=cv_s[0:NV, :], in_=cv_in[0:NV, :], func=AF.Silu), reads=[bs_cv], writes=[bs_cv])
        b_sT = Buf()
        for k in range(KC):
            P.op("pe", lambda h, k=k: h.transpose(out=SC[:, 0, 0:NV], in_=cv_s[0:NV, k * 128:(k + 1) * 128], identity=ident_f[0:NV, 0:NV]),
                 reads=[bs_cv, b_const], writes=[bSC[0]])
            P.op("dve", lambda h, k=k: h.tensor_copy(out=sT[:, k, :], in_=SC[:, 0, 0:NV]), reads=[bSC[0]], writes=[b_sT])

        ch_ada = [P.chan(), P.chan()]
        b_mods = Buf()
        nblk = (6 * D) // 256
        bi = 0
        for l in range(L):
            wv = w_ada[l * D:(l + 1) * D, :].rearrange("(k p) n -> p k n", p=128)
            for blk in range(nblk):
                s_ = bi % 2
                bi += 1
                bsl = bs_ada0 if s_ == 0 else bs_ada1
                P.dma("pool", lambda h, s_=s_, blk=blk, wv=wv: h.dma_start(out=ada_slot[s_], in_=wv[:, :, blk * 256:(blk + 1) * 256]), ch_ada[s_], writes=[bsl])
                for mm in range(2):
                    mg = blk * 2 + mm
                    for k in range(KC):
                        P.op("pe", lambda h, s_=s_, mm=mm, k=k: h.matmul(PV[:, mm * NV:(mm + 1) * NV], lhsT=ada_slot[s_][:, k, mm * 128:(mm + 1) * 128], rhs=sT[:, k, :], start=(k == 0), stop=(k == KC - 1)),
                             reads=[bsl, b_sT], writes=[bPV])
                    j6, kk_ = mg // KC, mg % KC
                    P.op("dve", lambda h, l=l, mm=mm, mg=mg, j6=j6, kk_=kk_: h.tensor_scalar(out=mods[:, l, j6, kk_, :], in0=PV[:, mm * NV:(mm + 1) * NV], scalar1=badaT[:, l, mg:mg + 1], scalar2=None, op0=ALU.add),
                         reads=[bPV, b_vecs], writes=[b_mods])
        for l in range(L):
            for k in range(KC):
                P.op("dve", lambda h, l=l, k=k: h.tensor_scalar(out=A1[:, l, k, :], in0=mods[:, l, 1, k, :], scalar1=1.0, scalar2=nmwT[:, l * KC + k:l * KC + k + 1], op0=ALU.add, op1=ALU.mult),
                     reads=[b_mods, b_vecs], writes=[b_mods])
                P.op("dve", lambda h, l=l, k=k: h.tensor_scalar(out=A2[:, l, k, :], in0=mods[:, l, 4, k, :], scalar1=1.0, scalar2=nfwT[:, l * KC + k:l * KC + k + 1], op0=ALU.add, op1=ALU.mult),
                     reads=[b_mods, b_vecs], writes=[b_mods])

        ch_tab = P.chan()
        b_rpbp, b_ctabd = Buf(), Buf()
        if NS > 0:
            P.op("pool", lambda h: h.memset(ztile, 0.0), writes=[bs_z])
            P.dma("sp", lambda h: h.dma_start(out=rpbp, in_=ztile[0:L * 60, 0:127]), ch_tab, reads=[bs_z], writes=[b_rpbp])
            P.dma("sp", lambda h: h.dma_start(out=rpbp[:, 48:79], in_=rpb), ch_tab, writes=[b_rpbp])
            ch_tab2 = P.chan()
            b_ct = bs_ada0
            b_cts = Buf()
            phase_bufs.append(b_cts)
            for l in range(L):
                src = rpbp[l * 60:(l + 1) * 60, :]
                first = True
                for w_ in range(64):
                    for ro in range(2):
                        pp_ = ro * 64 + w_
                        if first:
                            P.dma("sp", lambda h, pp_=pp_, w_=w_, src=src: h.dma_start(out=ctab_s[pp_:pp_ + 1, :, :], in_=src[:, 63 - w_:127 - w_].unsqueeze(0)),
                                  ch_tab2, reads=[b_rpbp], writes=[b_cts])
                            first = False
                        else:
                            ch_tab2.cnt += 16
                            ev = ("d", ch_tab2, ch_tab2.cnt)
                            P.ins["sp"].append([lambda h, pp_=pp_, w_=w_, src=src: h.dma_start(out=ctab_s[pp_:pp_ + 1, :, :], in_=src[:, 63 - w_:127 - w_].unsqueeze(0)), [], False, ch_tab2])
                            b_cts.w = ev
                P.op("dve", lambda h: h.tensor_tensor(out=ctab_s, in0=ctab_s, in1=colmask[:].unsqueeze(1).to_broadcast([128, 60, 64]), op=ALU.add),
                     reads=[b_cts, b_const], writes=[b_cts])
                P.dma("sp", lambda h, l=l: h.dma_start(out=ctab_d[l * 128:(l + 1) * 128, :], in_=ctab_s.rearrange("p a b -> p (a b)")), ch_tab, reads=[b_cts], writes=[b_ctabd])

        def precast(dst, src, rows_in, cols_out, l, nchunk_k):
            ch = P.chan()
            b = Buf()
            sv = src[l * rows_in:(l + 1) * rows_in, :].rearrange("(k p) (m c) -> m p k c", p=128, c=128)
            nm = cols_out // 128
            ev = None
            for m in range(nm):
                r0 = (l * nm + m) * 128
                for k0 in range(0, nchunk_k, 16):
                    k1 = min(nchunk_k, k0 + 16)
                    ch.cnt += 16
                    ev = ("d", ch, ch.cnt)
                    P.ins["pool"].append([lambda h, m=m, r0=r0, sv=sv, k0=k0, k1=k1: h.dma_start(out=dst[r0:r0 + 128, :].rearrange("p (k c) -> p k c", c=128)[:, k0:k1, :], in_=sv[m][:, k0:k1, :]), [], False, ch])
            b.w = ev
            return b

        bw_in, bw_out, bw_g, bw_u, bw_d = [], [], [], [], []
        for l in range(L if cfg.stop >= 2 else 0):
            bw_in.append(precast(w_in_r, w_in, D, 8192, l, KC))
            bw_out.append(precast(w_out_r, w_out, 2048, D, l, 16))
            bw_g.append(precast(wg_r, wg, D, FH, l, KC))
            bw_u.append(precast(wu_r, wu, D, FH, l, KC))
            bw_d.append(precast(wd_r, wd, FH, D, l, FC))

        ring_ch = [P.chan() for _ in range(NSLOT)]
        ring_b = [Buf() for _ in range(NSLOT)]
        ring_i = [0]

        def wload(src_tile, width, bsrc):
            s_ = ring_i[0] % NSLOT
            ring_i[0] += 1
            P.dma("sp", lambda h, s_=s_: h.dma_start(out=wring[s_][:, 0:width], in_=src_tile), ring_ch[s_], reads=[bsrc], writes=[ring_b[s_]])
            return wring[s_], ring_b[s_]

        bx = [[Buf() for _ in range(2)] for _ in range(KC)]
        bh = [[Buf() for _ in range(2)] for _ in range(KC)]
        bm = [[Buf() for _ in range(2)] for _ in range(16)]

        dense_banks = [(D0, bD0), (D1, bD1)]
        dense_banks4 = dense_banks + [(SC[:, 0, :], bSC[0]), (SC[:, 1, :], bSC[1])]
        dbi = [0]

        def next_bank(banks):
            b = banks[dbi[0] % len(banks)]
            dbi[0] += 1
            return b

        def hs(n):
            return slice(n * 512, (n + 1) * 512)

        nreal = [4]
        nh = [2]
        ch_xin = [P.chan(), P.chan()]
        ch_out = [P.chan(), P.chan()]

        def load_x(src_rows):
            (b_s0, b_s1) = new_phase(2)
            o = 0
            stg = []
            for i in range(2):
                a_, o = carve(o, [D], F32)
                stg.append(a_)
            bst = [b_s0, b_s1]
            ntile_real = nreal[0] * 2 if nreal[0] < 4 else NTT
            for tt in range(4 * nh[0]):
                s_ = tt % 2
                ts_ = tt % ntile_real
                P.dma("sp", lambda h, s_=s_, ts_=ts_: h.dma_start(out=stg[s_], in_=src_rows[ts_ * 128:(ts_ + 1) * 128, :]), ch_xin[s_], writes=[bst[s_]])
                for k0 in range(0, KC, 4):
                    nk_ = min(4, KC - k0)
                    for kk_ in range(nk_):
                        P.op("pe", lambda h, s_=s_, k0=k0, kk_=kk_: h.transpose(out=SC[:, 0, kk_ * 128:(kk_ + 1) * 128], in_=stg[s_][:, (k0 + kk_) * 128:(k0 + kk_ + 1) * 128], identity=ident_f[:]),
                             reads=[bst[s_], b_const], writes=[bSC[0]])
                    P.op("act", lambda h, k0=k0, nk_=nk_, tt=tt: h.activation(out=xT[:, k0:k0 + nk_, tt * 128:(tt + 1) * 128], in_=SC[:, 0, 0:nk_ * 128].rearrange("p (a b) -> p a b", b=128), func=AF.Copy),
                         reads=[bSC[0]], writes=[bx[k][tt // 4] for k in range(k0, k0 + nk_)])

        def norm_stats(n, rbuf_ap, b_r, sq, b_sq):
            for k in range(KC):
                s_ = k % 2
                P.op("act", lambda h, k=k, s_=s_: h.activation(out=sq[s_], in_=xT[:, k, hs(n)], func=AF.Square), reads=[bx[k][n]], writes=[b_sq[s_]])
                P.op("pe", lambda h, k=k, s_=s_: h.matmul(PV[:], lhsT=ones_b[:], rhs=sq[s_], start=(k == 0), stop=(k == KC - 1)), reads=[b_sq[s_], b_idb], writes=[bPV])
            P.op("act", lambda h: h.activation(out=rbuf_ap, in_=PV[:], func=AF.Sqrt, scale=1.0 / D, bias=eps_ap), reads=[bPV, b_eps], writes=[b_r])
            P.op("dve", lambda h: h.reciprocal(out=rbuf_ap, in_=rbuf_ap), reads=[b_r], writes=[b_r])

        eps_ap = small[:, 32:33]
        b_eps = Buf()
        P.op("pool", lambda h: h.memset(small[:, 32:33], EPS), writes=[b_eps])

        def norm_mod(l, v, Aap, shj):
            (b_q0, b_q1, b_r, b_t0, b_t1) = new_phase(5)
            o = 0
            sq = []
            for i in range(2):
                a_, o = carve(o, [512], BF16)
                sq.append(a_)
            rbuf, o = carve(o, [512], F32)
            tt_ = []
            for i in range(2):
                a_, o = carve(o, [512], F32)
                tt_.append(a_)
            b_t = [b_t0, b_t1]
            for n in range(nh[0]):
                norm_stats(n, rbuf, b_r, sq, [b_q0, b_q1])
                for k in range(KC):
                    s_ = k % 2
                    P.op("dve", lambda h, k=k, s_=s_, n=n: h.scalar_tensor_tensor(out=tt_[s_], in0=xT[:, k, hs(n)], scalar=Aap[:, l, k, v:v + 1], in1=rbuf, op0=ALU.mult, op1=ALU.mult),
                         reads=[bx[k][n], b_r, b_mods], writes=[b_t[s_]])
                    P.op("act", lambda h, k=k, s_=s_, n=n: h.activation(out=hT[:, k, hs(n)], in_=tt_[s_], func=AF.Identity, bias=mods[:, l, shj, k, v:v + 1], scale=1.0),
                         reads=[b_t[s_], b_mods], writes=[bh[k][n]])

        def inproj_fm(l, m, evac, halves=(0, 1)):
            slot, bsl = wload(w_in_r[(l * 64 + m) * 128:(l * 64 + m + 1) * 128, :], KC * 128, bw_in[l])
            for n in range(nh[0]):
                bank, bb = next_bank(dense_banks)
                for k in range(KC):
                    P.op("pe", lambda h, k=k, n=n, bank=bank, slot=slot: h.matmul(bank[:] if bank is D0 or bank is D1 else bank, lhsT=slot[:, k * 128:(k + 1) * 128], rhs=hT[:, k, hs(n)], start=(k == 0), stop=(k == KC - 1)),
                         reads=[bsl, bh[k][n]], writes=[bb])
                evac(n, bank[:] if bank is D0 or bank is D1 else bank, bb)

        def inproj_tm(l, m, evac):
            slot, bsl = wload(w_in_r[(l * 64 + m) * 128:(l * 64 + m + 1) * 128, :], KC * 128, bw_in[l])
            for n in range(nh[0]):
                bank, bb = next_bank(dense_banks)
                bap = bank[:] if (bank is D0 or bank is D1) else bank
                for t4 in range(4):
                    tt = n * 4 + t4
                    for k in range(KC):
                        P.op("pe", lambda h, k=k, tt=tt, t4=t4, bap=bap, slot=slot: h.matmul(bap[:, t4 * 128:(t4 + 1) * 128], lhsT=hT[:, k, tt * 128:(tt + 1) * 128], rhs=slot[:, k * 128:(k + 1) * 128], start=(k == 0), stop=(k == KC - 1)),
                             reads=[bsl, bh[k][n]], writes=[bb])
                evac(n, bap.rearrange("p (a b) -> p a b", b=128), bb)

        ch_ctx = P.chan()
        ch_ctx2 = P.chan()
        ch_tabl = P.chan()

        def attention(l, kind, u):
            nb_ = 14
            bb_ = new_phase(nb_)
            (b_q, b_k, b_v, b_ckb, b_ckT, b_cvb, b_sl, b_pf, b_pc, b_pt, b_st, b_og0, b_og1, b_tab) = bb_
            o = 0
            qTh, o = carve(o, [T], BF16)
            kTh, o = carve(o, [T], BF16)
            vh, o = carve(o, [NTT, 128], BF16)
            ckb, o = carve(o, [PT, 128], BF16)
            ckT, o = carve(o, [PAST], BF16)
            cvb, o = carve(o, [PT, 128], BF16)
            sloc, o = carve(o, [512], F32)
            pfull, o = carve(o, [640], BF16)
            pctx, o = carve(o, [512], BF16)
            ptr, o = carve(o, [9, 128], BF16)
            stat, o = carve(o, [16], F32)
            ostg = []
            for i in range(2):
                a_, o = carve(o, [4, 128], F32)
                ostg.append(a_)
            b_og = [b_og0, b_og1]
            if kind == "s":
                tab, o = carve(o, [60, 64], F32)
                P.dma("sp", lambda h: h.dma_start(out=tab, in_=ctab_d[l * 128:(l + 1) * 128, :].rearrange("p (a b) -> p a b", b=64)), ch_tabl, reads=[b_ctabd], writes=[b_tab])
            ogi = [0]
            for hd in range(4):
                def ev_q(n, pa, pb):
                    P.op("act", lambda h, n=n, pa=pa: h.activation(out=qTh[:, hs(n)], in_=pa, func=AF.Copy, scale=128.0 ** -0.5), reads=[pb], writes=[b_q])
                inproj_fm(l, 52 + hd, ev_q)

                def ev_k(n, pa, pb):
                    P.op("dve", lambda h, n=n, pa=pa: h.tensor_copy(out=kTh[:, hs(n)], in_=pa), reads=[pb], writes=[b_k])
                if getattr(cfg, "att_sub", 9) >= -1:
                    inproj_fm(l, 56 + hd, ev_k)

                def ev_v(n, pa, pb, hd=hd):
                    P.op("act", lambda h, n=n, pa=pa: h.activation(out=vh[:, n * 4:(n + 1) * 4, :], in_=pa, func=AF.Copy), reads=[pb], writes=[b_v])
                    if kind == "p" and getattr(cfg, "att_sub", 9) >= 1 and n * 2 < nreal[0]:
                        s_ = ogi[0] % 2
                        ogi[0] += 1
                        P.op("dve", lambda h, pa=pa, s_=s_: h.tensor_copy(out=ostg[s_], in_=pa), reads=[pb], writes=[b_og[s_]])
                        for t4 in range(4):
                            if n * 2 + t4 // 2 >= nreal[0]:
                                continue
                            sq_ = u * 4 + n * 2 + t4 // 2
                            r0 = (sq_ * L + l) * SEQ + (t4 % 2) * 128
                            P.dma("sp", lambda h, s_=s_, t4=t4, r0=r0, hd=hd: h.dma_start(out=o_nv[r0:r0 + 128, hd * 128:(hd + 1) * 128], in_=ostg[s_][:, t4, :]),
                                  ch_out[s_], reads=[b_og[s_]])
                if getattr(cfg, "att_sub", 9) >= 0:
                    inproj_tm(l, 60 + hd, ev_v)
                if getattr(cfg, "att_sub", 9) < 1:
                    continue
                if kind == "p":
                    def ev_ko(n, pa, pb, hd=hd):
                        if n * 2 >= nreal[0]:
                            return
                        s_ = ogi[0] % 2
                        ogi[0] += 1
                        P.op("dve", lambda h, pa=pa, s_=s_: h.tensor_copy(out=ostg[s_], in_=pa), reads=[pb], writes=[b_og[s_]])
                        for t4 in range(4):
                            if n * 2 + t4 // 2 >= nreal[0]:
                                continue
                            sq_ = u * 4 + n * 2 + t4 // 2
                            r0 = (sq_ * L + l) * SEQ + (t4 % 2) * 128
                            P.dma("sp", lambda h, s_=s_, t4=t4, r0=r0, hd=hd: h.dma_start(out=o_nk[r0:r0 + 128, hd * 128:(hd + 1) * 128], in_=ostg[s_][:, t4, :]),
                                  ch_out[s_], reads=[b_og[s_]])
                    inproj_tm(l, 56 + hd, ev_ko)
                else:
                    r0 = (u * L + l) * PAST
                    P.dma("pool", lambda h, r0=r0, hd=hd: h.dma_start(out=ckb, in_=ck[r0:r0 + PAST, hd * 128:(hd + 1) * 128].rearrange("(a p) c -> p a c", p=128)), ch_ctx, writes=[b_ckb])
                    P.dma("pool", lambda h, r0=r0, hd=hd: h.dma_start(out=cvb, in_=cv[r0:r0 + PAST, hd * 128:(hd + 1) * 128].rearrange("(a p) c -> p a c", p=128)), ch_ctx2, writes=[b_cvb])
                    for lt in range(PT):
                        P.op("pe", lambda h, lt=lt: h.transpose(out=TB[:, 1, lt * 128:(lt + 1) * 128], in_=ckb[:, lt, :], identity=ident_b[:]), reads=[b_ckb, b_idb], writes=[bTB[1]])
                    P.op("act", lambda h: h.activation(out=ckT, in_=TB[:, 1, 0:PAST], func=AF.Copy), reads=[bTB[1]], writes=[b_ckT])
                asub = getattr(cfg, "att_sub", 9)
                for R in range(4 * nh[0] if asub >= 2 else 0):
                    if kind == "p":
                        sq_ = R // 2
                        ktiles = [2 * sq_, 2 * sq_ + 1]
                        nloc = 2
                        P.op("pe", lambda h, R=R, sq_=sq_: h.matmul(SC[:, 0, 0:256], lhsT=qTh[:, R * 128:(R + 1) * 128], rhs=kTh[:, sq_ * 256:(sq_ + 1) * 256], start=True, stop=True),
                             reads=[b_q, b_k], writes=[bSC[0]])
                        P.op("dve", lambda h: h.tensor_reduce(out=stat[:, 0:1], in_=SC[:, 0, 0:256], axis=AX.X, op=ALU.max), reads=[bSC[0]], writes=[b_st])
                        P.op("dve", lambda h: h.tensor_scalar(out=stat[:, 1:2], in0=stat[:, 0:1], scalar1=-1.0, scalar2=None, op0=ALU.mult), reads=[b_st], writes=[b_st])
                        P.op("act", lambda h: h.activation(out=pfull[:, 0:256], in_=SC[:, 0, 0:256], func=AF.Exp, bias=stat[:, 1:2], scale=1.0, accum_out=stat[:, 2:3]),
                             reads=[bSC[0], b_st], writes=[b_pf, b_st])
                        P.op("dve", lambda h: h.reciprocal(out=stat[:, 3:4], in_=stat[:, 2:3]), reads=[b_st], writes=[b_st])
                        P.op("dve", lambda h: h.tensor_scalar(out=pfull[:, 0:256], in0=pfull[:, 0:256], scalar1=stat[:, 3:4], scalar2=None, op0=ALU.mult), reads=[b_st, b_pf], writes=[b_pf])
                        nctx = 0
                    else:
                        r_a, r_b = 2 * R, 2 * R + 1
                        rs_a = min(max(r_a - NA_KH // 2, 0), ROWS - NA_KH)
                        rs_b = min(max(r_b - NA_KH // 2, 0), ROWS - NA_KH)
                        kt0 = rs_a // 2
                        kt1 = (rs_b + NA_KH - 1) // 2
                        nloc = kt1 - kt0 + 1
                        ktiles = list(range(kt0, kt0 + nloc))
                        nctx = PT
                        w1 = min(nloc, 4) * 128
                        P.op("pe", lambda h, R=R, kt0=kt0, w1=w1: h.matmul(SC[:, 0, 0:w1], lhsT=qTh[:, R * 128:(R + 1) * 128], rhs=kTh[:, kt0 * 128:kt0 * 128 + w1], start=True, stop=True),
                             reads=[b_q, b_k], writes=[bSC[0]])
                        if nloc > 4:
                            P.op("pe", lambda h, R=R, kt0=kt0: h.matmul(SC[:, 1, 0:128], lhsT=qTh[:, R * 128:(R + 1) * 128], rhs=kTh[:, (kt0 + 4) * 128:(kt0 + 5) * 128], start=True, stop=True),
                                 reads=[b_q, b_k], writes=[bSC[1]])
                        P.op("pe", lambda h, R=R: h.matmul(SC[:, 2, 0:PAST], lhsT=qTh[:, R * 128:(R + 1) * 128], rhs=ckT, start=True, stop=True),
                             reads=[b_q, b_ckT], writes=[bSC[2]])
                        scflat = SC[:].rearrange("p a b -> p (a b)")
                        c0s = []
                        for ro, (r_, rs_) in enumerate(((r_a, rs_a), (r_b, rs_b))):
                            c0 = (rs_ - 2 * kt0) * 64
                            c0s.append(c0)
                            dr0 = rs_ - r_ + 7
                            psl = slice(ro * 64, ro * 64 + 64)
                            P.op("dve", lambda h, psl=psl, c0=c0, dr0=dr0, hd=hd: h.tensor_tensor(out=sloc[psl, :], in0=scflat[psl, c0:c0 + 512], in1=tab[psl, hd * 15 + dr0:hd * 15 + dr0 + 8, :].rearrange("p a b -> p (a b)"), op=ALU.add),
                                 reads=[bSC[0], bSC[1], b_tab], writes=[b_sl])
                        P.op("dve", lambda h: h.tensor_reduce(out=stat[:, 0:1], in_=sloc, axis=AX.X, op=ALU.max), reads=[b_sl], writes=[b_st])
                        P.op("dve", lambda h: h.tensor_reduce(out=stat[:, 4:5], in_=SC[:, 2, 0:PAST], axis=AX.X, op=ALU.max), reads=[bSC[2]], writes=[b_st])
                        P.op("dve", lambda h: h.tensor_tensor(out=stat[:, 0:1], in0=stat[:, 0:1], in1=stat[:, 4:5], op=ALU.max), reads=[b_st], writes=[b_st])
                        P.op("dve", lambda h: h.tensor_scalar(out=stat[:, 1:2], in0=stat[:, 0:1], scalar1=-1.0, scalar2=None, op0=ALU.mult), reads=[b_st], writes=[b_st])
                        P.op("pool", lambda h: h.memset(pfull, 0.0), writes=[b_pf])
                        for ro in range(2):
                            psl = slice(ro * 64, ro * 64 + 64)
                            c0 = c0s[ro]
                            P.op("act", lambda h, psl=psl, c0=c0: h.activation(out=pfull[psl, c0:c0 + 512], in_=sloc[psl, :], func=AF.Exp, bias=stat[psl, 1:2], scale=1.0, accum_out=stat[psl, 2:3]),
                                 reads=[b_sl, b_st], writes=[b_pf, b_st])
                        P.op("act", lambda h: h.activation(out=pctx, in_=SC[:, 2, 0:PAST], func=AF.Exp, bias=stat[:, 1:2], scale=1.0, accum_out=stat[:, 5:6]),
                             reads=[bSC[2], b_st], writes=[b_pc, b_st])
                        P.op("dve", lambda h: h.tensor_tensor(out=stat[:, 2:3], in0=stat[:, 2:3], in1=stat[:, 5:6], op=ALU.add), reads=[b_st], writes=[b_st])
                        P.op("dve", lambda h: h.reciprocal(out=stat[:, 3:4], in_=stat[:, 2:3]), reads=[b_st], writes=[b_st])
                        P.op("dve", lambda h, nloc=nloc: h.tensor_scalar(out=pfull[:, 0:nloc * 128], in0=pfull[:, 0:nloc * 128], scalar1=stat[:, 3:4], scalar2=None, op0=ALU.mult), reads=[b_st, b_pf], writes=[b_pf])
                        P.op("dve", lambda h: h.tensor_scalar(out=pctx, in0=pctx, scalar1=stat[:, 3:4], scalar2=None, op0=ALU.mult), reads=[b_st, b_pc], writes=[b_pc])
                    if asub < 3:
                        continue
                    ntot = nloc + nctx
                    for i in range(ntot):
                        src = pfull[:, i * 128:(i + 1) * 128] if i < nloc else pctx[:, (i - nloc) * 128:(i - nloc + 1) * 128]
                        bk = 0 if i < 8 else 1
                        ii = i if i < 8 else i - 8
                        P.op("pe", lambda h, src=src, bk=bk, ii=ii: h.transpose(out=TB[:, bk, ii * 128:(ii + 1) * 128], in_=src, identity=ident_b[:]),
                             reads=[b_pf, b_pc, b_idb], writes=[bTB[bk]])
                    n0 = min(ntot, 8)
                    P.op("act", lambda h, n0=n0: h.activation(out=ptr[:, 0:n0, :], in_=TB[:, 0, 0:n0 * 128].rearrange("p (a b) -> p a b", b=128), func=AF.Copy), reads=[bTB[0]], writes=[b_pt])
                    if ntot > 8:
                        P.op("act", lambda h: h.activation(out=ptr[:, 8, :], in_=TB[:, 1, 0:128], func=AF.Copy), reads=[bTB[1]], writes=[b_pt])
                    if asub < 4:
                        continue
                    for i in range(ntot):
                        if i < nloc:
                            lh = vh[:, ktiles[i], :]
                            rd = [b_v, b_pt]
                        else:
                            lh = cvb[:, i - nloc, :]
                            rd = [b_cvb, b_pt]
                        P.op("pe", lambda h, lh=lh, i=i, ntot=ntot: h.matmul(PV[:, 0:128], lhsT=lh, rhs=ptr[:, i, :], start=(i == 0), stop=(i == ntot - 1)), reads=rd, writes=[bPV])
                    P.op("act", lambda h, hd=hd, R=R: h.activation(out=mixT[:, 12 + hd, R * 128:(R + 1) * 128], in_=PV[:, 0:128], func=AF.Copy), reads=[bPV], writes=[bm[12 + hd][R // 4]])

        def conv_mixer(l, nseq, slen):
            (b_cc, b_u, b_y, b_cb) = new_phase(4)
            o = 0
            cc_sb, o = carve(o, [T], F32)
            u_sb, o = carve(o, [T], F32)
            y_sb, o = carve(o, [T], F32)
            cb_sb, o = carve(o, [T], F32)

            def v3(ap):
                return ap.rearrange("p (s t) -> p s t", t=slen)
            for j in range(4):
                def ev_cc(n, pa, pb):
                    P.op("act", lambda h, n=n, pa=pa: h.activation(out=cc_sb[:, hs(n)], in_=pa, func=AF.Copy), reads=[pb], writes=[b_cc])
                inproj_fm(l, 44 + j, ev_cc)

                def ev_cx(n, pa, pb):
                    P.op("dve", lambda h, n=n, pa=pa: h.tensor_tensor(out=u_sb[:, hs(n)], in0=pa, in1=cc_sb[:, hs(n)], op=ALU.mult), reads=[pb, b_cc], writes=[b_u])
                inproj_fm(l, 48 + j, ev_cx)

                def ev_cb(n, pa, pb):
                    P.op("act", lambda h, n=n, pa=pa: h.activation(out=cb_sb[:, hs(n)], in_=pa, func=AF.Copy), reads=[pb], writes=[b_cb])
                inproj_fm(l, 40 + j, ev_cb)
                c0 = (l * 3 + 0) * 4 + j
                c1 = (l * 3 + 1) * 4 + j
                c2 = (l * 3 + 2) * 4 + j
                P.op("dve", lambda h, c1=c1: h.tensor_scalar(out=y_sb, in0=u_sb, scalar1=cwT[:, c1:c1 + 1], scalar2=None, op0=ALU.mult), reads=[b_u, b_vecs], writes=[b_y])
                P.op("dve", lambda h, c0=c0: h.scalar_tensor_tensor(out=v3(y_sb)[:, :, 1:slen], in0=v3(u_sb)[:, :, 0:slen - 1], scalar=cwT[:, c0:c0 + 1], in1=v3(y_sb)[:, :, 1:slen], op0=ALU.mult, op1=ALU.add),
                     reads=[b_u, b_y, b_vecs], writes=[b_y])
                P.op("dve", lambda h, c2=c2: h.scalar_tensor_tensor(out=v3(y_sb)[:, :, 0:slen - 1], in0=v3(u_sb)[:, :, 1:slen], scalar=cwT[:, c2:c2 + 1], in1=v3(y_sb)[:, :, 0:slen - 1], op0=ALU.mult, op1=ALU.add),
                     reads=[b_u, b_y, b_vecs], writes=[b_y])
                for n in range(nh[0]):
                    P.op("dve", lambda h, n=n, j=j: h.tensor_tensor(out=mixT[:, 8 + j, hs(n)], in0=y_sb[:, hs(n)], in1=cb_sb[:, hs(n)], op=ALU.mult), reads=[b_y, b_cb], writes=[bm[8 + j][n]])

        ch_st = [P.chan(), P.chan()]
        ch_so = [P.chan() for _ in range(16)]

        def hgrn(l, kind, u):
            names = ["q", "xg", "v", "v4", "oacc", "s", "F", "kk", "Bc", "B", "tE", "qt", "kt", "kh", "khT", "U", "Sp", "at", "c0", "c1", "dec"]
            bl = new_phase(len(names) + 32)
            B_ = dict(zip(names, bl))
            bU = bl[len(names):len(names) + 16]
            bSp = bl[len(names) + 16:len(names) + 32]
            o = 0
            q_bf, o = carve(o, [T], BF16)
            xg, o = carve(o, [T], BF16)
            v_tok, o = carve(o, [NTT, 128], BF16)
            V4, o = carve(o, [4, 4 * 128], BF16)
            o_acc, o = carve(o, [T], F32)
            s_sb, o = carve(o, [T], F32)
            Fb, o = carve(o, [512], F32)
            kkb, o = carve(o, [512], BF16)
            Bc, o = carve(o, [512], F32)
            Bb, o = carve(o, [512], F32)
            tE, o = carve(o, [512], F32)
            qt, o = carve(o, [512], BF16)
            kt, o = carve(o, [512], BF16)
            kh, o = carve(o, [512], BF16)
            khT, o = carve(o, [4, 128], BF16)
            U, o = carve(o, [16, 128], F32)
            Sp, o = carve(o, [16, 128], BF16)
            at_sb, o = carve(o, [4, 128], BF16)
            carry = []
            for i in range(2):
                a_, o = carve(o, [128], F32)
                carry.append(a_)
            dec, o = carve(o, [16], F32)
            bcar = [B_["c0"], B_["c1"]]
            chunks_per_seq = (SEQ if kind == "p" else DSEQ) // 32

            for hd in range(8):
                def ev_q(n, pa, pb):
                    P.op("act", lambda h, n=n, pa=pa: h.activation(out=q_bf[:, hs(n)], in_=pa, func=AF.Copy, scale=128.0 ** -0.5), reads=[pb], writes=[B_["q"]])
                inproj_fm(l, hd, ev_q)

                def ev_g(n, pa, pb):
                    P.op("dve", lambda h, n=n, pa=pa: h.tensor_copy(out=xg[:, hs(n)], in_=pa), reads=[pb], writes=[B_["xg"]])
                inproj_fm(l, 32 + hd, ev_g)

                def ev_v(n, pa, pb):
                    P.op("act", lambda h, n=n, pa=pa: h.activation(out=v_tok[:, n * 4:(n + 1) * 4, :], in_=pa, func=AF.Copy), reads=[pb], writes=[B_["v"]])
                inproj_tm(l, 8 + hd, ev_v)

                for dr in range(2):
                    col = (dr * L + l) * 8 + hd
                    lb_ap, om_ap, nom_ap = LB[:, col:col + 1], OM[:, col:col + 1], NOM[:, col:col + 1]

                    def ev_f(n, pa, pb):
                        P.op("act", lambda h, n=n, pa=pa: h.activation(out=s_sb[:, hs(n)], in_=pa, func=AF.Sigmoid), reads=[pb], writes=[B_["s"]])
                    inproj_fm(l, (16 if dr == 0 else 24) + hd, ev_f)
                    mask = maskf if dr == 0 else maskb
                    ci = 0
                    have_state = False
                    for sg in ((0, 1) if dr == 0 else (1, 0)):
                        if sg >= nh[0]:
                            continue
                        cs_ = hs(sg)
                        P.op("dve", lambda h, cs_=cs_, om_ap=om_ap, lb_ap=lb_ap: h.tensor_scalar(out=Fb, in0=s_sb[:, cs_], scalar1=om_ap, scalar2=lb_ap, op0=ALU.mult, op1=ALU.add), reads=[B_["s"], b_lb], writes=[B_["F"]])
                        P.op("dve", lambda h, cs_=cs_, om_ap=om_ap, nom_ap=nom_ap: h.tensor_scalar(out=kkb, in0=s_sb[:, cs_], scalar1=nom_ap, scalar2=om_ap, op0=ALU.mult, op1=ALU.add), reads=[B_["s"], b_lb], writes=[B_["kk"]])
                        P.op("act", lambda h: h.activation(out=Fb, in_=Fb, func=AF.Ln), reads=[B_["F"]], writes=[B_["F"]])
                        P.op("dve", lambda h: h.tensor_tensor_scan(out=Bc, data0=cstart[:], data1=Fb, initial=0.0, op0=ALU.mult, op1=ALU.add), reads=[B_["F"], b_const], writes=[B_["Bc"]])
                        Bc3 = Bc.rearrange("p (c j) -> p c j", j=32)
                        tot = Bc3[:, :, 31]
                        totb = tot.unsqueeze(2).to_broadcast([128, 16, 32])
                        if dr == 0:
                            Bsrc, bB = Bc, B_["Bc"]
                        else:
                            P.op("dve", lambda h: h.tensor_tensor(out=Bb.rearrange("p (c j) -> p c j", j=32), in0=totb, in1=Bc3, op=ALU.subtract), reads=[B_["Bc"]], writes=[B_["B"]])
                            P.op("dve", lambda h: h.tensor_tensor(out=Bb, in0=Bb, in1=Fb, op=ALU.add), reads=[B_["B"], B_["F"]], writes=[B_["B"]])
                            Bsrc, bB = Bb, B_["B"]
                        P.op("act", lambda h, Bsrc=Bsrc: h.activation(out=tE, in_=Bsrc, func=AF.Exp), reads=[bB], writes=[B_["tE"]])
                        P.op("dve", lambda h, cs_=cs_: h.tensor_tensor(out=qt, in0=q_bf[:, cs_], in1=tE, op=ALU.mult), reads=[B_["q"], B_["tE"]], writes=[B_["qt"]])
                        P.op("act", lambda h, Bsrc=Bsrc: h.activation(out=tE, in_=Bsrc, func=AF.Exp, scale=-1.0), reads=[bB, B_["qt"]], writes=[B_["tE"]])
                        P.op("dve", lambda h: h.tensor_tensor(out=kt, in0=kkb, in1=tE, op=ALU.mult), reads=[B_["kk"], B_["tE"]], writes=[B_["kt"]])
                        P.op("dve", lambda h, Bsrc=Bsrc: h.tensor_tensor(out=tE.rearrange("p (c j) -> p c j", j=32), in0=totb, in1=Bsrc.rearrange("p (c j) -> p c j", j=32), op=ALU.subtract),
                             reads=[bB, B_["Bc"], B_["kt"]], writes=[B_["tE"]])
                        P.op("act", lambda h: h.activation(out=tE, in_=tE, func=AF.Exp), reads=[B_["tE"]], writes=[B_["tE"]])
                        P.op("dve", lambda h: h.tensor_tensor(out=kh, in0=kkb, in1=tE, op=ALU.mult), reads=[B_["kk"], B_["tE"]], writes=[B_["kh"]])
                        P.op("act", lambda h: h.activation(out=dec, in_=tot, func=AF.Exp), reads=[B_["Bc"]], writes=[B_["dec"]])
                        for t4 in range(4):
                            P.op("pe", lambda h, t4=t4: h.transpose(out=TB[:, 0, t4 * 128:(t4 + 1) * 128], in_=kh[:, t4 * 128:(t4 + 1) * 128], identity=ident_b[:]), reads=[B_["kh"], b_idb], writes=[bTB[0]])
                        P.op("act", lambda h: h.activation(out=khT, in_=TB[:, 0, 0:512].rearrange("p (a b) -> p a b", b=128), func=AF.Copy), reads=[bTB[0]], writes=[B_["khT"]])
                        V43 = V4.rearrange("p t (c v) -> p t c v", v=128)
                        for c in range(4):
                            P.op("pool", lambda h, c=c, sg=sg: h.tensor_scalar(out=V43[:, :, c, :], in0=v_tok[:, sg * 4:(sg + 1) * 4, :], scalar1=rowsel[:, c:c + 1], scalar2=None, op0=ALU.mult),
                                 reads=[B_["v"], b_const], writes=[B_["v4"]])
                        for t4 in range(4):
                            P.op("pe", lambda h, t4=t4: h.matmul(SC[:, 0, :], lhsT=khT[:, t4, :], rhs=V4[:, t4, :], start=True, stop=True), reads=[B_["khT"], B_["v4"]], writes=[bSC[0]])
                            P.op("act", lambda h, t4=t4: h.activation(out=U[:, t4 * 4:(t4 + 1) * 4, :], in_=SC[:, 0, :].rearrange("p (a b) -> p a b", b=128), func=AF.Copy), reads=[bSC[0]], writes=bU[t4 * 4:(t4 + 1) * 4])
                        order = list(range(16)) if dr == 0 else list(range(15, -1, -1))
                        prev = None
                        prev_b = None
                        for cg in order:
                            gch = sg * 16 + cg
                            sidx = gch // chunks_per_seq
                            pos = gch % chunks_per_seq
                            seq_start = (pos == 0) if dr == 0 else (pos == chunks_per_seq - 1)
                            seq_end = (pos == chunks_per_seq - 1) if dr == 0 else (pos == 0)
                            if seq_start:
                                if kind == "p":
                                    prev, prev_b = None, None
                                else:
                                    r0 = (((u * L + l) * 2 + dr) * 8 + hd) * 128
                                    P.dma("sp", lambda h, r0=r0, ci=ci: h.dma_start(out=carry[ci], in_=st[r0:r0 + 128, :]), ch_st[ci], writes=[bcar[ci]])
                                    prev, prev_b = carry[ci], bcar[ci]
                            elif cg == order[0]:
                                prev, prev_b = carry[ci], bcar[ci]
                            if prev is None:
                                P.op("pool", lambda h, cg=cg: h.memset(Sp[:, cg, :], 0.0), writes=[bSp[cg]])
                            else:
                                P.op("pool", lambda h, cg=cg, prev=prev: h.tensor_copy(out=Sp[:, cg, :], in_=prev), reads=[prev_b], writes=[bSp[cg]])
                                P.op("dve", lambda h, cg=cg, prev=prev: h.scalar_tensor_tensor(out=U[:, cg, :], in0=prev, scalar=dec[:, cg:cg + 1], in1=U[:, cg, :], op0=ALU.mult, op1=ALU.add),
                                     reads=[prev_b, B_["dec"], bU[cg]], writes=[bU[cg]])
                            prev, prev_b = U[:, cg, :], bU[cg]
                            if seq_end and kind == "p" and sidx < nreal[0]:
                                sq_ = u * 4 + sidx
                                r0 = (((sq_ * L + l) * 2 + dr) * 8 + hd) * 128
                                P.dma("sp", lambda h, r0=r0, cg=cg: h.dma_start(out=o_ns[r0:r0 + 128, :], in_=U[:, cg, :]), ch_so[cg], reads=[bU[cg]])
                        nci = 1 - ci
                        P.op("pool", lambda h, nci=nci, cg=order[-1]: h.tensor_copy(out=carry[nci], in_=U[:, cg, :]), reads=[bU[cg]], writes=[bcar[nci]])
                        ci = nci
                        for t4 in range(4):
                            P.op("pe", lambda h, t4=t4: h.matmul(SC[:, 1, t4 * 128:(t4 + 1) * 128], lhsT=kt[:, t4 * 128:(t4 + 1) * 128], rhs=qt[:, t4 * 128:(t4 + 1) * 128], start=True, stop=True),
                                 reads=[B_["kt"], B_["qt"]], writes=[bSC[1]])
                        P.op("dve", lambda h, mask=mask: h.tensor_tensor(out=at_sb, in0=SC[:, 1, :].rearrange("p (a b) -> p a b", b=128), in1=mask[:].unsqueeze(1).to_broadcast([128, 4, 128]), op=ALU.mult),
                             reads=[bSC[1], b_const], writes=[B_["at"]])
                        for t4 in range(4):
                            P.op("pe", lambda h, t4=t4, sg=sg: h.matmul(SC[:, 2, t4 * 128:(t4 + 1) * 128], lhsT=v_tok[:, sg * 4 + t4, :], rhs=at_sb[:, t4, :], start=True, stop=False),
                                 reads=[B_["v"], B_["at"]], writes=[bSC[2]])
                            for c in range(4):
                                cg = t4 * 4 + c
                                P.op("pe", lambda h, t4=t4, c=c, cg=cg: h.matmul(SC[:, 2, t4 * 128 + c * 32:t4 * 128 + (c + 1) * 32], lhsT=Sp[:, cg, :], rhs=qt[:, t4 * 128 + c * 32:t4 * 128 + (c + 1) * 32], start=False, stop=(c == 3)),
                                     reads=[bSp[cg], B_["qt"]], writes=[bSC[2]])
                        if dr == 0:
                            P.op("act", lambda h, cs_=cs_: h.activation(out=o_acc[:, cs_], in_=SC[:, 2, :], func=AF.Copy), reads=[bSC[2]], writes=[B_["oacc"]])
                        else:
                            P.op("dve", lambda h, cs_=cs_: h.tensor_tensor(out=o_acc[:, cs_], in0=SC[:, 2, :], in1=o_acc[:, cs_], op=ALU.add), reads=[bSC[2], B_["oacc"]], writes=[B_["oacc"]])
                for n in range(nh[0]):
                    P.op("act", lambda h, n=n: h.activation(out=qt, in_=o_acc[:, hs(n)], func=AF.Square), reads=[B_["oacc"]], writes=[B_["qt"]])
                    P.op("pe", lambda h: h.matmul(PV[:], lhsT=ones_b[:], rhs=qt, start=True, stop=True), reads=[B_["qt"], b_idb], writes=[bPV])
                    P.op("act", lambda h: h.activation(out=tE, in_=PV[:], func=AF.Sqrt, scale=1.0 / 128.0, bias=eps_ap), reads=[bPV, b_eps], writes=[B_["tE"]])
                    P.op("dve", lambda h: h.reciprocal(out=tE, in_=tE), reads=[B_["tE"]], writes=[B_["tE"]])
                    P.op("dve", lambda h, n=n: h.tensor_tensor(out=Bc, in0=o_acc[:, hs(n)], in1=tE, op=ALU.mult), reads=[B_["oacc"], B_["tE"]], writes=[B_["Bc"]])
                    P.op("act", lambda h, n=n: h.activation(out=Fb, in_=xg[:, hs(n)], func=AF.Sigmoid), reads=[B_["xg"]], writes=[B_["F"]])
                    P.op("dve", lambda h, n=n: h.tensor_tensor(out=Fb, in0=Fb, in1=xg[:, hs(n)], op=ALU.mult), reads=[B_["xg"], B_["F"]], writes=[B_["F"]])
                    P.op("dve", lambda h, n=n, hd=hd: h.scalar_tensor_tensor(out=mixT[:, hd, hs(n)], in0=Bc, scalar=gnwT[:, l:l + 1], in1=Fb, op0=ALU.mult, op1=ALU.mult),
                         reads=[B_["Bc"], B_["F"], b_vecs], writes=[bm[hd][n]])

        def out_proj(l, v):
            for m in range(KC):
                slot, bsl = wload(w_out_r[(l * KC + m) * 128:(l * KC + m + 1) * 128, :], 2048, bw_out[l])
                for n in range(nh[0]):
                    bank, bb = next_bank(dense_banks4)
                    bap = bank[:] if (bank is D0 or bank is D1) else bank
                    for k in range(16):
                        P.op("pe", lambda h, k=k, n=n, bap=bap, slot=slot: h.matmul(bap, lhsT=slot[:, k * 128:(k + 1) * 128], rhs=mixT[:, k, hs(n)], start=(k == 0), stop=(k == 15)),
                             reads=[bsl, bm[k][n]], writes=[bb])
                    P.op("dve", lambda h, m=m, n=n, bap=bap: h.scalar_tensor_tensor(out=xT[:, m, hs(n)], in0=bap, scalar=mods[:, l, 2, m, v:v + 1], in1=xT[:, m, hs(n)], op0=ALU.mult, op1=ALU.add),
                         reads=[bb, bx[m][n], b_mods], writes=[bx[m][n]])

        ch_wd = [P.chan(), P.chan()]

        def ffn(l, v):
            nb_ = 6
            (b_a, b_sg0, b_sg1, b_wd0, b_wd1, b_dummy) = new_phase(nb_)
            for k in range(16):
                for n in range(2):
                    if bm[k][n].w is not None:
                        b_a.rs.append(bm[k][n].w)
                    b_a.rs.extend(bm[k][n].rs)
            o = 0
            a_ext = None
            if FC > 32:
                a_ext, o = carve(o, [FC - 32, 512], BF16)
            sg_t = []
            for i in range(2):
                a_, o = carve(o, [512], F32)
                sg_t.append(a_)
            wds = []
            for i in range(2):
                a_, o = carve(o, [FC * 128], BF16)
                wds.append(a_)
            b_sg = [b_sg0, b_sg1]
            b_wd = [b_wd0, b_wd1]
            mflat = mixT[:].rearrange("p a b -> p (a b)")

            def a_ap(j):
                if j < 32:
                    return mflat[:, j * 512:(j + 1) * 512]
                return a_ext[:, j - 32, :]
            wdi = 0
            for n in range(nh[0]):
                for j in range(FC):
                    sl_g, bg_ = wload(wg_r[(l * FC + j) * 128:(l * FC + j + 1) * 128, :], KC * 128, bw_g[l])
                    sl_u, bu_ = wload(wu_r[(l * FC + j) * 128:(l * FC + j + 1) * 128, :], KC * 128, bw_u[l])
                    bank_g, bbg = next_bank(dense_banks4)
                    bank_u, bbu = next_bank(dense_banks4)
                    bg_ap = bank_g[:] if (bank_g is D0 or bank_g is D1) else bank_g
                    bu_ap = bank_u[:] if (bank_u is D0 or bank_u is D1) else bank_u
                    for k in range(KC):
                        P.op("pe", lambda h, k=k, n=n, bg_ap=bg_ap, sl_g=sl_g: h.matmul(bg_ap, lhsT=sl_g[:, k * 128:(k + 1) * 128], rhs=hT[:, k, hs(n)], start=(k == 0), stop=(k == KC - 1)),
                             reads=[bg_, bh[k][n]], writes=[bbg])
                    for k in range(KC):
                        P.op("pe", lambda h, k=k, n=n, bu_ap=bu_ap, sl_u=sl_u: h.matmul(bu_ap, lhsT=sl_u[:, k * 128:(k + 1) * 128], rhs=hT[:, k, hs(n)], start=(k == 0), stop=(k == KC - 1)),
                             reads=[bu_, bh[k][n]], writes=[bbu])
                    s_ = j % 2
                    P.op("act", lambda h, s_=s_, bg_ap=bg_ap: h.activation(out=sg_t[s_], in_=bg_ap, func=AF.Silu), reads=[bbg], writes=[b_sg[s_]])
                    P.op("dve", lambda h, s_=s_, bu_ap=bu_ap, j=j: h.tensor_tensor(out=a_ap(j), in0=bu_ap, in1=sg_t[s_], op=ALU.mult), reads=[bbu, b_sg[s_]], writes=[b_a])
                for m in range(KC):
                    s_ = wdi % 2
                    wdi += 1
                    P.dma("sp", lambda h, s_=s_, m=m: h.dma_start(out=wds[s_], in_=wd_r[(l * KC + m) * 128:(l * KC + m + 1) * 128, :]), ch_wd[s_], reads=[bw_d[l]], writes=[b_wd[s_]])
                    bank, bb = next_bank(dense_banks4)
                    bap = bank[:] if (bank is D0 or bank is D1) else bank
                    for j in range(FC):
                        P.op("pe", lambda h, j=j, s_=s_, bap=bap: h.matmul(bap, lhsT=wds[s_][:, j * 128:(j + 1) * 128], rhs=a_ap(j), start=(j == 0), stop=(j == FC - 1)),
                             reads=[b_wd[s_], b_a], writes=[bb])
                    P.op("dve", lambda h, m=m, n=n, bap=bap: h.scalar_tensor_tensor(out=xT[:, m, hs(n)], in0=bap, scalar=mods[:, l, 5, m, v:v + 1], in1=xT[:, m, hs(n)], op0=ALU.mult, op1=ALU.add),
                         reads=[bb, bx[m][n], b_mods], writes=[bx[m][n]])
            for k in range(16):
                for n in range(2):
                    bm[k][n].w = None
                    bm[k][n].rs = list(b_a.rs) + ([b_a.w] if b_a.w is not None else [])

        def final_store(dst_rows):
            (b_q0, b_q1, b_r, b_y0, b_y1, b_hy) = new_phase(6)
            for k in range(KC):
                for n in range(2):
                    if bh[k][n].w is not None:
                        b_hy.rs.append(bh[k][n].w)
                    b_hy.rs.extend(bh[k][n].rs)
            o = 0
            sq = []
            for i in range(2):
                a_, o = carve(o, [512], BF16)
                sq.append(a_)
            rbuf, o = carve(o, [512], F32)
            ystg = []
            for i in range(2):
                a_, o = carve(o, [D], F32)
                ystg.append(a_)
            b_ys = [b_y0, b_y1]
            yT = hT[:].rearrange("p a b -> p (a b)").bitcast(F32).rearrange("p (a b) -> p a b", b=512)
            si = 0
            for n in range(nh[0]):
                norm_stats(n, rbuf, b_r, sq, [b_q0, b_q1])
                for k in range(KC):
                    P.op("dve", lambda h, k=k, n=n: h.scalar_tensor_tensor(out=yT[:, k, :], in0=xT[:, k, hs(n)], scalar=fnwT[:, k:k + 1], in1=rbuf, op0=ALU.mult, op1=ALU.mult),
                         reads=[bx[k][n], b_r, b_vecs], writes=[b_hy])
                for t4 in range(4):
                    if (n * 4 + t4) >= (nreal[0] * 2 if nreal[0] < 4 else NTT):
                        continue
                    s_ = si % 2
                    si += 1
                    for k0 in range(0, KC, 4):
                        nk_ = min(4, KC - k0)
                        for kk_ in range(nk_):
                            P.op("pe", lambda h, k0=k0, kk_=kk_, t4=t4: h.transpose(out=SC[:, 0, kk_ * 128:(kk_ + 1) * 128], in_=yT[:, k0 + kk_, t4 * 128:(t4 + 1) * 128], identity=ident_f[:]),
                                 reads=[b_hy, b_const], writes=[bSC[0]])
                        P.op("act", lambda h, s_=s_, k0=k0, nk_=nk_: h.activation(out=ystg[s_][:, k0 * 128:(k0 + nk_) * 128], in_=SC[:, 0, 0:nk_ * 128], func=AF.Copy), reads=[bSC[0]], writes=[b_ys[s_]])
                    tt = n * 4 + t4
                    P.dma("sp", lambda h, s_=s_, tt=tt: h.dma_start(out=dst_rows[tt * 128:(tt + 1) * 128, :], in_=ystg[s_]), ch_out[s_], reads=[b_ys[s_]])
            for k in range(KC):
                for n in range(2):
                    bh[k][n].w = None
                    bh[k][n].rs = list(b_hy.rs) + ([b_hy.w] if b_hy.w is not None else [])

        units = [("p", u) for u in range((NP + 3) // 4)] + [("s", u) for u in range(NS)]
        for kind, u in units:
            nreal[0] = 4
            nh[0] = 2
            if kind == "p":
                nreal[0] = min(4, NP - 4 * u)
                nh[0] = 1 if nreal[0] <= 2 else 2
                nrow = nreal[0] * SEQ
                src = xp[u * T:u * T + nrow, :]
                dst = yp[u * T:u * T + nrow, :]
                v = 0
                nseq, slen = 4, SEQ
            else:
                src = xs[u * T:(u + 1) * T, :]
                dst = ys[u * T:(u + 1) * T, :]
                v = 1 + u
                nseq, slen = 1, DSEQ
            if cfg.stop < 3:
                break
            load_x(src)
            for l in range(L):
                norm_mod(l, v, A1, 0)
                if cfg.stop >= 4 and kind in getattr(cfg, "att_kinds", "ps"):
                    attention(l, kind, u)
                if cfg.stop >= 5:
                    conv_mixer(l, nseq, slen)
                if cfg.stop >= 6:
                    hgrn(l, kind, u)
                if cfg.stop >= 7:
                    out_proj(l, v)
                    norm_mod(l, v, A2, 3)
                    ffn(l, v)
            if cfg.stop >= 8:
                final_store(dst)

        fin = Buf()
        for ch in [ch_out[0], ch_out[1], ch_misc, ch_tab] + ch_so:
            if ch.cnt:
                fin.rs.append(("d", ch, ch.cnt))
        P.wait_all("sp", [fin] + phase_bufs)
        P.emit_all()
    return nc


N_ACTIVE = 8


def make_in_maps(cfg, n_act, x_prompt, x_sample, cache_na_k, cache_na_v, state_hgrn, c, c_ctx, w_ada, b_ada, norm_mix_w, w_in,
                 hgrn_lb_raw, hgrn_gnorm_w, conv_w, na_rpb, w_out, norm_ffn_w, w_ffn_gate, w_ffn_up, w_ffn_down, final_norm_w):
    f = lambda a: np.ascontiguousarray(np.asarray(a, dtype=np.float32))
    D, L, KC, NP, NS = cfg.D, cfg.L, cfg.KC, cfg.NP, cfg.NS
    consts = host_consts()
    shared = dict(
        w_ada=f(w_ada).reshape(L * D, 6 * D), b_ada=f(b_ada).reshape(L * 6 * KC, 128), nmw=f(norm_mix_w).reshape(L * KC, 128),
        w_in=f(w_in).reshape(L * D, 8192), lbr=f(hgrn_lb_raw).reshape(2 * L * 8, 128), gnw=f(hgrn_gnorm_w).reshape(L, 128),
        cw=f(conv_w).reshape(L * 12, 128), rpb=f(na_rpb).reshape(L * 60, 31), w_out=f(w_out).reshape(L * 2048, D),
        nfw=f(norm_ffn_w).reshape(L * KC, 128), wg=f(w_ffn_gate).reshape(L * D, cfg.FH), wu=f(w_ffn_up).reshape(L * D, cfg.FH),
        wd=f(w_ffn_down).reshape(L * cfg.FH, D), fnw=f(final_norm_w).reshape(KC, 128), **consts)
    maps = []
    for ci in range(n_act):
        m = dict(shared)
        m["xp"] = f(x_prompt[ci * NP:(ci + 1) * NP]).reshape(NP * SEQ, D)
        m["xs"] = f(x_sample[ci * NS:(ci + 1) * NS]).reshape(NS * DSEQ, D)
        m["ck"] = f(cache_na_k[ci * NS:(ci + 1) * NS]).reshape(NS * L * cfg.PAST, 512)
        m["cv"] = f(cache_na_v[ci * NS:(ci + 1) * NS]).reshape(NS * L * cfg.PAST, 512)
        m["st"] = f(state_hgrn[ci * NS:(ci + 1) * NS]).reshape(NS * L * 2 * 8 * 128, 128)
        m["cvec"] = np.concatenate([f(c_ctx).reshape(1, D), f(c[ci * NS:(ci + 1) * NS]).reshape(NS, D)], axis=0)
        maps.append(m)
    return maps


def gather_outputs(cfg, res):
    D, L, NP, NS = cfg.D, cfg.L, cfg.NP, cfg.NS
    yp = np.concatenate([r["yp"].reshape(NP, SEQ, D) for r in res], axis=0)
    ys = np.concatenate([r["ys"].reshape(NS, DSEQ, D) for r in res], axis=0)
    nk = np.concatenate([r["o_nk"].reshape(NP, L, SEQ, 4, 128) for r in res], axis=0)
    nv = np.concatenate([r["o_nv"].reshape(NP, L, SEQ, 4, 128) for r in res], axis=0)
    ns = np.concatenate([r["o_ns"].reshape(NP, L, 2, 8, 128, 128) for r in res], axis=0)
    return (yp.astype(np.float32), ys.astype(np.float32), nk.astype(np.float32), nv.astype(np.float32), ns.astype(np.float32))


def kernel(**inputs):
    xpr = np.asarray(inputs["x_prompt"])
    xsa = np.asarray(inputs["x_sample"])
    n_act = N_ACTIVE
    cfg = Cfg(D=xpr.shape[2], NP=xpr.shape[0] // n_act, NS=xsa.shape[0] // n_act, L=np.asarray(inputs["w_in"]).shape[0],
              PAST=np.asarray(inputs["cache_na_k"]).shape[2])
    nc = build(cfg)
    maps = make_in_maps(cfg, n_act, **inputs)
    res = run_bass_kernel_spmd(nc, maps, core_ids=list(range(n_act)))
    return gather_outputs(cfg, res.results)
```
